# Optimizing a Trainium2 kernel written in Bass

```python
import math
import jax, jax.numpy as jnp
from jax import lax
import numpy as np

D_MODEL = 1024
BATCH = 32
SEQ = 2048
DEPTH = 2
DEC_BATCH = 2
DEC_SEQ = 16384
PAST_LEN = 128

GRID_W = 64
HEAD_DIM = 64
EPS = 1e-6
NA_HEADS = 4
NA_WIN_ROWS = 8
NA_WIN_COLS = 16
NA_QCOLS = 16
NA_KCOLS = 32
S5_WIDTH = 256
S5_GROUP = 16
S5_GROUPS = S5_WIDTH // S5_GROUP
S5_STATE = 64
S5_DT_MIN = 1e-3
S5_DT_MAX = 1e-1
GQA_Q_HEADS = 4
GQA_KV_HEADS = 2
GQA_BLOCK = 128
ROPE_THETA = 10000.0
ROPE_AXIS_DIM = HEAD_DIM // 2
HGRN_HEADS = 4
HGRN_DK = 64
HGRN_DV = 64
HGRN_CHUNK = 64
MEM_TOKENS = 256
XA_HEADS = 4
XA_HEAD_DIM = 64
XA_W = XA_HEADS * XA_HEAD_DIM
D_FF = 2816
CONV_W = 3

NA_W = NA_HEADS * HEAD_DIM
GQA_W = GQA_Q_HEADS * HEAD_DIM
GQA_KV_W = GQA_KV_HEADS * HEAD_DIM
HGRN_KW = HGRN_HEADS * HGRN_DK
HGRN_W = HGRN_HEADS * HGRN_DV
MIX_W = NA_W + S5_WIDTH + GQA_W + HGRN_W
IN_SPLITS = (NA_W, NA_W, NA_W, S5_WIDTH, GQA_W, GQA_KV_W, GQA_KV_W, HGRN_KW, HGRN_KW, HGRN_KW, HGRN_W, HGRN_W)
IN_W = sum(IN_SPLITS)

kernel_name = "hybrid_bidir_na_s5_gqa_hgrn2_encoder"


def rms_norm(x, w):
    xf = x.astype(jnp.float32)
    y = xf * lax.rsqrt(jnp.mean(xf * xf, axis=-1, keepdims=True) + EPS)
    return y.astype(x.dtype) * w.astype(x.dtype)


def neighbourhood_attention(q, k, v, rpb):
    B, S, H, d = q.shape
    rows = S // GRID_W
    win_r = min(NA_WIN_ROWS, rows)
    qg = q.reshape(B, rows, GRID_W, H, d)
    kg = k.reshape(B, rows, GRID_W, H, d)
    vg = v.reshape(B, rows, GRID_W, H, d)
    r = np.arange(rows)
    r0 = np.clip(r - win_r // 2, 0, rows - win_r)
    row_idx = r0[:, None] + np.arange(win_r)[None, :]
    kr = kg[:, row_idx]
    vr = vg[:, row_idx]
    dr = row_idx - r[:, None] + (NA_WIN_ROWS - 1)
    scale = 1.0 / math.sqrt(d)
    outs = []
    for cb in range(GRID_W // NA_QCOLS):
        qs = cb * NA_QCOLS
        qcols = np.arange(qs, qs + NA_QCOLS)
        ks = int(np.clip(qs - NA_WIN_COLS // 2, 0, GRID_W - NA_KCOLS))
        kcols = np.arange(ks, ks + NA_KCOLS)
        c0 = np.clip(qcols - NA_WIN_COLS // 2, 0, GRID_W - NA_WIN_COLS)
        in_win = (kcols[None, :] >= c0[:, None]) & (kcols[None, :] < c0[:, None] + NA_WIN_COLS)
        dc = np.clip(kcols[None, :] - qcols[:, None], -(NA_WIN_COLS - 1), NA_WIN_COLS - 1) + (NA_WIN_COLS - 1)
        bias = rpb[:, dr[:, None, :, None], dc[None, :, None, :]]
        kb = kr[:, :, :, ks:ks + NA_KCOLS]
        vb = vr[:, :, :, ks:ks + NA_KCOLS]
        s = jnp.einsum('brqhd,brikhd->bhrqik', qg[:, :, qs:qs + NA_QCOLS], kb).astype(jnp.float32) * scale
        s = s + bias.astype(jnp.float32)[None]
        s = jnp.where(in_win[:, None, :], s, -jnp.inf)
        p = jax.nn.softmax(s.reshape(B, H, rows, NA_QCOLS, win_r * NA_KCOLS), axis=-1)
        p = p.reshape(s.shape).astype(v.dtype)
        outs.append(jnp.einsum('bhrqik,brikhd->brqhd', p, vb))
    o = jnp.concatenate(outs, axis=2)
    return o.reshape(B, S, H * d)


def s5_combine(e1, e2):
    a1r, a1i, b1r, b1i = e1
    a2r, a2i, b2r, b2i = e2
    return (a2r * a1r - a2i * a1i,
            a2r * a1i + a2i * a1r,
            a2r * b1r - a2i * b1i + b2r,
            a2r * b1i + a2i * b1r + b2i)


def s5_direction(u, lam_re, lam_im, log_dt, b_re, b_im, c_re, c_im):
    lre = jnp.minimum(lam_re.astype(jnp.float32), -1e-4)
    lim = lam_im.astype(jnp.float32)
    dt = jnp.exp(log_dt.astype(jnp.float32))[:, None]
    mag = jnp.exp(lre * dt)
    abar_re = mag * jnp.cos(lim * dt)
    abar_im = mag * jnp.sin(lim * dt)
    den = lre * lre + lim * lim
    nre = abar_re - 1.0
    nim = abar_im
    coef_re = (nre * lre + nim * lim) / den
    coef_im = (nim * lre - nre * lim) / den
    bre = b_re.astype(jnp.float32)
    bim = b_im.astype(jnp.float32)
    bbar_re = coef_re[..., None] * bre - coef_im[..., None] * bim
    bbar_im = coef_re[..., None] * bim + coef_im[..., None] * bre
    bu_re = jnp.einsum('bsgc,gnc->bsgn', u, bbar_re)
    bu_im = jnp.einsum('bsgc,gnc->bsgn', u, bbar_im)
    a_re = jnp.broadcast_to(abar_re, bu_re.shape)
    a_im = jnp.broadcast_to(abar_im, bu_im.shape)
    _, _, xr, xi = lax.associative_scan(s5_combine, (a_re, a_im, bu_re, bu_im), axis=1)
    return (jnp.einsum('bsgn,gcn->bsgc', xr, c_re.astype(jnp.float32))
            - jnp.einsum('bsgn,gcn->bsgc', xi, c_im.astype(jnp.float32)))


def s5_mixer(u, lam_re, lam_im, log_dt, b_re, b_im, c_re, c_im, d_skip, glu_w, glu_b):
    B, S, _ = u.shape
    uf = u.astype(jnp.float32)
    ug = uf.reshape(B, S, S5_GROUPS, S5_GROUP)
    y_f = s5_direction(ug, lam_re[0], lam_im[0], log_dt[0], b_re[0], b_im[0], c_re[0], c_im[0])
    y_b = jnp.flip(s5_direction(jnp.flip(ug, axis=1), lam_re[1], lam_im[1], log_dt[1],
                                b_re[1], b_im[1], c_re[1], c_im[1]), axis=1)
    y = (y_f + y_b).reshape(B, S, S5_WIDTH) + d_skip.astype(jnp.float32) * uf
    h = jax.nn.gelu(y).astype(u.dtype)
    return h * jax.nn.sigmoid(h @ glu_w + glu_b)


def axial_rope(S, dtype):
    t = jnp.arange(S)
    inv = 1.0 / (ROPE_THETA ** (jnp.arange(0, ROPE_AXIS_DIM, 2, dtype=jnp.float32) / ROPE_AXIS_DIM))
    ang = jnp.concatenate([(t // GRID_W).astype(jnp.float32)[:, None] * inv,
                           (t % GRID_W).astype(jnp.float32)[:, None] * inv], axis=-1)
    return jnp.cos(ang)[:, None, :].astype(dtype), jnp.sin(ang)[:, None, :].astype(dtype)


def apply_rope(x, cos, sin):
    x1 = x[..., :ROPE_AXIS_DIM]
    x2 = x[..., ROPE_AXIS_DIM:]
    return jnp.concatenate([x1 * cos - x2 * sin, x2 * cos + x1 * sin], axis=-1)


def gqa_axial(q, k, v, q_norm_w, k_norm_w):
    B, S, Hq, d = q.shape
    Hkv = k.shape[2]
    grp = Hq // Hkv
    cos, sin = axial_rope(S, q.dtype)
    q = apply_rope(rms_norm(q, q_norm_w), cos, sin)
    k = apply_rope(rms_norm(k, k_norm_w), cos, sin)
    nb = S // GQA_BLOCK
    qb = q.reshape(B, nb, GQA_BLOCK, Hkv, grp, d).transpose(1, 0, 2, 3, 4, 5)
    scale = 1.0 / math.sqrt(d)

    def attend_block(qi):
        s = jnp.einsum('bqhgd,bkhd->bhgqk', qi, k).astype(jnp.float32) * scale
        p = jax.nn.softmax(s, axis=-1).astype(v.dtype)
        return jnp.einsum('bhgqk,bkhd->bqhgd', p, v)

    o = lax.map(attend_block, qb)
    return o.transpose(1, 0, 2, 3, 4, 5).reshape(B, S, Hq * d)


def hgrn2_lower_bounds(lb_param):
    sm = jax.nn.softmax(lb_param.astype(jnp.float32), axis=0)
    return jnp.concatenate([jnp.zeros_like(sm[:1]), jnp.cumsum(sm, axis=0)[:-1]], axis=0)


def hgrn2_scan(q, logf, kin, v):
    B, S, H, dk = q.shape
    dv = v.shape[-1]
    C = HGRN_CHUNK
    nc = S // C

    def to_chunks(a):
        return a.reshape(B, nc, C, H, a.shape[-1]).transpose(1, 0, 3, 2, 4)

    tri = jnp.tril(jnp.ones((C, C), dtype=bool))

    def step(state, xs):
        qc, lfc, kc, vc = xs
        b = jnp.cumsum(lfc, axis=2)
        o_inter = jnp.einsum('bhtk,bhkv->bhtv', qc * jnp.exp(b), state)
        diff = b[:, :, :, None, :] - b[:, :, None, :, :]
        decay = jnp.exp(jnp.where(tri[:, :, None], diff, -jnp.inf))
        att = jnp.einsum('bhtsk,bhsk->bhts', decay * qc[:, :, :, None, :], kc)
        o_intra = jnp.einsum('bhts,bhsv->bhtv', att, vc)
        b_last = b[:, :, -1:, :]
        new_state = (jnp.exp(b_last[:, :, 0, :])[..., None] * state
                     + jnp.einsum('bhsk,bhsv->bhkv', kc * jnp.exp(b_last - b), vc))
        return new_state, o_inter + o_intra

    state0 = jnp.zeros((B, H, dk, dv), dtype=jnp.float32)
    _, o = lax.scan(step, state0, (to_chunks(q), to_chunks(logf), to_chunks(kin), to_chunks(v)))
    return o.transpose(1, 0, 3, 2, 4).reshape(B, S, H, dv)


def hgrn2_bidir(q, zf_fwd, zf_bwd, v, lb):
    B, S, _ = q.shape
    q4 = q.astype(jnp.float32).reshape(B, S, HGRN_HEADS, HGRN_DK)
    v4 = v.astype(jnp.float32).reshape(B, S, HGRN_HEADS, HGRN_DV)
    lbf = lb.astype(jnp.float32).reshape(HGRN_HEADS, HGRN_DK)

    def gates(z):
        z4 = z.astype(jnp.float32).reshape(B, S, HGRN_HEADS, HGRN_DK)
        logf = jnp.logaddexp(jnp.log(lbf), jnp.log1p(-lbf) + jax.nn.log_sigmoid(z4))
        kin = (1.0 - lbf) * jax.nn.sigmoid(-z4)
        return logf, kin

    lf_f, k_f = gates(zf_fwd)
    lf_b, k_b = gates(zf_bwd)
    o_f = hgrn2_scan(q4, lf_f, k_f, v4)
    o_b = jnp.flip(hgrn2_scan(jnp.flip(q4, 1), jnp.flip(lf_b, 1), jnp.flip(k_b, 1), jnp.flip(v4, 1)), 1)
    return (o_f + o_b).reshape(B, S, HGRN_W).astype(q.dtype)


def token_mixer(h, w_in, na_rpb, s5_lambda_re, s5_lambda_im, s5_log_dt, s5_b_re, s5_b_im,
                s5_c_re, s5_c_im, s5_d, s5_glu_w, s5_glu_b, gqa_q_norm_w, gqa_k_norm_w,
                hgrn_lb, mix_out_norm_w, w_out):
    B, S, _ = h.shape
    proj = h @ w_in
    offs = [int(i) for i in np.cumsum(IN_SPLITS)[:-1]]
    qa, ka, va, ub, qc, kc, vc, qd, zf_f, zf_b, vd, gd = jnp.split(proj, offs, axis=-1)
    o_a = neighbourhood_attention(qa.reshape(B, S, NA_HEADS, HEAD_DIM), ka.reshape(B, S, NA_HEADS, HEAD_DIM),
                                  va.reshape(B, S, NA_HEADS, HEAD_DIM), na_rpb)
    o_b = s5_mixer(ub, s5_lambda_re, s5_lambda_im, s5_log_dt, s5_b_re, s5_b_im, s5_c_re, s5_c_im,
                   s5_d, s5_glu_w, s5_glu_b)
    o_c = gqa_axial(qc.reshape(B, S, GQA_Q_HEADS, HEAD_DIM), kc.reshape(B, S, GQA_KV_HEADS, HEAD_DIM),
                    vc.reshape(B, S, GQA_KV_HEADS, HEAD_DIM), gqa_q_norm_w, gqa_k_norm_w)
    o_d = hgrn2_bidir(qd, zf_f, zf_b, vd, hgrn_lb)
    g_a, g_b, g_c, g_d = jnp.split(mix_out_norm_w, [NA_W, NA_W + S5_WIDTH, NA_W + S5_WIDTH + GQA_W])
    merged = jnp.concatenate([rms_norm(o_a, g_a), rms_norm(o_b, g_b), rms_norm(o_c, g_c),
                              rms_norm(o_d, g_d) * jax.nn.silu(gd)], axis=-1)
    return merged @ w_out


def memory_cross_attention(h, mem, norm_mem_w, w_q, w_kv, w_o):
    B, S, _ = h.shape
    M = mem.shape[1]
    mn = rms_norm(mem, norm_mem_w)
    q = (h @ w_q).reshape(B, S, XA_HEADS, XA_HEAD_DIM)
    k, v = jnp.split(mn @ w_kv, 2, axis=-1)
    k = k.reshape(B, M, XA_HEADS, XA_HEAD_DIM)
    v = v.reshape(B, M, XA_HEADS, XA_HEAD_DIM)
    s = jnp.einsum('bshd,bmhd->bhsm', q, k).astype(jnp.float32) / math.sqrt(XA_HEAD_DIM)
    p = jax.nn.softmax(s, axis=-1).astype(v.dtype)
    o = jnp.einsum('bhsm,bmhd->bshd', p, v).reshape(B, S, XA_W)
    return o @ w_o


def conv_glu_ffn(h, w_up, conv_w, conv_b, w_down):
    S = h.shape[1]
    a, g = jnp.split(h @ w_up, 2, axis=-1)
    pad = CONV_W // 2
    gp = jnp.pad(g, ((0, 0), (pad, pad), (0, 0)))
    gc = conv_b + gp[:, 0:S] * conv_w[0]
    for j in range(1, CONV_W):
        gc = gc + gp[:, j:j + S] * conv_w[j]
    return (jax.nn.silu(gc) * a) @ w_down


def encode(x, mem, norm_mix_w, w_in, na_rpb, s5_lambda_re, s5_lambda_im, s5_log_dt, s5_b_re, s5_b_im,
           s5_c_re, s5_c_im, s5_d, s5_glu_w, s5_glu_b, gqa_q_norm_w, gqa_k_norm_w, hgrn_lower_bound,
           mix_out_norm_w, w_out, norm_xattn_w, norm_mem_w, xattn_w_q, xattn_w_kv, xattn_w_o,
           norm_ffn_w, ffn_w_up, ffn_conv_w, ffn_conv_b, ffn_w_down, final_norm_w):
    lb_all = hgrn2_lower_bounds(hgrn_lower_bound)
    for l in range(DEPTH):
        x = x + token_mixer(rms_norm(x, norm_mix_w[l]), w_in[l], na_rpb[l], s5_lambda_re[l], s5_lambda_im[l],
                            s5_log_dt[l], s5_b_re[l], s5_b_im[l], s5_c_re[l], s5_c_im[l], s5_d[l],
                            s5_glu_w[l], s5_glu_b[l], gqa_q_norm_w[l], gqa_k_norm_w[l], lb_all[l],
                            mix_out_norm_w[l], w_out[l])
        x = x + memory_cross_attention(rms_norm(x, norm_xattn_w[l]), mem, norm_mem_w[l],
                                       xattn_w_q[l], xattn_w_kv[l], xattn_w_o[l])
        x = x + conv_glu_ffn(rms_norm(x, norm_ffn_w[l]), ffn_w_up[l], ffn_conv_w[l], ffn_conv_b[l], ffn_w_down[l])
    return rms_norm(x, final_norm_w)


def setup_inputs(seed: int = 0) -> dict:
    key = jax.random.key(seed)
    ks = iter(jax.random.split(key, 48))
    f32 = jnp.float32

    def nrm(shape, scale):
        return jax.random.normal(next(ks), shape, f32) * scale

    def gain(shape):
        return 1.0 + 0.05 * jax.random.normal(next(ks), shape, f32)

    L = DEPTH
    lam_im_base = math.pi * jnp.arange(S5_STATE, dtype=f32)
    return {
        "x_prompt": nrm((BATCH, SEQ, D_MODEL), 1.0),
        "x_sample": nrm((DEC_BATCH, DEC_SEQ, D_MODEL), 1.0),
        "mem_prompt": nrm((BATCH, MEM_TOKENS, D_MODEL), 1.0),
        "mem_sample": nrm((DEC_BATCH, MEM_TOKENS, D_MODEL), 1.0),
        "norm_mix_w": gain((L, D_MODEL)),
        "w_in": nrm((L, D_MODEL, IN_W), D_MODEL ** -0.5),
        "na_rpb": nrm((L, NA_HEADS, 2 * NA_WIN_ROWS - 1, 2 * NA_WIN_COLS - 1), 0.1),
        "s5_lambda_re": -0.5 + nrm((L, 2, S5_GROUPS, S5_STATE), 0.01),
        "s5_lambda_im": lam_im_base + nrm((L, 2, S5_GROUPS, S5_STATE), 0.01),
        "s5_log_dt": jax.random.uniform(next(ks), (L, 2, S5_GROUPS), f32,
                                        math.log(S5_DT_MIN), math.log(S5_DT_MAX)),
        "s5_b_re": nrm((L, 2, S5_GROUPS, S5_STATE, S5_GROUP), (2 * S5_GROUP) ** -0.5),
        "s5_b_im": nrm((L, 2, S5_GROUPS, S5_STATE, S5_GROUP), (2 * S5_GROUP) ** -0.5),
        "s5_c_re": nrm((L, 2, S5_GROUPS, S5_GROUP, S5_STATE), (2 * S5_STATE) ** -0.5),
        "s5_c_im": nrm((L, 2, S5_GROUPS, S5_GROUP, S5_STATE), (2 * S5_STATE) ** -0.5),
        "s5_d": nrm((L, S5_WIDTH), 1.0),
        "s5_glu_w": nrm((L, S5_WIDTH, S5_WIDTH), S5_WIDTH ** -0.5),
        "s5_glu_b": nrm((L, S5_WIDTH), 0.01),
        "gqa_q_norm_w": gain((L, HEAD_DIM)),
        "gqa_k_norm_w": gain((L, HEAD_DIM)),
        "hgrn_lower_bound": nrm((L, HGRN_KW), 0.5),
        "mix_out_norm_w": gain((L, MIX_W)),
        "w_out": nrm((L, MIX_W, D_MODEL), MIX_W ** -0.5),
        "norm_xattn_w": gain((L, D_MODEL)),
        "norm_mem_w": gain((L, D_MODEL)),
        "xattn_w_q": nrm((L, D_MODEL, XA_W), D_MODEL ** -0.5),
        "xattn_w_kv": nrm((L, D_MODEL, 2 * XA_W), D_MODEL ** -0.5),
        "xattn_w_o": nrm((L, XA_W, D_MODEL), XA_W ** -0.5),
        "norm_ffn_w": gain((L, D_MODEL)),
        "ffn_w_up": nrm((L, D_MODEL, 2 * D_FF), D_MODEL ** -0.5),
        "ffn_conv_w": nrm((L, CONV_W, D_FF), CONV_W ** -0.5),
        "ffn_conv_b": nrm((L, D_FF), 0.01),
        "ffn_w_down": nrm((L, D_FF, D_MODEL), D_FF ** -0.5),
        "final_norm_w": gain((D_MODEL,)),
    }


def reference(x_prompt, x_sample, mem_prompt, mem_sample, norm_mix_w, w_in, na_rpb, s5_lambda_re, s5_lambda_im,
              s5_log_dt, s5_b_re, s5_b_im, s5_c_re, s5_c_im, s5_d, s5_glu_w, s5_glu_b, gqa_q_norm_w,
              gqa_k_norm_w, hgrn_lower_bound, mix_out_norm_w, w_out, norm_xattn_w, norm_mem_w, xattn_w_q,
              xattn_w_kv, xattn_w_o, norm_ffn_w, ffn_w_up, ffn_conv_w, ffn_conv_b, ffn_w_down, final_norm_w):
    y_prompt = encode(x_prompt, mem_prompt, norm_mix_w, w_in, na_rpb, s5_lambda_re, s5_lambda_im, s5_log_dt,
                      s5_b_re, s5_b_im, s5_c_re, s5_c_im, s5_d, s5_glu_w, s5_glu_b, gqa_q_norm_w, gqa_k_norm_w,
                      hgrn_lower_bound, mix_out_norm_w, w_out, norm_xattn_w, norm_mem_w, xattn_w_q, xattn_w_kv,
                      xattn_w_o, norm_ffn_w, ffn_w_up, ffn_conv_w, ffn_conv_b, ffn_w_down, final_norm_w)
    y_sample = encode(x_sample, mem_sample, norm_mix_w, w_in, na_rpb, s5_lambda_re, s5_lambda_im, s5_log_dt,
                      s5_b_re, s5_b_im, s5_c_re, s5_c_im, s5_d, s5_glu_w, s5_glu_b, gqa_q_norm_w, gqa_k_norm_w,
                      hgrn_lower_bound, mix_out_norm_w, w_out, norm_xattn_w, norm_mem_w, xattn_w_q, xattn_w_kv,
                      xattn_w_o, norm_ffn_w, ffn_w_up, ffn_conv_w, ffn_conv_b, ffn_w_down, final_norm_w)
    return (y_prompt, y_sample)
```

```python
import os
from contextlib import ExitStack

import numpy as np
import concourse.bass as bass
import concourse.mybir as mybir
from concourse.bass_utils import run_bass_kernel_spmd

F32 = mybir.dt.float32
BF16 = mybir.dt.bfloat16
AF = mybir.ActivationFunctionType
ALU = mybir.AluOpType
AX = mybir.AxisListType

SEM_LIMIT = 30000
SAME_ENGINE_WAITS = os.environ.get("SAMEW", "1") == "1"


class Buf:
    def __init__(self, name, t=None):
        self.name = name
        self.t = t
        self.writers = []
        self.readers = []
        self.war = []
        self.accum = False
        self.base = None
        self.scope = None
        self.ctr = None


class Counter:
    def __init__(self, sched, name, step):
        self.s = sched
        self.name = name
        self.step = step
        self.sem = None
        self.val = 0
        self.k = 0

    def bump(self):
        if self.sem is None or self.val + self.step > SEM_LIMIT:
            self.sem = self.s.new_sem(f"{self.name}_{self.k}")
            self.k += 1
            self.val = 0
        self.val += self.step
        return (self.sem, self.val)


class Op:
    __slots__ = ("eng", "fn", "reads", "writes", "is_dma", "deps", "sig", "tok", "strong")

    def __init__(self, eng, fn, reads, writes, is_dma):
        self.eng = eng
        self.fn = fn
        self.reads = reads
        self.writes = writes
        self.is_dma = is_dma
        self.deps = []
        self.sig = False
        self.tok = None
        self.strong = False


class Sched:
    ENGS = ("pe", "act", "dve", "pool", "sp")

    def __init__(self, nc):
        self.nc = nc
        self.ops = []
        self.stack = ExitStack()
        self.nsem = 0
        self.sems = []
        self.epoch = 0
        self.trace_names = None
        self.block_every = int(os.environ.get("MK_BLOCK_EVERY", "0"))

    def new_sem(self, name):
        self.nsem += 1
        s = self.stack.enter_context(self.nc.semaphore(f"s{self.nsem}_{name}"))
        self.sems.append(s)
        return s

    def sbuf(self, name, shape, dtype):
        t = self.stack.enter_context(self.nc.sbuf_tensor(name, list(shape), dtype))
        return Buf(name, t)

    def psum(self, name, shape, dtype):
        t = self.stack.enter_context(self.nc.psum_tensor(name, list(shape), dtype))
        return Buf(name, t)

    def dram(self, name, shape, dtype):
        t = self.nc.dram_tensor(name, list(shape), dtype, kind="Internal")
        return Buf(name, t)

    def dram_buf(self, name):
        return Buf(name)

    def view(self, name, t=None):
        return Buf(name, t)

    def op(self, eng, fn, reads=(), writes=()):
        if eng == "pool":
            eng = "dve"
        self.ops.append(Op(eng, fn, list(reads), list(writes), False))

    def barrier(self):
        self.epoch += 1
        for e in self.ENGS:
            o = Op(e, None, [], [], False)
            self.ops.append(o)
            o.sig = None
            o.tok = self.epoch

    def dma(self, eng, out, in_, reads=(), writes=(), **kw):
        if eng == "pool":
            eng = "act"
        def fn(e, out=out, in_=in_, kw=kw):
            return e.dma_start(out=out, in_=in_, **kw)

        self.ops.append(Op(eng, fn, list(reads), list(writes), True))


    def mm(self, out, lhsT, rhs, start, stop, reads, writes):
        self.op("pe", lambda e: e.matmul(out, lhsT, rhs, start=start, stop=stop), reads, writes)

    def tr(self, out, in_, ident, reads, writes):
        self.op("pe", lambda e: e.transpose(out, in_, ident), reads, writes)

    def act(self, out, in_, func, reads, writes, bias=None, scale=None, accum_out=None, eng="act"):
        kw = {}
        if bias is not None:
            kw["bias"] = bias
        if scale is not None:
            kw["scale"] = scale
        if accum_out is not None:
            kw["accum_out"] = accum_out
        self.op(eng, lambda e: e.activation(out=out, in_=in_, func=func, **kw), reads, writes)

    def tt(self, eng, out, in0, in1, op, reads, writes):
        self.op(eng, lambda e: e.tensor_tensor(out=out, in0=in0, in1=in1, op=op), reads, writes)

    def ts(self, eng, out, in0, s1, s2, op0, op1, reads, writes, accum_out=None):
        kw = {}
        if op1 is not None:
            kw["op1"] = op1
        if accum_out is not None:
            kw["accum_out"] = accum_out
        self.op(eng, lambda e: e.tensor_scalar(out=out, in0=in0, scalar1=s1, scalar2=s2, op0=op0, **kw), reads, writes)

    def stt(self, eng, out, in0, scalar, in1, op0, op1, reads, writes):
        self.op(eng, lambda e: e.scalar_tensor_tensor(out=out, in0=in0, scalar=scalar, in1=in1, op0=op0, op1=op1), reads, writes)

    def copy(self, eng, out, in_, reads, writes):
        if eng == "act":
            self.op(eng, lambda e: e.activation(out=out, in_=in_, func=AF.Copy), reads, writes)
        else:
            self.op(eng, lambda e: e.tensor_copy(out=out, in_=in_), reads, writes)

    def memset(self, eng, ap, val, writes):
        self.op(eng, lambda e: e.memset(ap, val), (), writes)
        self.ops[-1].strong = True

    def recip(self, out, in_, reads, writes):
        self.op("dve", lambda e: e.reciprocal(out=out, in_=in_), reads, writes)

    @staticmethod
    def _cbuf(o):
        w = o.writes[0]
        if getattr(w, "is_dram", False):
            for r in o.reads:
                if not getattr(r, "is_dram", False):
                    return r
        return w

    def finish(self):
        ops = self.ops
        last_eng = {}
        last_dma = {}
        for i, o in enumerate(ops):
            if o.fn is None:
                o.sig = False
                dl = [j for e2, j in last_eng.items() if e2 != o.eng] + list(last_dma.values())
                for j in dl:
                    ops[j].sig = True
                o.deps = sorted(set(dl))
                continue
            if o.is_dma:
                last_dma[id(self._cbuf(o))] = i
            else:
                last_eng[o.eng] = i
            deps = set()
            for b in o.reads:
                deps.update(b.writers)
            for b in o.writes:
                if b.readers:
                    b.war = b.readers
                    b.readers = []
                    b.writers = []
                deps.update(b.war)
                if o.strong or not b.accum:
                    deps.update(b.writers)
                    b.writers = [i]
                    b.base = i if o.strong else None
                else:
                    if b.base is not None:
                        deps.add(b.base)
                    b.writers.append(i)
            for b in o.reads:
                if b not in o.writes:
                    b.readers.append(i)
            deps.discard(i)
            dl = []
            for j in sorted(deps):
                pj = ops[j]
                if (not pj.is_dma) and (not o.is_dma) and pj.eng == o.eng and (o.eng == "pe" or not SAME_ENGINE_WAITS):
                    continue
                dl.append(j)
                pj.sig = True
            o.deps = dl
        ectr = {e: Counter(self, e, 1) for e in self.ENGS}
        free_ctrs = []
        active = []
        for o in ops:
            if o.fn is None:
                ep = o.tok
                o.tok = None
                keep = []
                for b in active:
                    if b.scope is not None and b.scope < ep:
                        free_ctrs.append(b.ctr)
                    else:
                        keep.append(b)
                active = keep
                continue
            if o.is_dma:
                b = self._cbuf(o)
                if b.ctr is None:
                    if b.scope is not None and free_ctrs:
                        b.ctr = free_ctrs.pop()
                    else:
                        b.ctr = Counter(self, "d_" + b.name, 16)
                    if b.scope is not None:
                        active.append(b)
                o.tok = b.ctr.bump()
            elif o.sig:
                o.tok = ectr[o.eng].bump()
        finals = {}
        for o in ops:
            if o.is_dma:
                finals[id(o.tok[0])] = o.tok
        per = {e: [] for e in self.ENGS}
        for i, o in enumerate(ops):
            per[o.eng].append(i)
        sidx = {}
        for o in ops:
            if o.tok is not None and id(o.tok[0]) not in sidx:
                sidx[id(o.tok[0])] = len(sidx)
        NS = len(sidx)
        seenv = {e: [0] * NS for e in self.ENGS}
        know = {}
        waits_of = {}
        for i, o in enumerate(ops):
            se = seenv[o.eng]
            wl = []
            for j in sorted(o.deps, reverse=True):
                sem, val = ops[j].tok
                k = sidx[id(sem)]
                if se[k] >= val:
                    continue
                wl.append((sem, val))
                kj = know[j]
                for q in range(NS):
                    if kj[q] > se[q]:
                        se[q] = kj[q]
            if wl:
                waits_of[i] = wl
            if o.tok is not None:
                kk = list(se)
                k = sidx[id(o.tok[0])]
                if o.tok[1] > kk[k]:
                    kk[k] = o.tok[1]
                know[i] = kk
        nc = self.nc
        self.n_waits = 0
        sched = self

        def emit(e, name, do_final):
            seen = {}
            for i in per[name]:
                o = ops[i]
                for sem, val in waits_of.get(i, ()):
                    seen[id(sem)] = max(seen.get(id(sem), 0), val)
                    e.wait_ge(sem, val)
                    sched.n_waits += 1
                if o.fn is None:
                    continue
                ins = o.fn(e)
                if sched.trace_names is not None:
                    try:
                        sched.trace_names[ins.ins.name] = (name, i, getattr(o, "desc", None))
                    except Exception:
                        pass
                if o.tok is not None:
                    ins.then_inc(o.tok[0], 16 if o.is_dma else 1)
            if do_final:
                for sem, val in finals.values():
                    if seen.get(id(sem), 0) < val:
                        e.wait_ge(sem, val)

        bounds = [0]
        if os.environ.get("MK_ONEBLOCK", "0") != "1":
            for i, o in enumerate(ops):
                if o.fn is None and o.eng == self.ENGS[0] and i > 0:
                    bounds.append(i)
        if self.block_every:
            bounds = sorted(set(bounds) | set(range(0, len(ops), self.block_every)))
        bounds.append(len(ops))
        seen_all = {e_: {} for e_ in self.ENGS}
        for bi in range(len(bounds) - 1):
            lo, hi = bounds[bi], bounds[bi + 1]
            lastblk = bi == len(bounds) - 2

            def emit_rng(e, name, lo=lo, hi=hi, lastblk=lastblk):
                seen = seen_all[name]
                for i in per[name]:
                    if i < lo or i >= hi:
                        continue
                    o = ops[i]
                    for sem, val in waits_of.get(i, ()):
                        seen[id(sem)] = max(seen.get(id(sem), 0), val)
                        e.wait_ge(sem, val)
                        sched.n_waits += 1
                    if o.fn is None:
                        continue
                    ins = o.fn(e)
                    if o.tok is not None:
                        ins.then_inc(o.tok[0], 16 if o.is_dma else 1)
                if lastblk and name == "sp":
                    for sem, val in finals.values():
                        if seen.get(id(sem), 0) < val:
                            e.wait_ge(sem, val)

            with nc.Block() as block:
                block.tensor(lambda e: emit_rng(e, "pe"))
                block.scalar(lambda e: emit_rng(e, "act"))
                block.vector(lambda e: emit_rng(e, "dve"))
                block.gpsimd(lambda e: emit_rng(e, "pool"))
                block.sync(lambda e: emit_rng(e, "sp"))
        self.stack.close()


D = 1024
INW = 2816
DFF = 2816
MEMT = 256
EPS = 1e-6
NEG = -30000.0
U8 = mybir.dt.uint8

O_QA, O_KA, O_VA, O_UB, O_QC, O_KC, O_VC, O_QD, O_ZF, O_ZB, O_VD, O_GD = (
    0, 256, 512, 768, 1024, 1280, 1408, 1536, 1792, 2048, 2304, 2560)

WEIGHT_SPECS = [
    ("norm_mix_w", (D,)), ("w_in", (D, INW)), ("na_rpb", (4, 15, 31)),
    ("s5_lambda_re", (2, 16, 64)), ("s5_lambda_im", (2, 16, 64)), ("s5_log_dt", (2, 16)),
    ("s5_b_re", (2, 16, 64, 16)), ("s5_b_im", (2, 16, 64, 16)),
    ("s5_c_re", (2, 16, 16, 64)), ("s5_c_im", (2, 16, 16, 64)),
    ("s5_d", (256,)), ("s5_glu_w", (256, 256)), ("s5_glu_b", (256,)),
    ("gqa_q_norm_w", (64,)), ("gqa_k_norm_w", (64,)), ("hgrn_lower_bound", (256,)),
    ("mix_out_norm_w", (D,)), ("w_out", (D, D)), ("norm_xattn_w", (D,)), ("norm_mem_w", (D,)),
    ("xattn_w_q", (D, 256)), ("xattn_w_kv", (D, 512)), ("xattn_w_o", (256, D)),
    ("norm_ffn_w", (D,)), ("ffn_w_up", (D, 2 * DFF)), ("ffn_conv_w", (3, DFF)),
    ("ffn_conv_b", (DFF,)), ("ffn_w_down", (DFF, D)),
]


class Arena:
    def __init__(self, sched, t, nbytes):
        self.sched = sched
        self.t = t
        self.n = nbytes
        self.off = 0

    def reset(self):
        self.off = 0

    def alloc(self, name, free_shape, dtype, parts=128):
        esz = 4 if dtype == F32 else 2
        n = int(np.prod(free_shape))
        nb = (n * esz + 31) // 32 * 32
        assert self.off + nb <= self.n, f"arena overflow at {name}: {self.off}+{nb}>{self.n}"
        v = self.t[0:parts, self.off // 4:(self.off + nb) // 4]
        scope = self.sched.epoch
        self.off += nb
        if dtype != F32:
            v = v.bitcast(dtype)
        v = v[:, 0:n]
        if len(free_shape) == 2:
            v = v.rearrange("p (a b) -> p a b", a=free_shape[0])
        elif len(free_shape) == 3:
            v = v.rearrange("p (a b c) -> p a b c", a=free_shape[0], b=free_shape[1])
        b = Buf(name, v)
        b.scope = scope
        return b


def host_consts(maxS):
    import ml_dtypes
    c = {}
    c["ident"] = np.eye(128, dtype=np.float32).astype(ml_dtypes.bfloat16)
    c["identf"] = np.eye(128, dtype=np.float32)
    t = np.arange(maxS)
    inv = 1.0 / (10000.0 ** (np.arange(0, 32, 2, dtype=np.float32) / 32.0))
    ang = np.concatenate([(t // 64).astype(np.float32)[:, None] * inv,
                          (t % 64).astype(np.float32)[:, None] * inv], axis=-1)
    cos = np.cos(ang).T.astype(np.float32)
    sin = np.sin(ang).T.astype(np.float32)
    c["ropec"] = np.concatenate([cos, cos], 0)
    c["ropes"] = np.concatenate([-sin, sin], 0)
    P = np.zeros((64, 64), np.float32)
    for i in range(32):
        P[32 + i, i] = 1.0
        P[i, 32 + i] = 1.0
    c["rotp"] = P.astype(ml_dtypes.bfloat16)
    c["ones64"] = np.full((64, 64), 1.0 / 64.0, np.float32).astype(ml_dtypes.bfloat16)
    cm = np.zeros((32, 64, 64), np.float32)
    for qc in range(64):
        c0 = int(np.clip(qc - 8, 0, 48))
        for kc in range(64):
            if c0 <= kc < c0 + 16:
                j = int(np.clip(kc - qc, -15, 15)) + 15
                cm[j, kc, qc] = 1.0
            else:
                cm[31, kc, qc] = NEG
    c["na_cm"] = cm.reshape(32, 4096)
    s_ = np.arange(128)
    c["tri_le"] = (s_[:, None] <= s_[None, :]).astype(np.float32)
    c["tri_ge"] = (s_[:, None] >= s_[None, :]).astype(np.float32)
    return c


CONST_SPECS = [("ident", (128, 128), BF16), ("identf", (128, 128), F32), ("ropec", None, F32), ("ropes", None, F32),
               ("rotp", (64, 64), BF16), ("ones64", (64, 64), BF16), ("na_cm", (32, 4096), F32),
               ("tri_le", (128, 128), F32), ("tri_ge", (128, 128), F32)]


def _kpow_table():
    ks = [8.0 * (2 ** m) for m in range(12)]
    f = [7 - s for s in range(8)] + [t - 7 for t in range(8)] + [t + 1 for t in range(8)] + ks
    b = [s for s in range(8)] + [-t for t in range(8)] + [8 - t for t in range(8)] + ks
    f = f + [1.0]
    b = b + [1.0]
    return np.array([f, b], np.float32)


KP_P, KP_Q, KP_C, KP_KS, KP_ONE, NKP = 0, 8, 16, 24, 36, 37
MAGIC = 12582912.0
TWO_PI = 6.283185307179586


class Prog:
    def __init__(self, seqs, L=2, dbg=(), layers=None):
        self.seqs = list(seqs)
        self.L = L
        self.layers = list(range(L)) if layers is None else list(layers)
        self.dbg = dbg
        self.maxS = maxS = max(seqs)
        nc = self.nc = bass.Bass("TRN2", target_bir_lowering=False)
        S = self.S = Sched(nc)
        self.ext = Buf("ext")
        self.ext.is_dram = True
        self.x_in = [nc.dram_tensor(f"x{i}", [s, D], F32, kind="ExternalInput").ap() for i, s in enumerate(seqs)]
        self.m_in = [nc.dram_tensor(f"m{i}", [MEMT, D], F32, kind="ExternalInput").ap() for i, s in enumerate(seqs)]
        self.y_out = [nc.dram_tensor(f"y{i}", [s, D], F32, kind="ExternalOutput").ap() for i, s in enumerate(seqs)]
        self.yb = [Buf(f"y{i}") for i in range(len(seqs))]
        for b in self.yb:
            b.accum = True
            b.is_dram = True
        self.W = {n: nc.dram_tensor(n, [L] + list(sh), F32, kind="ExternalInput").ap() for n, sh in WEIGHT_SPECS}
        self.fnw = nc.dram_tensor("final_norm_w", [D], F32, kind="ExternalInput").ap()
        self.C = {}
        for n, sh, dt in CONST_SPECS:
            if sh is None:
                sh = (64, maxS)
            self.C[n] = nc.dram_tensor("c_" + n, list(sh), dt, kind="ExternalInput").ap()
        self.C["tri_gt"] = nc.dram_tensor("c_tri_gt", [128, 128], F32, kind="ExternalInput").ap()
        self.C["tri_lt"] = nc.dram_tensor("c_tri_lt", [128, 128], F32, kind="ExternalInput").ap()
        self.C["kpow"] = nc.dram_tensor("c_kpow", [2, NKP], F32, kind="ExternalInput").ap()
        self.C["bmask"] = nc.dram_tensor("c_bmask", [2, 128, 128], F32, kind="ExternalInput").ap()

        def scr(name, shape, dt):
            kind = "ExternalOutput" if name in dbg else "Internal"
            t = nc.dram_tensor("s_" + name, list(shape), dt, kind=kind).ap()
            b = Buf(name, t)
            b.accum = True
            b.is_dram = True
            return b
        self.xA = scr("xA", [maxS, D], F32)
        self.x2 = scr("x2", [maxS, D], F32)
        self.x2b = scr("x2b", [maxS, D], F32)
        self.qaT = scr("qaT", [4, 64, maxS], BF16)
        self.kaT = scr("kaT", [4, 64, maxS], BF16)
        self.qcT = scr("qcT", [4, 64, maxS], BF16)
        self.kcT = scr("kcT", [2, 64, maxS], BF16)
        self.qdT = scr("qdT", [4, 64, maxS], BF16)
        self.va = scr("va", [maxS, 256], BF16)
        self.ub = scr("ub", [maxS, 256], BF16)
        self.vc = scr("vc", [maxS, 128], BF16)
        self.zf = scr("zf", [maxS, 512], F32)
        self.vd = scr("vd", [maxS, 256], BF16)
        self.gd = scr("gd", [maxS, 256], F32)
        self.mixA = scr("mixA", [maxS, 256], F32)
        self.mixB = scr("mixB", [maxS, 256], F32)
        self.mixC = scr("mixC", [maxS, 256], F32)
        self.mixD = scr("mixD", [maxS, 256], F32)
        self.mixDf = scr("mixDf", [maxS, 256], F32)
        self.Gd = scr("Gd", [4, 15, 4096], F32)
        self.ident = S.sbuf("ident", [128, 128], BF16)
        self.identf = S.sbuf("identf", [128, 128], F32)
        S.dma("sp", self.ident.t[:, :], self.C["ident"], reads=[self.ext], writes=[self.ident])
        S.dma("sp", self.identf.t[:, :], self.C["identf"], reads=[self.ext], writes=[self.identf])
        ARENA_F32 = 45056
        at = S.stack.enter_context(nc.sbuf_tensor("arena", [128, ARENA_F32], F32))
        self.A = Arena(S, at, ARENA_F32 * 4)
        self.banks = []
        for i in range(8):
            b = S.psum(f"bank{i}", [128, 512], F32)
            b.f = b.t[:, :]
            b.bf = b.t[:, :].bitcast(BF16)
            self.banks.append(b)
        self.bi = 0

    def bank(self):
        b = self.banks[self.bi % 8]
        self.bi += 1
        return b

    def load_w(self, dst, src, gain, K, N, tag, dap=None):
        S, A = self.S, self.A
        nchunk = (N + 1407) // 1408
        cw = N // nchunk
        cache = getattr(self, "_stg", None)
        if cache is not None and cache[0] == S.epoch and cache[1] >= cw:
            stg = cache[2]
        else:
            stg = [A.alloc(f"{tag}_stg{i}", [max(cw, 1408)], F32) for i in range(2)]
            self._stg = (S.epoch, max(cw, 1408), stg)
        gt = None
        if gain is not None:
            gt = A.alloc(f"{tag}_g", [K], F32)
            S.dma("sp", gt.t, gain.rearrange("(k p) -> p k", p=128), reads=[self.ext], writes=[gt],
                  allow_slow_non_contiguous=True)
        n = 0
        for k in range(K):
            for c in range(nchunk):
                st = stg[n % 2]
                S.dma("sp" if n % 2 == 0 else "act", st.t[:, 0:cw], src[k * 128:(k + 1) * 128, c * cw:(c + 1) * cw],
                      reads=[self.ext], writes=[st])
                o = (dst.t if dap is None else dap)[:, k, c * cw:(c + 1) * cw]
                if gt is not None:
                    if n % 2 == 0:
                        S.act(o, st.t[:, 0:cw], AF.Copy, [st, gt], [dst], scale=gt.t[:, k:k + 1])
                    else:
                        S.ts("pool", o, st.t[:, 0:cw], gt.t[:, k:k + 1], None, ALU.mult, None, [st, gt], [dst])
                else:
                    S.copy("act" if n % 2 == 0 else "pool", o, st.t[:, 0:cw], [st], [dst])
                n += 1

    def rsqrt(self, ob, o, i, ibufs):
        S = self.S
        S.ts("dve", o, i, EPS, None, ALU.add, None, ibufs, [ob])
        S.recip(o, o, [ob], [ob])
        S.act(o, o, AF.Sqrt, [ob], [ob])

    def norm_tile(self, xt, nj, xn, ss, junk):
        S = self.S
        S.memset("dve", ss.t, 0.0, [ss])
        for j in range(nj):
            S.act(junk.t, xt.t[:, j, :], AF.Square, [xt, ss], [junk, ss], scale=1.0 / 32.0, accum_out=ss.t[:, j:j + 1])
        self.rsqrt(ss, ss.t, ss.t, [ss])
        for j in range(nj):
            S.act(xn.t[:, j, :], xt.t[:, j, :], AF.Copy, [xt, ss], [xn], scale=ss.t[:, j:j + 1])

    def transpose_to(self, hT, xn, nj, ncol0=0):
        S = self.S
        for k in range(8):
            b = self.bank()
            for j in range(nj):
                S.tr(b.bf[:, j * 128:(j + 1) * 128], xn.t[:, j, k * 128:(k + 1) * 128], self.ident.t[:, :],
                     [xn, self.ident], [b])
            S.copy("dve" if k % 2 == 0 else "act", hT.t[:, k, ncol0:ncol0 + nj * 128], b.bf[:, 0:nj * 128], [b], [hT])

    def phase_A(self, l, si):
        S, A, W = self.S, self.A, self.W
        Sq = self.seqs[si]
        S.barrier()
        A.reset()
        if l == self.layers[0]:
            xsrc, xb = self.x_in[si], self.ext
        else:
            xsrc, xb = self.xA.t, self.xA
        w = A.alloc("w_in", [8, INW], BF16)
        self.load_w(w, W["w_in"][l], W["norm_mix_w"][l], 8, INW, "win")
        import os
        STG = int(os.environ.get("STG", "9"))
        if STG <= 1:
            return
        qnw = A.alloc("qnw", [1], F32, parts=64)
        knw = A.alloc("knw", [1], F32, parts=64)
        S.dma("sp", qnw.t, W["gqa_q_norm_w"][l].rearrange("(p o) -> p o", o=1), reads=[self.ext], writes=[qnw])
        S.dma("sp", knw.t, W["gqa_k_norm_w"][l].rearrange("(p o) -> p o", o=1), reads=[self.ext], writes=[knw])
        rotp = A.alloc("rotp", [64], BF16, parts=64)
        ones64 = A.alloc("ones64", [64], BF16, parts=64)
        S.dma("sp", rotp.t, self.C["rotp"], reads=[self.ext], writes=[rotp])
        S.dma("sp", ones64.t, self.C["ones64"], reads=[self.ext], writes=[ones64])
        xts = [A.alloc(f"xt{i}", [4, D], F32) for i in range(2)]
        xn = A.alloc("xn", [4, D], BF16)
        junk = A.alloc("junk", [D], BF16)
        sss = [A.alloc(f"ss{i}", [4], F32) for i in range(2)]
        hTs = [A.alloc(f"hT{i}", [8, 512], BF16) for i in range(2)]
        rcs = [A.alloc(f"rc{i}", [512], F32, parts=64) for i in range(2)]
        rss = [A.alloc(f"rs{i}", [512], F32, parts=64) for i in range(2)]
        st_va = [A.alloc(f"st_va{i}", [4, 256], BF16) for i in range(2)]
        st_ub = [A.alloc(f"st_ub{i}", [4, 256], BF16) for i in range(2)]
        st_vc = [A.alloc(f"st_vc{i}", [4, 128], BF16) for i in range(2)]
        st_zf = [A.alloc(f"st_zf{i}", [4, 512], F32) for i in range(2)]
        st_vd = [A.alloc(f"st_vd{i}", [4, 256], BF16) for i in range(2)]
        st_gd = [A.alloc(f"st_gd{i}", [4, 256], F32) for i in range(2)]
        for lst in (st_va, st_ub, st_vc, st_zf, st_vd, st_gd):
            for b in lst:
                b.accum = True
        fst = [A.alloc(f"fst{i}", [512], BF16, parts=64) for i in range(4)]
        sq = A.alloc("sq", [512], BF16, parts=64)
        rstd = A.alloc("rstd", [512], F32, parts=64)
        qh = A.alloc("qh", [512], BF16, parts=64)
        t1 = A.alloc("t1", [512], F32, parts=64)
        t2 = A.alloc("t2", [512], F32, parts=64)
        nf = 0
        for ti in range(Sq // 512):
            t0 = ti * 512
            p = ti % 2
            xt, ss, hT = xts[p], sss[p], hTs[p]
            S.dma("sp", xt.t, xsrc[t0:t0 + 512, :].rearrange("(j p) d -> p j d", p=128), reads=[xb], writes=[xt])
            S.dma("sp", rcs[p].t, self.C["ropec"][:, t0:t0 + 512], reads=[self.ext], writes=[rcs[p]])
            S.dma("sp", rss[p].t, self.C["ropes"][:, t0:t0 + 512], reads=[self.ext], writes=[rss[p]])
            self.norm_tile(xt, 4, xn, ss, junk)
            if STG <= 2:
                return
            self.transpose_to(hT, xn, 4)
            if STG <= 3:
                return
            for j in range(4):
                for (c0, n, outs) in ((O_VA, 512, ((st_va[p], 0, 256), (st_ub[p], 256, 256))),
                                      (O_VC, 128, ((st_vc[p], 0, 128),)),
                                      (O_ZF, 512, ((st_zf[p], 0, 512),)),
                                      (O_VD, 512, ((st_vd[p], 0, 256), (st_gd[p], 256, 256)))):
                    b = self.bank()
                    for k in range(8):
                        S.mm(b.f[:, 0:n], hT.t[:, k, j * 128:(j + 1) * 128], w.t[:, k, c0:c0 + n], k == 0, k == 7,
                             [hT, w, b], [b])
                    if os.environ.get("NOEV"):
                        continue
                    for ii, (st, o0, nn) in enumerate(outs):
                        S.copy(os.environ.get("EVE", "dve"), st.t[:, j, :], b.f[:, o0:o0 + nn], [b], [st])
            for st, dst in ((st_va[p], self.va), (st_ub[p], self.ub), (st_vc[p], self.vc), (st_zf[p], self.zf),
                            (st_vd[p], self.vd), (st_gd[p], self.gd)):
                if os.environ.get("NOST"):
                    continue
                S.dma(os.environ.get("STQ", "pool"), dst.t[t0:t0 + 512, :].rearrange("(j p) c -> p j c", p=128), st.t, reads=[st], writes=[dst])
            if STG <= 4:
                return
            for (c0, nh, dst, nw) in ((O_QA, 4, self.qaT, None), (O_KA, 4, self.kaT, None), (O_QD, 4, self.qdT, None),
                                      (O_QC, 4, self.qcT, qnw), (O_KC, 2, self.kcT, knw)):
                for h in range(nh):
                    b = self.bank()
                    for k in range(8):
                        S.mm(b.f[0:64, :], w.t[:, k, c0 + h * 64:c0 + (h + 1) * 64], hT.t[:, k, :], k == 0, k == 7,
                             [hT, w, b], [b])
                    fs = fst[nf % 4]
                    nf += 1
                    if nw is None:
                        S.copy("act", fs.t, b.f[0:64, :], [b], [fs])
                    else:
                        S.act(sq.t, b.f[0:64, :], AF.Square, [b], [sq])
                        b2 = self.bank()
                        S.mm(b2.f[0:64, :], ones64.t, sq.t, True, True, [ones64, sq], [b2])
                        self.rsqrt(rstd, rstd.t, b2.f[0:64, :], [b2])
                        S.stt("dve", qh.t, b.f[0:64, :], nw.t[:, 0:1], rstd.t, ALU.mult, ALU.mult, [b, nw, rstd], [qh])
                        b3 = self.bank()
                        S.mm(b3.f[0:64, :], rotp.t, qh.t, True, True, [rotp, qh], [b3])
                        S.tt("pool", t1.t, qh.t, rcs[p].t, ALU.mult, [qh, rcs[p]], [t1])
                        S.tt("dve", t2.t, b3.f[0:64, :], rss[p].t, ALU.mult, [b3, rss[p]], [t2])
                        S.tt("pool", fs.t, t1.t, t2.t, ALU.add, [t1, t2], [fs])
                    S.dma("pool", dst.t[h, :, t0:t0 + 512], fs.t, reads=[fs], writes=[dst])

    def phase_NA(self, l, si):
        S, A, W = self.S, self.A, self.W
        Sq = self.seqs[si]
        rows = Sq // 64
        S.barrier()
        A.reset()
        rpbT = A.alloc("rpbT", [4, 15], F32, parts=32)
        S.memset("dve", rpbT.t, 1.0, [rpbT])
        S.dma("sp", rpbT.t[0:31], W["na_rpb"][l].rearrange("h r j -> j h r"), reads=[self.ext], writes=[rpbT],
              allow_slow_non_contiguous=True)
        cm = A.alloc("cm", [4096], F32, parts=32)
        S.dma("sp", cm.t, self.C["na_cm"], reads=[self.ext], writes=[cm])
        gst = A.alloc("gst", [4096], F32, parts=15)
        gst.accum = True
        for h in range(4):
            for c in range(8):
                b = self.bank()
                S.mm(b.f[0:15, :], rpbT.t[:, h, :], cm.t[:, c * 512:(c + 1) * 512], True, True, [rpbT, cm], [b])
                S.copy("act" if c % 2 else "dve", gst.t[:, c * 512:(c + 1) * 512], b.f[0:15, :], [b], [gst])
            S.dma("sp", self.Gd.t[h], gst.t, reads=[gst], writes=[self.Gd])
        Gsh = A.alloc("Gsh", [4, 16, 64], F32)
        Gsh.accum = True
        S.memset("pool", Gsh.t, 0.0, [Gsh])
        for h in range(4):
            S.dma("sp", Gsh.t[0:64, h, 0:15, :], self.Gd.t[h].rearrange("r (k q) -> k r q", q=64),
                  reads=[self.Gd], writes=[Gsh])
            S.dma("sp", Gsh.t[64:128, h, 0:14, :], self.Gd.t[h, 1:15, :].rearrange("r (k q) -> k r q", q=64),
                  reads=[self.Gd], writes=[Gsh])
        BR = min(32, rows)
        qT = [A.alloc(f"na_q{i}", [4, BR * 64], BF16, parts=64) for i in range(1)]
        kT = [A.alloc(f"na_k{i}", [4, (BR + 7) * 64], BF16, parts=64) for i in range(1)]
        NP = (BR + 7 + 1) // 2
        Ve = [A.alloc(f"na_ve{i}", [NP, 4, 65], BF16) for i in range(1)]
        Vo = [A.alloc(f"na_vo{i}", [NP, 4, 65], BF16) for i in range(1)]
        sb = [A.alloc(f"na_s{i}", [4, 64], F32) for i in range(2)]
        pT = [A.alloc(f"na_p{i}", [4, 64], BF16) for i in range(2)]
        rec = [A.alloc(f"na_r{i}", [1], F32, parts=64) for i in range(2)]
        og = [A.alloc(f"na_o{i}", [8, 256], F32, parts=64) for i in range(2)]
        for b_ in og + Ve + Vo:
            b_.accum = True
        n = 0
        for bi, rb0 in enumerate(range(0, rows, BR)):
            rb1 = rb0 + BR
            p = 0
            r0f = lambda r: int(np.clip(r - 4, 0, rows - 8))
            kr_lo = r0f(rb0)
            kr_hi = r0f(rb1 - 1) + 8
            nkr = kr_hi - kr_lo
            S.dma("sp", qT[p].t, self.qaT.t[:, :, rb0 * 64:rb1 * 64].rearrange("h d s -> d h s"),
                  reads=[self.qaT], writes=[qT[p]])
            S.dma("sp", kT[p].t[:, :, 0:nkr * 64], self.kaT.t[:, :, kr_lo * 64:kr_hi * 64].rearrange("h d s -> d h s"),
                  reads=[self.kaT], writes=[kT[p]])
            S.memset("pool", Ve[p].t, 1.0, [Ve[p]])
            S.memset("pool", Vo[p].t, 1.0, [Vo[p]])
            npe = nkr // 2
            npo = (nkr - 1) // 2
            for h in range(4):
                S.dma("sp", Ve[p].t[:, 0:npe, h, 0:64],
                      self.va.t[kr_lo * 64:(kr_lo + 2 * npe) * 64, h * 64:(h + 1) * 64].rearrange("(m p) d -> p m d", p=128),
                      reads=[self.va], writes=[Ve[p]])
                S.dma("sp", Vo[p].t[:, 0:npo, h, 0:64],
                      self.va.t[(kr_lo + 1) * 64:(kr_lo + 1 + 2 * npo) * 64, h * 64:(h + 1) * 64].rearrange("(m p) d -> p m d", p=128),
                      reads=[self.va], writes=[Vo[p]])
            for r in range(rb0, rb1):
                kr = r0f(r)
                o = kr - r
                ogb = og[(r // 8) % 2]
                for h in range(4):
                    q = n % 2
                    n += 1
                    b = self.bank()
                    for c in range(4):
                        ko = (kr + 2 * c - kr_lo) * 64
                        S.mm(b.f[:, c * 64:(c + 1) * 64], kT[p].t[:, h, ko:ko + 128],
                             qT[p].t[:, h, (r - rb0) * 64:(r - rb0 + 1) * 64], True, True, [kT[p], qT[p]], [b])
                    S.stt("dve", sb[q].t, b.f[:, 0:256].rearrange("p (c q) -> p c q", c=4), 0.125,
                          Gsh.t[:, h, o + 7:o + 15:2, :], ALU.mult, ALU.add, [b, Gsh], [sb[q]])
                    S.act(pT[q].t, sb[q].t, AF.Exp, [sb[q]], [pT[q]])
                    b2 = self.bank()
                    for c in range(4):
                        rel = kr + 2 * c - kr_lo
                        Vs = Ve[p] if rel % 2 == 0 else Vo[p]
                        S.mm(b2.f[0:64, 0:65], pT[q].t[:, c, :], Vs.t[:, rel // 2, h, :], c == 0, c == 3,
                             [pT[q], Vs, b2], [b2])
                    S.recip(rec[q].t, b2.f[0:64, 64:65], [b2], [rec[q]])
                    S.ts("dve", ogb.t[:, r % 8, h * 64:(h + 1) * 64], b2.f[0:64, 0:64], rec[q].t[:, 0:1], None,
                         ALU.mult, None, [b2, rec[q]], [ogb])
                if r % 8 == 7:
                    S.dma("pool", self.mixA.t[(r - 7) * 64:(r + 1) * 64, :].rearrange("(r p) c -> p r c", p=64),
                          ogb.t, reads=[ogb], writes=[self.mixA])

    def phase_GQA(self, l, si):
        S, A = self.S, self.A
        Sq = self.seqs[si]
        nkt = Sq // 128
        S.barrier()
        A.reset()
        kT = A.alloc("g_k", [2, Sq], BF16, parts=64)
        S.dma("sp", kT.t, self.kcT.t[:, :, 0:Sq].rearrange("h d s -> d h s"), reads=[self.kcT], writes=[kT])
        V = A.alloc("g_v", [nkt, 2, 65], BF16)
        V.accum = True
        S.memset("pool", V.t, 1.0, [V])
        for h in range(2):
            S.dma("sp", V.t[:, :, h, 0:64], self.vc.t[0:Sq, h * 64:(h + 1) * 64].rearrange("(m p) d -> p m d", p=128),
                  reads=[self.vc], writes=[V])
        qT = [A.alloc(f"g_q{i}", [4, 512], BF16, parts=64) for i in range(2)]
        pT = [A.alloc(f"g_p{i}", [512], BF16) for i in range(3)]
        oT = A.alloc("g_oT", [512], F32, parts=65)
        rec = A.alloc("g_rec", [4], F32)
        osb = [A.alloc(f"g_o{i}", [4, 256], F32) for i in range(2)]
        for b_ in osb:
            b_.accum = True
        n = 0
        for qb in range(Sq // 512):
            p = qb % 2
            S.dma("sp", qT[p].t, self.qcT.t[:, :, qb * 512:(qb + 1) * 512].rearrange("h d s -> d h s"),
                  reads=[self.qcT], writes=[qT[p]])
            for h in range(4):
                hk = h // 2
                acc = self.bank()
                for kt in range(nkt):
                    b = self.bank()
                    if b is acc:
                        b = self.bank()
                    S.mm(b.f[:, :], kT.t[:, hk, kt * 128:(kt + 1) * 128], qT[p].t[:, h, :], True, True,
                         [kT, qT[p]], [b])
                    pp = pT[n % 3]
                    n += 1
                    S.act(pp.t, b.f[:, :], AF.Exp, [b], [pp], scale=0.125)
                    S.mm(acc.f[0:65, :], V.t[:, kt, hk, :], pp.t, kt == 0, kt == nkt - 1, [V, pp, acc], [acc])
                S.copy("dve", oT.t, acc.f[0:65, :], [acc], [oT])
                b = self.bank()
                for j in range(4):
                    S.tr(b.f[:, j * 65:(j + 1) * 65], oT.t[:, j * 128:(j + 1) * 128], self.identf.t[0:65, 0:65],
                         [oT, self.identf], [b])
                bv = b.f[:, 0:260].rearrange("p (j c) -> p j c", j=4)
                S.recip(rec.t, bv[:, :, 64], [b], [rec])
                S.tt("dve", osb[p].t[:, :, h * 64:(h + 1) * 64], bv[:, :, 0:64],
                     rec.t.unsqueeze(2).to_broadcast([128, 4, 64]), ALU.mult, [b, rec], [osb[p]])
            S.dma("pool", self.mixC.t[qb * 512:(qb + 1) * 512, :].rearrange("(j p) c -> p j c", p=128), osb[p].t,
                  reads=[osb[p]], writes=[self.mixC])

    def phase_HGRN(self, l, si):
        S, A, W = self.S, self.A, self.W
        Sq = self.seqs[si]
        nch = Sq // 128
        S.barrier()
        A.reset()
        L = self.L
        pj = A.alloc("pj", [L, 256], F32)
        pj.accum = True
        for j in range(L):
            S.dma("sp", pj.t[:, j, :], W["hgrn_lower_bound"][j].partition_broadcast(128), reads=[self.ext], writes=[pj])
        S.act(pj.t, pj.t, AF.Exp, [pj], [pj])
        tot = A.alloc("tot", [256], F32)
        lb = A.alloc("lb", [256], F32)
        oml = A.alloc("oml", [256], F32)
        S.copy("dve", tot.t, pj.t[:, 0, :], [pj], [tot])
        for j in range(1, L):
            S.tt("dve", tot.t, tot.t, pj.t[:, j, :], ALU.add, [tot, pj], [tot])
        S.memset("dve", lb.t, 0.0, [lb])
        for j in range(l):
            S.tt("dve", lb.t, lb.t, pj.t[:, j, :], ALU.add, [lb, pj], [lb])
        S.recip(tot.t, tot.t, [tot], [tot])
        S.tt("dve", lb.t, lb.t, tot.t, ALU.mult, [lb, tot], [lb])
        S.ts("dve", oml.t, lb.t, -1.0, 1.0, ALU.mult, ALU.add, [lb], [oml])
        msk = {}
        for n_ in ("tri_le", "tri_ge", "tri_gt", "tri_lt"):
            msk[n_] = A.alloc(n_, [128], F32)
            S.dma("sp", msk[n_].t, self.C[n_], reads=[self.ext], writes=[msk[n_]])
        zt = [A.alloc(f"zt{i}", [256], F32) for i in range(2)]
        vt = [A.alloc(f"vt{i}", [256], BF16) for i in range(2)]
        qT = [A.alloc(f"hq{i}", [4, 128], BF16, parts=64) for i in range(2)]
        ofw = [A.alloc(f"ofw{i}", [2, 256], F32, parts=64) for i in range(2)]
        e = A.alloc("e", [256], F32)
        f = A.alloc("f", [256], F32)
        logf = A.alloc("logf", [256], F32)
        kin = A.alloc("kin", [256], F32)
        bm = A.alloc("bm", [4], F32, parts=64)
        d1 = A.alloc("d1", [4, 128], F32, parts=64)
        e1 = A.alloc("e1", [4, 128], F32, parts=64)
        e2 = A.alloc("e2", [4, 128], F32, parts=64)
        e3 = A.alloc("e3", [4, 128], F32, parts=64)
        qtl = A.alloc("qtl", [4, 128], BF16, parts=64)
        ktl = A.alloc("ktl", [4, 128], BF16, parts=64)
        qbr = A.alloc("qbr", [4, 128], BF16, parts=64)
        ek = A.alloc("ek", [256], F32)
        khat = A.alloc("khat", [256], BF16)
        attF = A.alloc("attF", [4, 128], BF16, parts=64)
        attG = A.alloc("attG", [4, 64], BF16, parts=64)
        v2 = [A.alloc(f"v2{i}", [2, 256], BF16, parts=64) for i in range(2)]
        mF = [A.alloc(f"mF{i}", [128], F32, parts=64) for i in range(2)]
        mG = [A.alloc(f"mG{i}", [64], F32, parts=64) for i in range(2)]
        S.dma("sp", mF[0].t, self.C["tri_le"][0:64, :], reads=[self.ext], writes=[mF[0]])
        S.dma("sp", mF[1].t, self.C["tri_ge"][64:128, :], reads=[self.ext], writes=[mF[1]])
        S.dma("sp", mG[0].t, self.C["tri_le"][0:64, 0:64], reads=[self.ext], writes=[mG[0]])
        S.dma("sp", mG[1].t, self.C["tri_ge"][0:64, 0:64], reads=[self.ext], writes=[mG[1]])
        St = A.alloc("St", [4, 64], F32, parts=64)
        Sbf = A.alloc("Sbf", [4, 64], BF16, parts=64)
        osb = [A.alloc(f"ho{i}", [2, 256], F32, parts=64) for i in range(2)]
        for d in range(2):
            M1 = msk["tri_le"] if d == 0 else msk["tri_ge"]
            M2 = msk["tri_gt"] if d == 0 else msk["tri_lt"]
            order = range(nch) if d == 0 else range(nch - 1, -1, -1)
            for ci, c in enumerate(order):
                p = ci % 2
                t0 = c * 128
                first = ci == 0
                S.dma("sp", zt[p].t, self.zf.t[t0:t0 + 128, d * 256:(d + 1) * 256], reads=[self.zf], writes=[zt[p]])
                S.dma("sp", vt[p].t, self.vd.t[t0:t0 + 128, :], reads=[self.vd], writes=[vt[p]])
                S.dma("sp", v2[p].t, self.vd.t[t0:t0 + 128, :].rearrange("(a p) c -> p a c", p=64), reads=[self.vd], writes=[v2[p]])
                S.dma("sp", qT[p].t, self.qdT.t[:, :, t0:t0 + 128].rearrange("h d s -> d h s"), reads=[self.qdT], writes=[qT[p]])
                if d == 1:
                    S.dma("sp", ofw[p].t, self.mixDf.t[t0:t0 + 128, :].rearrange("(a p) c -> p a c", p=64), reads=[self.mixDf], writes=[ofw[p]])
                S.act(e.t, zt[p].t, AF.Exp, [zt[p]], [e], scale=-1.0)
                S.ts("dve", e.t, e.t, 1.0, None, ALU.add, None, [e], [e])
                S.recip(e.t, e.t, [e], [e])
                S.tt("dve", f.t, e.t, oml.t, ALU.mult, [e, oml], [f])
                S.tt("dve", f.t, f.t, lb.t, ALU.add, [f, lb], [f])
                S.act(logf.t, f.t, AF.Ln, [f], [logf])
                S.ts("dve", kin.t, f.t, -1.0, 1.0, ALU.mult, ALU.add, [f], [kin])
                b1, b2, b3 = self.bank(), self.bank(), self.bank()
                for h in range(4):
                    S.mm(b1.f[0:64, h * 128:(h + 1) * 128], logf.t[:, h * 64:(h + 1) * 64], M1.t, True, True, [logf, M1], [b1])
                S.mm(b2.f[:, 0:256], M2.t, logf.t, True, True, [logf, M2], [b2])
                for h in range(4):
                    S.tr(b3.f[0:64, h * 128:(h + 1) * 128], kin.t[:, h * 64:(h + 1) * 64], self.identf.t[:, :], [kin, self.identf], [b3])
                b1v = b1.f[0:64, :].rearrange("p (h t) -> p h t", h=4)
                S.copy("dve", bm.t, b1v[:, :, 64], [b1], [bm])
                S.tt("dve", d1.t, b1v, bm.t.unsqueeze(2).to_broadcast([64, 4, 128]), ALU.subtract, [b1, bm], [d1])
                S.act(e1.t, d1.t, AF.Exp, [d1], [e1])
                S.act(e2.t, d1.t, AF.Exp, [d1], [e2], scale=-1.0)
                S.act(e3.t, b1v, AF.Exp, [b1], [e3])
                S.tt("dve", qtl.t, qT[p].t, e1.t, ALU.mult, [qT[p], e1], [qtl])
                S.tt("dve", ktl.t, b3.f[0:64, :].rearrange("p (h t) -> p h t", h=4), e2.t, ALU.mult, [b3, e2], [ktl])
                S.tt("dve", qbr.t, qT[p].t, e3.t, ALU.mult, [qT[p], e3], [qbr])
                S.act(ek.t, b2.f[:, 0:256], AF.Exp, [b2], [ek])
                S.tt("dve", khat.t, kin.t, ek.t, ALU.mult, [kin, ek], [khat])
                Fs, Gs = (slice(0, 64), slice(64, 128)) if d == 0 else (slice(64, 128), slice(0, 64))
                fa, ga = (0, 1) if d == 0 else (1, 0)
                b4, b4g = self.bank(), self.bank()
                for h in range(4):
                    S.mm(b4.f[0:64, h * 128:(h + 1) * 128], ktl.t[:, h, Fs], qtl.t[:, h, :], True, True, [ktl, qtl], [b4])
                    S.mm(b4g.f[0:64, h * 64:(h + 1) * 64], ktl.t[:, h, Gs], qtl.t[:, h, Gs], True, True, [ktl, qtl], [b4g])
                S.tt("dve", attF.t, b4.f[0:64, :].rearrange("p (h t) -> p h t", h=4),
                     mF[d].t.unsqueeze(1).to_broadcast([64, 4, 128]), ALU.mult, [b4, mF[d]], [attF])
                S.tt("dve", attG.t, b4g.f[0:64, 0:256].rearrange("p (h t) -> p h t", h=4),
                     mG[d].t.unsqueeze(1).to_broadcast([64, 4, 64]), ALU.mult, [b4g, mG[d]], [attG])
                b5 = self.bank()
                for h in range(4):
                    oF = b5.f[0:64, fa * 256 + h * 64:fa * 256 + (h + 1) * 64]
                    oG = b5.f[0:64, ga * 256 + h * 64:ga * 256 + (h + 1) * 64]
                    vF = v2[p].t[:, fa, h * 64:(h + 1) * 64]
                    vG = v2[p].t[:, ga, h * 64:(h + 1) * 64]
                    S.mm(oF, attF.t[:, h, Fs], vF, True, first, [attF, v2[p]], [b5])
                    if not first:
                        S.mm(oF, qbr.t[:, h, Fs], Sbf.t[:, h, :], False, True, [qbr, Sbf, b5], [b5])
                    S.mm(oG, attF.t[:, h, Gs], vF, True, False, [attF, v2[p]], [b5])
                    S.mm(oG, attG.t[:, h, :], vG, False, first, [attG, v2[p], b5], [b5])
                    if not first:
                        S.mm(oG, qbr.t[:, h, Gs], Sbf.t[:, h, :], False, True, [qbr, Sbf, b5], [b5])
                b6 = self.bank()
                for h in range(4):
                    S.mm(b6.f[0:64, h * 64:(h + 1) * 64], khat.t[:, h * 64:(h + 1) * 64], vt[p].t[:, h * 64:(h + 1) * 64],
                         True, True, [khat, vt[p]], [b6])
                b6v = b6.f[0:64, 0:256].rearrange("p (h v) -> p h v", h=4)
                if first:
                    S.copy("dve", St.t, b6v, [b6], [St])
                else:
                    col = 127 if d == 0 else 0
                    S.tt("dve", St.t, St.t, e3.t[:, :, col:col + 1].to_broadcast([64, 4, 64]), ALU.mult, [St, e3], [St])
                    S.tt("dve", St.t, St.t, b6v, ALU.add, [St, b6], [St])
                S.copy("dve", Sbf.t, St.t, [St], [Sbf])
                o = osb[p]
                b5v = b5.f[0:64, :].rearrange("p (a c) -> p a c", a=2)
                if d == 0:
                    S.copy("dve", o.t, b5v, [b5], [o])
                    S.dma("pool", self.mixDf.t[t0:t0 + 128, :].rearrange("(a p) c -> p a c", p=64), o.t, reads=[o], writes=[self.mixDf])
                else:
                    S.tt("dve", o.t, b5v, ofw[p].t, ALU.add, [b5, ofw[p]], [o])
                    S.dma("pool", self.mixD.t[t0:t0 + 128, :].rearrange("(a p) c -> p a c", p=64), o.t, reads=[o], writes=[self.mixD])

    def phase_S5(self, l, si):
        S, A, W = self.S, self.A, self.W
        Sq = self.seqs[si]
        SEG = 1024
        assert Sq % SEG == 0
        T1 = SEG // 8
        nseg = Sq // SEG
        S.barrier()
        A.reset()
        PR = [A.alloc(f"PR{d}", [16, NKP], F32, parts=64) for d in range(2)]
        PI = [A.alloc(f"PI{d}", [16, NKP], F32, parts=64) for d in range(2)]
        TPb = A.alloc("TPb", [16, 128], BF16)
        TPb.accum = True
        LW = [[A.alloc(f"LW{d}{ri}", [16, 64], BF16) for ri in range(2)] for d in range(2)]
        CS = [[A.alloc(f"CS{d}{ri}", [16, 128], BF16, parts=64) for ri in range(2)] for d in range(2)]
        cf = [A.alloc(f"cf{ri}", [16], F32, parts=64) for ri in range(2)]
        cbs = [[A.alloc(f"cbs{g}_{ri}", [16], F32, parts=64) for ri in range(2)] for g in range(nseg)]
        mark = A.off
        acc = A.alloc("acc", [16, 128], F32)
        acc.accum = True
        dv = A.alloc("dv", [16], F32)
        dv.accum = True
        for s0 in range(8):
            S.dma("sp", dv.t[s0 * 16:(s0 + 1) * 16, :], W["s5_d"][l].rearrange("(g c) -> c g", c=16), reads=[self.ext],
                  writes=[dv], allow_slow_non_contiguous=True)
        bmk = [A.alloc(f"bmk{d}", [128], F32) for d in range(2)]
        for d in range(2):
            S.dma("sp", bmk[d].t, self.C["bmask"][d], reads=[self.ext], writes=[bmk[d]])
        lre = A.alloc("lre", [16], F32, parts=64)
        lim = A.alloc("lim", [16], F32, parts=64)
        dt = A.alloc("dt", [16], F32, parts=64)
        xx = A.alloc("xx", [16], F32, parts=64)
        ang = A.alloc("ang", [16], F32, parts=64)
        kp = A.alloc("kp", [NKP], F32, parts=64)
        Ta = A.alloc("Ta", [16, NKP], F32, parts=64)
        Tb = A.alloc("Tb", [16, NKP], F32, parts=64)
        Tc = A.alloc("Tc", [16, NKP], F32, parts=64)
        EX = A.alloc("EX", [16, NKP], F32, parts=64)
        SN = A.alloc("SN", [16, NKP], F32, parts=64)
        CN = A.alloc("CN", [16, NKP], F32, parts=64)
        sm = [A.alloc(f"sm{i}", [16], F32, parts=64) for i in range(6)]
        Bre = A.alloc("Bre", [16, 16], F32, parts=64)
        Bim = A.alloc("Bim", [16, 16], F32, parts=64)
        bR = A.alloc("bR", [16, 16], F32, parts=64)
        bI = A.alloc("bI", [16, 16], F32, parts=64)
        t16 = A.alloc("t16", [16, 16], F32, parts=64)
        Craw = A.alloc("Craw", [2, 64], F32)
        Cre = A.alloc("Cre", [16, 16], F32, parts=64)
        Cim = A.alloc("Cim", [16, 16], F32, parts=64)
        PmR = A.alloc("PmR", [16, 8, 16], F32, parts=64)
        PmI = A.alloc("PmI", [16, 8, 16], F32, parts=64)
        QmR = A.alloc("QmR", [16, 8, 16], F32, parts=64)
        NQmI = A.alloc("NQmI", [16, 8, 16], F32, parts=64)
        ta = A.alloc("ta", [16, 8, 16], F32, parts=64)
        tb = A.alloc("tb", [16, 8, 16], F32, parts=64)
        B4 = [64, 16, 8, 16]

        def bk(ap):
            return ap.unsqueeze(3).to_broadcast(B4)

        def bc(ap):
            return ap.unsqueeze(2).to_broadcast(B4)

        def cplx(oR, oI, pr, pi, xr, xi, neg_im, bufs_in):
            S.tt("dve", ta.t, bk(pr), bc(xr.t), ALU.mult, bufs_in, [ta])
            S.tt("dve", tb.t, bk(pi), bc(xi.t), ALU.mult, bufs_in, [tb])
            S.tt("dve", oR.t, ta.t, tb.t, ALU.subtract, [ta, tb], [oR])
            S.tt("dve", ta.t, bk(pr), bc(xi.t), ALU.mult, bufs_in, [ta])
            S.tt("dve", tb.t, bk(pi), bc(xr.t), ALU.mult, bufs_in, [tb])
            if neg_im:
                S.stt("dve", oI.t, ta.t, -1.0, tb.t, ALU.mult, ALU.subtract, [ta, tb], [oI])
            else:
                S.tt("dve", oI.t, ta.t, tb.t, ALU.add, [ta, tb], [oI])

        for d in range(2):
            S.dma("sp", lre.t, W["s5_lambda_re"][l, d].rearrange("g n -> n g"), reads=[self.ext], writes=[lre],
                  allow_slow_non_contiguous=True)
            S.dma("sp", lim.t, W["s5_lambda_im"][l, d].rearrange("g n -> n g"), reads=[self.ext], writes=[lim],
                  allow_slow_non_contiguous=True)
            S.dma("sp", dt.t, W["s5_log_dt"][l, d].partition_broadcast(64), reads=[self.ext], writes=[dt])
            S.dma("sp", kp.t, self.C["kpow"][d].partition_broadcast(64), reads=[self.ext], writes=[kp])
            S.dma("sp", Bre.t, W["s5_b_re"][l, d].rearrange("g n c -> n g c"), reads=[self.ext], writes=[Bre])
            S.dma("sp", Bim.t, W["s5_b_im"][l, d].rearrange("g n c -> n g c"), reads=[self.ext], writes=[Bim])
            S.ts("dve", lre.t, lre.t, -1e-4, None, ALU.min, None, [lre], [lre])
            S.act(dt.t, dt.t, AF.Exp, [dt], [dt])
            S.tt("dve", xx.t, lre.t, dt.t, ALU.mult, [lre, dt], [xx])
            S.tt("dve", ang.t, lim.t, dt.t, ALU.mult, [lim, dt], [ang])
            B3 = [64, 16, NKP]
            kpb = kp.t.unsqueeze(1).to_broadcast(B3)
            S.tt("dve", Ta.t, xx.t.unsqueeze(2).to_broadcast(B3), kpb, ALU.mult, [xx, kp], [Ta])
            S.act(EX.t, Ta.t, AF.Exp, [Ta], [EX])
            S.tt("dve", Ta.t, ang.t.unsqueeze(2).to_broadcast(B3), kpb, ALU.mult, [ang, kp], [Ta])
            for (dst, shift) in ((SN, 0.0), (CN, 1.5707963267948966)):
                if shift:
                    S.ts("dve", Tc.t, Ta.t, shift, None, ALU.add, None, [Ta], [Tc])
                    src = Tc
                else:
                    src = Ta
                S.ts("dve", Tb.t, src.t, 1.0 / TWO_PI, MAGIC, ALU.mult, ALU.add, [src], [Tb])
                S.ts("dve", Tb.t, Tb.t, -MAGIC, None, ALU.add, None, [Tb], [Tb])
                S.stt("dve", Tb.t, Tb.t, -TWO_PI, src.t, ALU.mult, ALU.add, [Tb, src], [Tb])
                S.act(dst.t, Tb.t, AF.Sin, [Tb], [dst])
            S.tt("dve", PR[d].t, EX.t, CN.t, ALU.mult, [EX, CN], [PR[d]])
            S.tt("dve", PI[d].t, EX.t, SN.t, ALU.mult, [EX, SN], [PI[d]])
            aR = PR[d].t[:, :, KP_ONE]
            aI = PI[d].t[:, :, KP_ONE]
            den, nre, cR, cI, u0, u1 = sm
            S.tt("dve", den.t, lre.t, lre.t, ALU.mult, [lre], [den])
            S.tt("dve", u0.t, lim.t, lim.t, ALU.mult, [lim], [u0])
            S.tt("dve", den.t, den.t, u0.t, ALU.add, [den, u0], [den])
            S.recip(den.t, den.t, [den], [den])
            S.ts("dve", nre.t, aR, -1.0, None, ALU.add, None, [PR[d]], [nre])
            S.tt("dve", u0.t, nre.t, lre.t, ALU.mult, [nre, lre], [u0])
            S.tt("dve", u1.t, aI, lim.t, ALU.mult, [PI[d], lim], [u1])
            S.tt("dve", u0.t, u0.t, u1.t, ALU.add, [u0, u1], [u0])
            S.tt("dve", cR.t, u0.t, den.t, ALU.mult, [u0, den], [cR])
            S.tt("dve", u0.t, aI, lre.t, ALU.mult, [PI[d], lre], [u0])
            S.tt("dve", u1.t, nre.t, lim.t, ALU.mult, [nre, lim], [u1])
            S.tt("dve", u0.t, u0.t, u1.t, ALU.subtract, [u0, u1], [u0])
            S.tt("dve", cI.t, u0.t, den.t, ALU.mult, [u0, den], [cI])
            B16 = [64, 16, 16]
            cRb = cR.t.unsqueeze(2).to_broadcast(B16)
            cIb = cI.t.unsqueeze(2).to_broadcast(B16)
            S.tt("dve", bR.t, cRb, Bre.t, ALU.mult, [cR, Bre], [bR])
            S.tt("dve", t16.t, cIb, Bim.t, ALU.mult, [cI, Bim], [t16])
            S.tt("dve", bR.t, bR.t, t16.t, ALU.subtract, [bR, t16], [bR])
            S.tt("dve", bI.t, cRb, Bim.t, ALU.mult, [cR, Bim], [bI])
            S.tt("dve", t16.t, cIb, Bre.t, ALU.mult, [cI, Bre], [t16])
            S.tt("dve", bI.t, bI.t, t16.t, ALU.add, [bI, t16], [bI])
            for (src_w, dstC) in (("s5_c_re", Cre), ("s5_c_im", Cim)):
                S.dma("sp", Craw.t, W[src_w][l, d].rearrange("(j g) c n -> (g c) j n", j=2), reads=[self.ext], writes=[Craw])
                b = self.bank()
                for j in range(2):
                    S.tr(b.f[0:64, j * 128:(j + 1) * 128], Craw.t[:, j, :], self.identf.t[:, :], [Craw, self.identf], [b])
                S.copy("dve", dstC.t, b.f[0:64, 0:256].rearrange("p (g c) -> p g c", g=16), [b], [dstC])
            pin = [PR[d], PI[d]]
            cplx(PmR, PmI, PR[d].t[:, :, KP_P:KP_P + 8], PI[d].t[:, :, KP_P:KP_P + 8], bR, bI, False, pin + [bR, bI])
            cplx(QmR, NQmI, PR[d].t[:, :, KP_Q:KP_Q + 8], PI[d].t[:, :, KP_Q:KP_Q + 8], Cre, Cim, True, pin + [Cre, Cim])
            prc, pic = PR[d].t[:, :, KP_C:KP_C + 8], PI[d].t[:, :, KP_C:KP_C + 8]
            cs0 = CS[d][0].t.rearrange("p g (t c) -> p g t c", t=8)
            cs1 = CS[d][1].t.rearrange("p g (t c) -> p g t c", t=8)
            S.tt("dve", ta.t, bk(prc), bc(Cre.t), ALU.mult, pin + [Cre], [ta])
            S.tt("dve", tb.t, bk(pic), bc(Cim.t), ALU.mult, pin + [Cim], [tb])
            S.tt("dve", cs0, ta.t, tb.t, ALU.subtract, [ta, tb], [CS[d][0]])
            S.tt("dve", ta.t, bk(prc), bc(Cim.t), ALU.mult, pin + [Cim], [ta])
            S.tt("dve", tb.t, bk(pic), bc(Cre.t), ALU.mult, pin + [Cre], [tb])
            S.stt("dve", cs1, ta.t, -1.0, tb.t, ALU.mult, ALU.subtract, [ta, tb], [CS[d][1]])
            for g in range(16):
                b = self.bank()
                S.mm(b.f[:, 0:128], PmR.t[:, g].rearrange("p s c -> p (s c)"), QmR.t[:, g].rearrange("p s c -> p (s c)"),
                     True, False, [PmR, QmR], [b])
                S.mm(b.f[:, 0:128], PmI.t[:, g].rearrange("p s c -> p (s c)"), NQmI.t[:, g].rearrange("p s c -> p (s c)"),
                     False, True, [PmI, NQmI, b], [b])
                if d == 0:
                    S.tt("dve", acc.t[:, g, :], b.f[:, 0:128], bmk[0].t, ALU.mult, [b, bmk[0]], [acc])
                else:
                    S.tt("dve", bmk[0].t, b.f[:, 0:128], bmk[1].t, ALU.mult, [b, bmk[1]], [bmk[0]])
                    S.tt("dve", acc.t[:, g, :], acc.t[:, g, :], bmk[0].t, ALU.add, [acc, bmk[0]], [acc])
            for ri, Pm in ((0, PmR), (1, PmI)):
                for g0 in (0, 8):
                    b = self.bank()
                    for j in range(8):
                        S.tr(b.f[:, j * 64:(j + 1) * 64], Pm.t[:, g0 + j].rearrange("p s c -> p (s c)"),
                             self.identf.t[0:64, 0:64], [Pm, self.identf], [b])
                    S.copy("dve", LW[d][ri].t[:, g0:g0 + 8, :], b.f[:, :].rearrange("p (g n) -> p g n", g=8), [b], [LW[d][ri]])
        for g in range(16):
            S.stt("dve", TPb.t[:, g, :], self.identf.t[:, :], dv.t[:, g:g + 1], acc.t[:, g, :], ALU.mult, ALU.add,
                  [self.identf, dv, acc], [TPb])
        S.barrier()
        A.off = mark
        U = A.alloc("U", [8, 16, 16], BF16)
        U2 = A.alloc("U2", [16, 8, 16], BF16)
        Ut = A.alloc("Ut", [16, T1], BF16)
        Ut.accum = True
        X = [[A.alloc(f"X{pp}{ri}", [16, T1], F32, parts=64) for ri in range(2)] for pp in range(2)]
        for pp in range(2):
            for ri in range(2):
                X[pp][ri].accum = True
        tmp = A.alloc("tmp", [16, T1], F32, parts=64)
        XP = [A.alloc(f"XP{ri}", [16, T1], BF16, parts=64) for ri in range(2)]
        XN = [A.alloc(f"XN{ri}", [16, T1], BF16, parts=64) for ri in range(2)]
        for b_ in XP + XN:
            b_.accum = True
        Yg = [A.alloc(f"Yg{i}", [T1], F32) for i in range(2)]
        Ytok = A.alloc("Ytok", [8, 16, 16], F32)
        Ytok.accum = True
        zc = A.alloc("zc", [16], F32, parts=64)
        S.memset("dve", zc.t, 0.0, [zc])
        ru = [A.alloc(f"ru{i}", [16], F32, parts=64) for i in range(2)]
        NLV = 7

        def load_seg(seg):
            S.dma("sp", U.t.rearrange("p s g c -> p (s g c)"),
                  self.ub.t[seg * SEG:(seg + 1) * SEG, :].rearrange("(t s) c -> t (s c)", s=8), reads=[self.ub], writes=[U])
            S.copy("act", U2.t, U.t.rearrange("p s g c -> p g s c"), [U], [U2])
            for g0 in range(0, 16, 4):
                b = self.bank()
                for j in range(4):
                    S.tr(b.bf[:, j * 128:(j + 1) * 128], U2.t[:, g0 + j].rearrange("p s c -> p (s c)"), self.ident.t[:, :],
                         [U2, self.ident], [b])
                S.copy("dve", Ut.t[:, g0:g0 + 4, :], b.bf[:, 0:512].rearrange("p (g t) -> p g t", g=4), [b], [Ut])

        def scan(d, carry):
            for ri in range(2):
                for g0 in range(0, 16, 4):
                    b = self.bank()
                    for j in range(4):
                        S.mm(b.f[0:64, j * T1:(j + 1) * T1], LW[d][ri].t[:, g0 + j, :], Ut.t[:, g0 + j, :], True, True,
                             [LW[d][ri], Ut], [b])
                    S.copy("act", X[0][ri].t[:, g0:g0 + 4, :], b.f[0:64, 0:4 * T1].rearrange("p (g t) -> p g t", g=4),
                           [b], [X[0][ri]])
            col = 0 if d == 0 else T1 - 1
            a8R = PR[d].t[:, :, KP_KS]
            a8I = PI[d].t[:, :, KP_KS]
            pin = [PR[d], PI[d]] + carry
            u0, u1 = ru
            for (ri, p1, c1, p2, c2, op2) in ((0, a8R, carry[0], a8I, carry[1], ALU.subtract),
                                               (1, a8R, carry[1], a8I, carry[0], ALU.add)):
                S.tt("dve", u0.t, p1, c1.t, ALU.mult, pin, [u0])
                S.tt("dve", u1.t, p2, c2.t, ALU.mult, pin, [u1])
                S.tt("dve", u0.t, u0.t, u1.t, op2, [u0, u1], [u0])
                S.tt("dve", X[0][ri].t[:, :, col], X[0][ri].t[:, :, col], u0.t, ALU.add, [X[0][ri], u0], [X[0][ri]])
            cur = 0
            for m in range(NLV):
                dist = 1 << m
                n = T1 - dist
                src, dstb = X[cur], X[1 - cur]
                if d == 0:
                    hi, lo, keep = slice(dist, T1), slice(0, n), slice(0, dist)
                else:
                    hi, lo, keep = slice(0, n), slice(dist, T1), slice(n, T1)
                cRb = PR[d].t[:, :, KP_KS + m].unsqueeze(2).to_broadcast([64, 16, n])
                cIb = PI[d].t[:, :, KP_KS + m].unsqueeze(2).to_broadcast([64, 16, n])
                tv = tmp.t[:, :, 0:n]
                for (ri, ca, xa, cb_, xb, op2) in ((0, cRb, 0, cIb, 1, ALU.subtract), (1, cRb, 1, cIb, 0, ALU.add)):
                    S.copy("act", dstb[ri].t[:, :, keep], src[ri].t[:, :, keep], [src[ri]], [dstb[ri]])
                    S.tt("dve", tv, ca, src[xa].t[:, :, lo], ALU.mult, [PR[d], src[xa]], [tmp])
                    S.tt("dve", dstb[ri].t[:, :, hi], src[ri].t[:, :, hi], tv, ALU.add, [src[ri], tmp], [dstb[ri]])
                    S.tt("dve", tv, cb_, src[xb].t[:, :, lo], ALU.mult, [PI[d], src[xb]], [tmp])
                    S.tt("dve", dstb[ri].t[:, :, hi], dstb[ri].t[:, :, hi], tv, op2, [dstb[ri], tmp], [dstb[ri]])
                cur = 1 - cur
            return X[cur]

        for ri in range(2):
            S.copy("dve", cbs[nseg - 1][ri].t, zc.t, [zc], [cbs[nseg - 1][ri]])
            S.copy("dve", cf[ri].t, zc.t, [zc], [cf[ri]])
        for seg in range(nseg - 1, 0, -1):
            load_seg(seg)
            Xs = scan(1, cbs[seg])
            for ri in range(2):
                S.copy("dve", cbs[seg - 1][ri].t, Xs[ri].t[:, :, 0], [Xs[ri]], [cbs[seg - 1][ri]])
        for seg in range(nseg):
            load_seg(seg)
            Xf = scan(0, cf)
            for ri in range(2):
                S.copy("act", XP[ri].t[:, :, 1:T1], Xf[ri].t[:, :, 0:T1 - 1], [Xf[ri]], [XP[ri]])
                S.copy("act", XP[ri].t[:, :, 0], cf[ri].t, [cf[ri]], [XP[ri]])
            for ri in range(2):
                S.copy("dve", cf[ri].t, Xf[ri].t[:, :, T1 - 1], [Xf[ri]], [cf[ri]])
            Xb = scan(1, cbs[seg])
            for ri in range(2):
                S.copy("act", XN[ri].t[:, :, 0:T1 - 1], Xb[ri].t[:, :, 1:T1], [Xb[ri]], [XN[ri]])
                S.copy("act", XN[ri].t[:, :, T1 - 1], cbs[seg][ri].t, [cbs[seg][ri]], [XN[ri]])
            for g0 in range(0, 16, 4):
                bt = self.bank()
                for j in range(4):
                    g = g0 + j
                    b = self.bank()
                    if b is bt:
                        b = self.bank()
                    S.mm(b.f[:, 0:T1], TPb.t[:, g, :], Ut.t[:, g, :], True, False, [TPb, Ut], [b])
                    S.mm(b.f[:, 0:T1], CS[0][0].t[:, g, :], XP[0].t[:, g, :], False, False, [CS[0][0], XP[0], b], [b])
                    S.mm(b.f[:, 0:T1], CS[0][1].t[:, g, :], XP[1].t[:, g, :], False, False, [CS[0][1], XP[1], b], [b])
                    S.mm(b.f[:, 0:T1], CS[1][0].t[:, g, :], XN[0].t[:, g, :], False, False, [CS[1][0], XN[0], b], [b])
                    S.mm(b.f[:, 0:T1], CS[1][1].t[:, g, :], XN[1].t[:, g, :], False, True, [CS[1][1], XN[1], b], [b])
                    yg = Yg[g % 2]
                    S.copy("act", yg.t, b.f[:, 0:T1], [b], [yg])
                    S.tr(bt.f[:, j * 128:(j + 1) * 128], yg.t, self.identf.t[:, :], [yg, self.identf], [bt])
                S.copy("dve", Ytok.t[:, :, g0:g0 + 4, :], bt.f[:, :].rearrange("p (g t c) -> p t g c", g=4, t=8), [bt], [Ytok])
            S.dma("act", self.mixB.t[seg * SEG:(seg + 1) * SEG, :].rearrange("(t s) c -> t (s c)", s=8),
                  Ytok.t.rearrange("p s g c -> p (s g c)"), reads=[Ytok], writes=[self.mixB])

    def phase_stub(self, l, si):
        S, A = self.S, self.A
        Sq = self.seqs[si]
        S.barrier()
        A.reset()
        z = A.alloc("z", [4, 256], F32)
        S.memset("dve", z.t, 0.0, [z])
        for ti in range(Sq // 512):
            for dst in (self.mixB, self.mixD):
                S.dma("sp", dst.t[ti * 512:(ti + 1) * 512, :].rearrange("(j p) c -> p j c", p=128), z.t,
                      reads=[z], writes=[dst])

    def phase_C1(self, l, si):
        S, A, W = self.S, self.A, self.W
        Sq = self.seqs[si]
        S.barrier()
        A.reset()
        if l == self.layers[0]:
            xsrc, xb = self.x_in[si], self.ext
        else:
            xsrc, xb = self.xA.t, self.xA
        wout = A.alloc("wout", [8, 1024], BF16)
        self.load_w(wout, W["w_out"][l], W["mix_out_norm_w"][l], 8, 1024, "wo")
        wq = A.alloc("wq", [8, 256], BF16)
        self.load_w(wq, W["xattn_w_q"][l], W["norm_xattn_w"][l], 8, 256, "wq")
        wkv = A.alloc("wkv", [8, 512], BF16)
        self.load_w(wkv, W["xattn_w_kv"][l], W["norm_mem_w"][l], 8, 512, "wkv")
        wo2 = A.alloc("wo2", [2, 1024], BF16)
        self.load_w(wo2, W["xattn_w_o"][l], None, 2, 1024, "wo2")
        glu = A.alloc("glu", [2, 256], BF16)
        self.load_w(glu, W["s5_glu_w"][l], None, 2, 256, "glu")
        glub = A.alloc("glub", [256], F32)
        S.dma("sp", glub.t, W["s5_glu_b"][l].partition_broadcast(128), reads=[self.ext], writes=[glub])
        junk = A.alloc("junk", [D], BF16)
        mt = A.alloc("mt", [2, D], F32)
        S.dma("sp", mt.t, self.m_in[si].rearrange("(j p) d -> p j d", p=128), reads=[self.ext], writes=[mt])
        mn = A.alloc("mn", [2, D], BF16)
        ssm = A.alloc("ssm", [2], F32)
        self.norm_tile(mt, 2, mn, ssm, junk)
        hTm = A.alloc("hTm", [8, 256], BF16)
        self.transpose_to(hTm, mn, 2)
        kmT = A.alloc("kmT", [4, 256], BF16, parts=64)
        kmT.accum = True
        for h in range(4):
            b = self.bank()
            for k in range(8):
                S.mm(b.f[0:64, 0:256], wkv.t[:, k, h * 64:(h + 1) * 64], hTm.t[:, k, :], k == 0, k == 7, [wkv, hTm, b], [b])
            S.copy("dve", kmT.t[:, h, :], b.f[0:64, 0:256], [b], [kmT])
        vm = A.alloc("vm", [2, 4, 65], BF16)
        vm.accum = True
        S.memset("dve", vm.t, 1.0, [vm])
        for c in range(2):
            b = self.bank()
            for k in range(8):
                S.mm(b.f[:, 0:256], hTm.t[:, k, c * 128:(c + 1) * 128], wkv.t[:, k, 256:512], k == 0, k == 7, [wkv, hTm, b], [b])
            S.copy("dve", vm.t[:, c, :, 0:64], b.f[:, 0:256].rearrange("p (h d) -> p h d", h=4), [b], [vm])
        mx = A.alloc("mx", [4, 256], F32)
        mx.accum = True
        gdt = A.alloc("gdt", [256], F32)
        xt = A.alloc("xt", [1, D], F32)
        u = A.alloc("u", [256], F32)
        sg = A.alloc("sg", [256], F32)
        hb = A.alloc("hb", [256], F32)
        hb16 = A.alloc("hb16", [1, 256], BF16)
        hbT = A.alloc("hbT", [2, 128], BF16)
        ssg = A.alloc("ssg", [4], F32)
        mg = A.alloc("mg", [1, D], BF16)
        mg.accum = True
        sil = A.alloc("sil", [256], F32)
        mT = A.alloc("mT", [8, 128], BF16)
        x1 = A.alloc("x1", [1, D], F32)
        x1.accum = True
        xn1 = A.alloc("xn1", [1, D], BF16)
        ss1 = A.alloc("ss1", [1], F32)
        h1T = A.alloc("h1T", [8, 128], BF16)
        qT = A.alloc("qT", [4, 128], BF16, parts=64)
        qT.accum = True
        pT = A.alloc("pT", [2, 128], BF16)
        rec = A.alloc("rec", [1], F32)
        xo = A.alloc("xo", [1, 256], BF16)
        xo.accum = True
        xoT = A.alloc("xoT", [2, 128], BF16)
        x2t = A.alloc("x2t", [D], F32)
        x2t.accum = True
        for ti in range(Sq // 128):
            t0 = ti * 128
            for g, src in enumerate((self.mixA, self.mixB, self.mixC, self.mixD)):
                S.dma("sp", mx.t[:, g, :], src.t[t0:t0 + 128, :], reads=[src], writes=[mx])
            S.dma("sp", gdt.t, self.gd.t[t0:t0 + 128, :], reads=[self.gd], writes=[gdt])
            S.dma("sp", xt.t[:, 0, :], xsrc[t0:t0 + 128, :], reads=[xb], writes=[xt])
            y = mx.t[:, 1, :]
            S.tt("dve", u.t, y, y, ALU.mult, [mx], [u])
            S.ts("dve", u.t, u.t, 0.044715, 1.0, ALU.mult, ALU.add, [u], [u])
            S.tt("dve", u.t, u.t, y, ALU.mult, [u, mx], [u])
            S.act(sg.t, u.t, AF.Sigmoid, [u], [sg], scale=1.5957691216057308)
            S.tt("dve", hb.t, y, sg.t, ALU.mult, [mx, sg], [hb])
            S.copy("dve", hb16.t[:, 0, :], hb.t, [hb], [hb16])
            b = self.bank()
            for k in range(2):
                S.tr(b.bf[:, k * 128:(k + 1) * 128], hb16.t[:, 0, k * 128:(k + 1) * 128], self.ident.t[:, :], [hb16, self.ident], [b])
            S.copy("dve", hbT.t, b.bf[:, 0:256].rearrange("p (k t) -> p k t", k=2), [b], [hbT])
            b = self.bank()
            for k in range(2):
                S.mm(b.f[:, 0:256], hbT.t[:, k, :], glu.t[:, k, :], k == 0, k == 1, [hbT, glu, b], [b])
            S.tt("dve", u.t, b.f[:, 0:256], glub.t, ALU.add, [b, glub], [u])
            S.act(sg.t, u.t, AF.Sigmoid, [u], [sg])
            S.tt("dve", mx.t[:, 1, :], hb.t, sg.t, ALU.mult, [hb, sg], [mx])
            S.memset("dve", ssg.t, 0.0, [ssg])
            for g in range(4):
                S.act(junk.t[:, 0:256], mx.t[:, g, :], AF.Square, [mx, ssg], [junk, ssg], scale=1.0 / 16.0,
                      accum_out=ssg.t[:, g:g + 1])
            self.rsqrt(ssg, ssg.t, ssg.t, [ssg])
            for g in range(3):
                S.act(mg.t[:, 0, g * 256:(g + 1) * 256], mx.t[:, g, :], AF.Copy, [mx, ssg], [mg], scale=ssg.t[:, g:g + 1])
            S.act(sil.t, gdt.t, AF.Silu, [gdt], [sil])
            S.stt("dve", mg.t[:, 0, 768:1024], mx.t[:, 3, :], ssg.t[:, 3:4], sil.t, ALU.mult, ALU.mult, [mx, ssg, sil], [mg])
            self.transpose_to(mT, mg, 1)
            for c in range(2):
                b = self.bank()
                for k in range(8):
                    S.mm(b.f[:, :], mT.t[:, k, :], wout.t[:, k, c * 512:(c + 1) * 512], k == 0, k == 7, [mT, wout, b], [b])
                S.tt("dve", x1.t[:, 0, c * 512:(c + 1) * 512], b.f[:, :], xt.t[:, 0, c * 512:(c + 1) * 512], ALU.add, [b, xt], [x1])
            self.norm_tile(x1, 1, xn1, ss1, junk)
            self.transpose_to(h1T, xn1, 1)
            for h in range(4):
                b = self.bank()
                for k in range(8):
                    S.mm(b.f[0:64, 0:128], wq.t[:, k, h * 64:(h + 1) * 64], h1T.t[:, k, :], k == 0, k == 7, [wq, h1T, b], [b])
                S.copy("dve", qT.t[:, h, :], b.f[0:64, 0:128], [b], [qT])
            for h in range(4):
                b = self.bank()
                for c in range(2):
                    S.mm(b.f[:, c * 128:(c + 1) * 128], kmT.t[:, h, c * 128:(c + 1) * 128], qT.t[:, h, :], True, True, [kmT, qT], [b])
                S.act(pT.t, b.f[:, 0:256].rearrange("p (c t) -> p c t", c=2), AF.Exp, [b], [pT], scale=0.125)
                b2 = self.bank()
                for c in range(2):
                    S.mm(b2.f[:, 0:65], pT.t[:, c, :], vm.t[:, c, h, :], c == 0, c == 1, [pT, vm, b2], [b2])
                S.recip(rec.t, b2.f[:, 64:65], [b2], [rec])
                S.ts("dve", xo.t[:, 0, h * 64:(h + 1) * 64], b2.f[:, 0:64], rec.t[:, 0:1], None, ALU.mult, None, [b2, rec], [xo])
            b = self.bank()
            for k in range(2):
                S.tr(b.bf[:, k * 128:(k + 1) * 128], xo.t[:, 0, k * 128:(k + 1) * 128], self.ident.t[:, :], [xo, self.ident], [b])
            S.copy("dve", xoT.t, b.bf[:, 0:256].rearrange("p (k t) -> p k t", k=2), [b], [xoT])
            for c in range(2):
                b = self.bank()
                for k in range(2):
                    S.mm(b.f[:, :], xoT.t[:, k, :], wo2.t[:, k, c * 512:(c + 1) * 512], k == 0, k == 1, [xoT, wo2, b], [b])
                S.tt("dve", x2t.t[:, c * 512:(c + 1) * 512], b.f[:, :], x1.t[:, 0, c * 512:(c + 1) * 512], ALU.add, [b, x1], [x2t])
            S.dma("pool", self.x2.t[t0:t0 + 128, :], x2t.t, reads=[x2t], writes=[self.x2])

    def phase_FFN(self, l, si, hf):
        S, A, W = self.S, self.A, self.W
        Sq = self.seqs[si]
        last = (l == self.layers[-1])
        final = (l == self.L - 1)
        S.barrier()
        A.reset()
        H = 1408
        wu = A.alloc("wu", [8, 2 * H], BF16)
        self.load_w(wu, W["ffn_w_up"][l][:, hf * H:(hf + 1) * H], W["norm_ffn_w"][l], 8, H, "wua", dap=wu.t[:, :, 0:H])
        self.load_w(wu, W["ffn_w_up"][l][:, DFF + hf * H:DFF + (hf + 1) * H], W["norm_ffn_w"][l], 8, H, "wug", dap=wu.t[:, :, H:2 * H])
        wd = A.alloc("wd", [11, D], BF16)
        self.load_w(wd, W["ffn_w_down"][l][hf * H:(hf + 1) * H, :], None, 11, D, "wd")
        cw = A.alloc("cw", [3, 11], F32)
        for j in range(3):
            S.dma("sp", cw.t[:, j, :], W["ffn_conv_w"][l][j, hf * H:(hf + 1) * H].rearrange("(c p) -> p c", p=128),
                  reads=[self.ext], writes=[cw], allow_slow_non_contiguous=True)
        cb = A.alloc("cb", [11], F32)
        S.dma("sp", cb.t, W["ffn_conv_b"][l][hf * H:(hf + 1) * H].rearrange("(c p) -> p c", p=128),
              reads=[self.ext], writes=[cb], allow_slow_non_contiguous=True)
        cw.accum = True
        fnw = None
        if final and hf == 1:
            fnw = A.alloc("fnw", [D], F32)
            S.dma("sp", fnw.t, self.fnw.partition_broadcast(128), reads=[self.ext], writes=[fnw])
        xt = A.alloc("xt", [4, D], F32)
        rin = xt if hf == 0 else A.alloc("rin", [4, D], F32)
        xn = A.alloc("xn", [4, D], BF16)
        junk = A.alloc("junk", [D], BF16)
        ss = A.alloc("ss", [4], F32)
        hT = A.alloc("hT", [8, 514], BF16)
        hT.accum = True
        xh = [A.alloc(f"xh{i}", [1, D], F32, parts=1) for i in range(2)]
        xhn = [A.alloc(f"xhn{i}", [1, D], BF16, parts=1) for i in range(2)]
        ssh = [A.alloc(f"ssh{i}", [1], F32, parts=1) for i in range(2)]
        jh = A.alloc("jh", [D], BF16, parts=1)
        mT = A.alloc("mT", [11, 512], BF16)
        mT.accum = True
        gs = [A.alloc(f"gs{i}", [514], F32) for i in range(2)]
        for g_ in gs:
            g_.accum = True
        tb = [A.alloc(f"tb{i}", [512], F32) for i in range(2)]
        sl = A.alloc("sl", [512], F32)
        ro = rin
        so = A.alloc("so", [4], F32)
        dst = self.x2b if hf == 0 else (self.yb[si] if last else self.xA)
        dst_ap = self.x2b.t if hf == 0 else (self.y_out[si] if last else self.xA.t)
        nt = Sq // 512
        for ti in range(nt):
            t0 = ti * 512
            S.dma("sp", xt.t, self.x2.t[t0:t0 + 512, :].rearrange("(j p) d -> p j d", p=128), reads=[self.x2], writes=[xt])
            if hf == 1:
                S.dma("sp", rin.t, self.x2b.t[t0:t0 + 512, :].rearrange("(j p) d -> p j d", p=128), reads=[self.x2b], writes=[rin])
            self.norm_tile(xt, 4, xn, ss, junk)
            for k in range(8):
                b = self.bank()
                for j in range(4):
                    S.tr(b.bf[:, j * 128:(j + 1) * 128], xn.t[:, j, k * 128:(k + 1) * 128], self.ident.t[:, :], [xn, self.ident], [b])
                S.copy("dve" if k % 2 == 0 else "act", hT.t[:, k, 1:513], b.bf[:, 0:512], [b], [hT])
            for i, tok in enumerate((t0 - 1, t0 + 512)):
                col = 0 if i == 0 else 513
                if tok < 0 or tok >= Sq:
                    S.memset("dve", hT.t[:, :, col:col + 1], 0.0, [hT])
                    continue
                S.dma("sp", xh[i].t[:, 0, :], self.x2.t[tok:tok + 1, :], reads=[self.x2], writes=[xh[i]])
                self.norm_tile(xh[i], 1, xhn[i], ssh[i], jh)
                b = self.bank()
                for k in range(8):
                    S.tr(b.bf[:, 2 * k:2 * k + 1], xhn[i].t[:, 0, k * 128:(k + 1) * 128], self.ident.t[0:1, 0:1], [xhn[i], self.ident], [b])
                S.copy("dve", hT.t[:, :, col:col + 1], b.bf[:, 0:16:2].unsqueeze(2), [b], [hT])
            for c in range(11):
                ba, bg, bh = self.bank(), self.bank(), self.bank()
                for k in range(8):
                    S.mm(ba.f[:, :], wu.t[:, k, c * 128:(c + 1) * 128], hT.t[:, k, 1:513], k == 0, k == 7, [wu, hT, ba], [ba])
                for k in range(8):
                    S.mm(bg.f[:, :], wu.t[:, k, H + c * 128:H + (c + 1) * 128], hT.t[:, k, 1:513], k == 0, k == 7, [wu, hT, bg], [bg])
                for k in range(8):
                    S.mm(bh.f[:, 0:2], wu.t[:, k, H + c * 128:H + (c + 1) * 128], hT.t[:, k, 0:514:513], k == 0, k == 7, [wu, hT, bh], [bh])
                g = gs[c % 2]
                t = tb[c % 2]
                S.copy("act", g.t[:, 1:513], bg.f[:, :], [bg], [g])
                S.copy("dve", g.t[:, 0:514:513], bh.f[:, 0:2], [bh], [g])
                S.ts("dve", t.t, g.t[:, 0:512], cw.t[:, 0, c:c + 1], cb.t[:, c:c + 1], ALU.mult, ALU.add, [g, cw, cb], [t])
                S.stt("dve", t.t, g.t[:, 1:513], cw.t[:, 1, c:c + 1], t.t, ALU.mult, ALU.add, [g, cw, t], [t])
                S.stt("dve", t.t, g.t[:, 2:514], cw.t[:, 2, c:c + 1], t.t, ALU.mult, ALU.add, [g, cw, t], [t])
                S.act(sl.t, t.t, AF.Silu, [t], [sl])
                S.tt("dve", mT.t[:, c, :], sl.t, ba.f[:, :], ALU.mult, [sl, ba], [mT])
            for j in range(4):
                for cc in range(2):
                    b = self.bank()
                    for c in range(11):
                        S.mm(b.f[:, :], mT.t[:, c, j * 128:(j + 1) * 128], wd.t[:, c, cc * 512:(cc + 1) * 512], c == 0, c == 10, [mT, wd, b], [b])
                    S.tt("dve", ro.t[:, j, cc * 512:(cc + 1) * 512], b.f[:, :], rin.t[:, j, cc * 512:(cc + 1) * 512], ALU.add, [b, rin], [ro])
            if fnw is not None:
                S.memset("dve", so.t, 0.0, [so])
                for j in range(4):
                    S.act(junk.t, ro.t[:, j, :], AF.Square, [ro, so], [junk, so], scale=1.0 / 32.0, accum_out=so.t[:, j:j + 1])
                self.rsqrt(so, so.t, so.t, [so])
                for j in range(4):
                    S.stt("dve", ro.t[:, j, :], ro.t[:, j, :], so.t[:, j:j + 1], fnw.t, ALU.mult, ALU.mult, [ro, so, fnw], [ro])
            S.dma("pool", dst_ap[t0:t0 + 512, :].rearrange("(j p) d -> p j d", p=128), ro.t, reads=[ro], writes=[dst])

    def build_all(self):
        for si in range(len(self.seqs)):
            for l in self.layers:
                self.phase_A(l, si)
                self.phase_NA(l, si)
                self.phase_GQA(l, si)
                self.phase_S5(l, si)
                self.phase_HGRN(l, si)
                self.phase_C1(l, si)
                self.phase_FFN(l, si, 0)
                self.phase_FFN(l, si, 1)
        self.S.finish()

    def in_map(self, inputs, xs, ms):
        c = host_consts(self.maxS)
        s_ = np.arange(128)
        c["tri_gt"] = (s_[:, None] > s_[None, :]).astype(np.float32)
        c["tri_lt"] = (s_[:, None] < s_[None, :]).astype(np.float32)
        c["kpow"] = _kpow_table()
        blk = s_ // 16
        c["bmask"] = np.stack([(blk[:, None] <= blk[None, :]), (blk[:, None] >= blk[None, :])]).astype(np.float32)
        m = {}
        for i, (x, mm_) in enumerate(zip(xs, ms)):
            m[f"x{i}"] = np.ascontiguousarray(x, dtype=np.float32)
            m[f"m{i}"] = np.ascontiguousarray(mm_, dtype=np.float32)
        for n, sh in WEIGHT_SPECS:
            m[n] = np.ascontiguousarray(inputs[n], dtype=np.float32)
        m["final_norm_w"] = np.ascontiguousarray(inputs["final_norm_w"], dtype=np.float32)
        for n, v in c.items():
            m["c_" + n] = np.ascontiguousarray(v)
        return m


N_CORES = 8
SEQS = [2048, 2048, 2048, 2048, 16384]
FUSED = os.environ.get("MK_FUSED", "0") == "1"


def _kernel_fused(inputs, xp, xs, mp, ms):
    P = Prog(SEQS, L=2)
    P.build_all()
    in_maps = []
    for c in range(N_CORES):
        sb = c // 4
        xs_c = [xp[4 * c + i] for i in range(4)] + [xs[sb]]
        ms_c = [mp[4 * c + i] for i in range(4)] + [ms[sb]]
        in_maps.append(P.in_map(inputs, xs_c, ms_c))
    res = run_bass_kernel_spmd(P.nc, in_maps, core_ids=list(range(N_CORES)))
    yp = np.empty_like(xp)
    ys = np.empty_like(xs)
    for c in range(N_CORES):
        for i in range(4):
            yp[4 * c + i] = res.results[c][f"y{i}"]
    ys[0] = res.results[0]["y4"]
    ys[1] = res.results[4]["y4"]
    return yp, ys


def _kernel_two_launch(inputs, xp, xs, mp, ms):
    P = Prog([2048] * 4, L=2)
    P.build_all()
    in_maps = [P.in_map(inputs, [xp[4 * c + i] for i in range(4)], [mp[4 * c + i] for i in range(4)])
               for c in range(N_CORES)]
    res = run_bass_kernel_spmd(P.nc, in_maps, core_ids=list(range(N_CORES)))
    yp = np.empty_like(xp)
    for c in range(N_CORES):
        for i in range(4):
            yp[4 * c + i] = res.results[c][f"y{i}"]
    cur = [xs[c] for c in range(2)]
    for l in range(2):
        P2 = Prog([16384], L=2, layers=[l])
        P2.build_all()
        in_maps2 = [P2.in_map(inputs, [cur[c]], [ms[c]]) for c in range(2)]
        res2 = run_bass_kernel_spmd(P2.nc, in_maps2, core_ids=[0, 1])
        cur = [res2.results[c]["y0"] for c in range(2)]
    ys = np.empty_like(xs)
    for c in range(2):
        ys[c] = cur[c]
    return yp, ys


def kernel(**inputs):
    xp = np.asarray(inputs["x_prompt"], dtype=np.float32)
    xs = np.asarray(inputs["x_sample"], dtype=np.float32)
    mp = np.asarray(inputs["mem_prompt"], dtype=np.float32)
    ms = np.asarray(inputs["mem_sample"], dtype=np.float32)
    if FUSED:
        return _kernel_fused(inputs, xp, xs, mp, ms)
    return _kernel_two_launch(inputs, xp, xs, mp, ms)
```

```python
import os
from contextlib import ExitStack

import numpy as np
import concourse.bass as bass
import concourse.mybir as mybir
from concourse.bass_utils import run_bass_kernel_spmd

F32 = mybir.dt.float32
BF16 = mybir.dt.bfloat16
AF = mybir.ActivationFunctionType
ALU = mybir.AluOpType
AX = mybir.AxisListType

SEM_LIMIT = 30000
SAME_ENGINE_WAITS = os.environ.get("SAMEW", "1") == "1"


class Buf:
    def __init__(self, name, t=None):
        self.name = name
        self.t = t
        self.writers = []
        self.readers = []
        self.war = []
        self.accum = False
        self.base = None
        self.scope = None
        self.ctr = None


class Counter:
    def __init__(self, sched, name, step):
        self.s = sched
        self.name = name
        self.step = step
        self.sem = None
        self.val = 0
        self.k = 0

    def bump(self):
        if self.sem is None or self.val + self.step > SEM_LIMIT:
            self.sem = self.s.new_sem(f"{self.name}_{self.k}")
            self.k += 1
            self.val = 0
        self.val += self.step
        return (self.sem, self.val)


class Op:
    __slots__ = ("eng", "fn", "reads", "writes", "is_dma", "deps", "sig", "tok", "strong")

    def __init__(self, eng, fn, reads, writes, is_dma):
        self.eng = eng
        self.fn = fn
        self.reads = reads
        self.writes = writes
        self.is_dma = is_dma
        self.deps = []
        self.sig = False
        self.tok = None
        self.strong = False


class Sched:
    ENGS = ("pe", "act", "dve", "pool", "sp")

    def __init__(self, nc):
        self.nc = nc
        self.ops = []
        self.stack = ExitStack()
        self.nsem = 0
        self.sems = []
        self.epoch = 0
        self.trace_names = None
        self.block_every = int(os.environ.get("MK_BLOCK_EVERY", "0"))

    def new_sem(self, name):
        self.nsem += 1
        s = self.stack.enter_context(self.nc.semaphore(f"s{self.nsem}_{name}"))
        self.sems.append(s)
        return s

    def sbuf(self, name, shape, dtype):
        t = self.stack.enter_context(self.nc.sbuf_tensor(name, list(shape), dtype))
        return Buf(name, t)

    def psum(self, name, shape, dtype):
        t = self.stack.enter_context(self.nc.psum_tensor(name, list(shape), dtype))
        return Buf(name, t)

    def dram(self, name, shape, dtype):
        t = self.nc.dram_tensor(name, list(shape), dtype, kind="Internal")
        return Buf(name, t)

    def dram_buf(self, name):
        return Buf(name)

    def view(self, name, t=None):
        return Buf(name, t)

    def op(self, eng, fn, reads=(), writes=()):
        if eng == "pool":
            eng = "dve"
        self.ops.append(Op(eng, fn, list(reads), list(writes), False))

    def barrier(self):
        self.epoch += 1
        for e in self.ENGS:
            o = Op(e, None, [], [], False)
            self.ops.append(o)
            o.sig = None
            o.tok = self.epoch

    def dma(self, eng, out, in_, reads=(), writes=(), **kw):
        if eng == "pool":
            eng = "act"
        def fn(e, out=out, in_=in_, kw=kw):
            return e.dma_start(out=out, in_=in_, **kw)

        self.ops.append(Op(eng, fn, list(reads), list(writes), True))


    def mm(self, out, lhsT, rhs, start, stop, reads, writes):
        self.op("pe", lambda e: e.matmul(out, lhsT, rhs, start=start, stop=stop), reads, writes)

    def tr(self, out, in_, ident, reads, writes):
        self.op("pe", lambda e: e.transpose(out, in_, ident), reads, writes)

    def act(self, out, in_, func, reads, writes, bias=None, scale=None, accum_out=None, eng="act"):
        kw = {}
        if bias is not None:
            kw["bias"] = bias
        if scale is not None:
            kw["scale"] = scale
        if accum_out is not None:
            kw["accum_out"] = accum_out
        self.op(eng, lambda e: e.activation(out=out, in_=in_, func=func, **kw), reads, writes)

    def tt(self, eng, out, in0, in1, op, reads, writes):
        self.op(eng, lambda e: e.tensor_tensor(out=out, in0=in0, in1=in1, op=op), reads, writes)

    def ts(self, eng, out, in0, s1, s2, op0, op1, reads, writes, accum_out=None):
        kw = {}
        if op1 is not None:
            kw["op1"] = op1
        if accum_out is not None:
            kw["accum_out"] = accum_out
        self.op(eng, lambda e: e.tensor_scalar(out=out, in0=in0, scalar1=s1, scalar2=s2, op0=op0, **kw), reads, writes)

    def stt(self, eng, out, in0, scalar, in1, op0, op1, reads, writes):
        self.op(eng, lambda e: e.scalar_tensor_tensor(out=out, in0=in0, scalar=scalar, in1=in1, op0=op0, op1=op1), reads, writes)

    def copy(self, eng, out, in_, reads, writes):
        if eng == "act":
            self.op(eng, lambda e: e.activation(out=out, in_=in_, func=AF.Copy), reads, writes)
        else:
            self.op(eng, lambda e: e.tensor_copy(out=out, in_=in_), reads, writes)

    def memset(self, eng, ap, val, writes):
        self.op(eng, lambda e: e.memset(ap, val), (), writes)
        self.ops[-1].strong = True

    def recip(self, out, in_, reads, writes):
        self.op("dve", lambda e: e.reciprocal(out=out, in_=in_), reads, writes)

    @staticmethod
    def _cbuf(o):
        w = o.writes[0]
        if getattr(w, "is_dram", False):
            for r in o.reads:
                if not getattr(r, "is_dram", False):
                    return r
        return w

    def finish(self):
        ops = self.ops
        last_eng = {}
        last_dma = {}
        for i, o in enumerate(ops):
            if o.fn is None:
                o.sig = False
                dl = [j for e2, j in last_eng.items() if e2 != o.eng] + list(last_dma.values())
                for j in dl:
                    ops[j].sig = True
                o.deps = sorted(set(dl))
                continue
            if o.is_dma:
                last_dma[id(self._cbuf(o))] = i
            else:
                last_eng[o.eng] = i
            deps = set()
            for b in o.reads:
                deps.update(b.writers)
            for b in o.writes:
                if b.readers:
                    b.war = b.readers
                    b.readers = []
                    b.writers = []
                deps.update(b.war)
                if o.strong or not b.accum:
                    deps.update(b.writers)
                    b.writers = [i]
                    b.base = i if o.strong else None
                else:
                    if b.base is not None:
                        deps.add(b.base)
                    b.writers.append(i)
            for b in o.reads:
                if b not in o.writes:
                    b.readers.append(i)
            deps.discard(i)
            dl = []
            for j in sorted(deps):
                pj = ops[j]
                if (not pj.is_dma) and (not o.is_dma) and pj.eng == o.eng and (o.eng == "pe" or not SAME_ENGINE_WAITS):
                    continue
                dl.append(j)
                pj.sig = True
            o.deps = dl
        ectr = {e: Counter(self, e, 1) for e in self.ENGS}
        free_ctrs = []
        active = []
        for o in ops:
            if o.fn is None:
                ep = o.tok
                o.tok = None
                keep = []
                for b in active:
                    if b.scope is not None and b.scope < ep:
                        free_ctrs.append(b.ctr)
                    else:
                        keep.append(b)
                active = keep
                continue
            if o.is_dma:
                b = self._cbuf(o)
                if b.ctr is None:
                    if b.scope is not None and free_ctrs:
                        b.ctr = free_ctrs.pop()
                    else:
                        b.ctr = Counter(self, "d_" + b.name, 16)
                    if b.scope is not None:
                        active.append(b)
                o.tok = b.ctr.bump()
            elif o.sig:
                o.tok = ectr[o.eng].bump()
        finals = {}
        for o in ops:
            if o.is_dma:
                finals[id(o.tok[0])] = o.tok
        per = {e: [] for e in self.ENGS}
        for i, o in enumerate(ops):
            per[o.eng].append(i)
        sidx = {}
        for o in ops:
            if o.tok is not None and id(o.tok[0]) not in sidx:
                sidx[id(o.tok[0])] = len(sidx)
        NS = len(sidx)
        seenv = {e: [0] * NS for e in self.ENGS}
        know = {}
        waits_of = {}
        for i, o in enumerate(ops):
            se = seenv[o.eng]
            wl = []
            for j in sorted(o.deps, reverse=True):
                sem, val = ops[j].tok
                k = sidx[id(sem)]
                if se[k] >= val:
                    continue
                wl.append((sem, val))
                kj = know[j]
                for q in range(NS):
                    if kj[q] > se[q]:
                        se[q] = kj[q]
            if wl:
                waits_of[i] = wl
            if o.tok is not None:
                kk = list(se)
                k = sidx[id(o.tok[0])]
                if o.tok[1] > kk[k]:
                    kk[k] = o.tok[1]
                know[i] = kk
        nc = self.nc
        self.n_waits = 0
        sched = self

        def emit(e, name, do_final):
            seen = {}
            for i in per[name]:
                o = ops[i]
                for sem, val in waits_of.get(i, ()):
                    seen[id(sem)] = max(seen.get(id(sem), 0), val)
                    e.wait_ge(sem, val)
                    sched.n_waits += 1
                if o.fn is None:
                    continue
                ins = o.fn(e)
                if sched.trace_names is not None:
                    try:
                        sched.trace_names[ins.ins.name] = (name, i, getattr(o, "desc", None))
                    except Exception:
                        pass
                if o.tok is not None:
                    ins.then_inc(o.tok[0], 16 if o.is_dma else 1)
            if do_final:
                for sem, val in finals.values():
                    if seen.get(id(sem), 0) < val:
                        e.wait_ge(sem, val)

        bounds = [0]
        if os.environ.get("MK_ONEBLOCK", "0") != "1":
            for i, o in enumerate(ops):
                if o.fn is None and o.eng == self.ENGS[0] and i > 0:
                    bounds.append(i)
        if self.block_every:
            bounds = sorted(set(bounds) | set(range(0, len(ops), self.block_every)))
        bounds.append(len(ops))
        seen_all = {e_: {} for e_ in self.ENGS}
        for bi in range(len(bounds) - 1):
            lo, hi = bounds[bi], bounds[bi + 1]
            lastblk = bi == len(bounds) - 2

            def emit_rng(e, name, lo=lo, hi=hi, lastblk=lastblk):
                seen = seen_all[name]
                for i in per[name]:
                    if i < lo or i >= hi:
                        continue
                    o = ops[i]
                    for sem, val in waits_of.get(i, ()):
                        seen[id(sem)] = max(seen.get(id(sem), 0), val)
                        e.wait_ge(sem, val)
                        sched.n_waits += 1
                    if o.fn is None:
                        continue
                    ins = o.fn(e)
                    if o.tok is not None:
                        ins.then_inc(o.tok[0], 16 if o.is_dma else 1)
                if lastblk and name == "sp":
                    for sem, val in finals.values():
                        if seen.get(id(sem), 0) < val:
                            e.wait_ge(sem, val)

            with nc.Block() as block:
                block.tensor(lambda e: emit_rng(e, "pe"))
                block.scalar(lambda e: emit_rng(e, "act"))
                block.vector(lambda e: emit_rng(e, "dve"))
                block.gpsimd(lambda e: emit_rng(e, "pool"))
                block.sync(lambda e: emit_rng(e, "sp"))
        self.stack.close()


D = 1024
INW = 2816
DFF = 2816
MEMT = 256
EPS = 1e-6
NEG = -30000.0
U8 = mybir.dt.uint8

O_QA, O_KA, O_VA, O_UB, O_QC, O_KC, O_VC, O_QD, O_ZF, O_ZB, O_VD, O_GD = (
    0, 256, 512, 768, 1024, 1280, 1408, 1536, 1792, 2048, 2304, 2560)

WEIGHT_SPECS = [
    ("norm_mix_w", (D,)), ("w_in", (D, INW)), ("na_rpb", (4, 15, 31)),
    ("s5_lambda_re", (2, 16, 64)), ("s5_lambda_im", (2, 16, 64)), ("s5_log_dt", (2, 16)),
    ("s5_b_re", (2, 16, 64, 16)), ("s5_b_im", (2, 16, 64, 16)),
    ("s5_c_re", (2, 16, 16, 64)), ("s5_c_im", (2, 16, 16, 64)),
    ("s5_d", (256,)), ("s5_glu_w", (256, 256)), ("s5_glu_b", (256,)),
    ("gqa_q_norm_w", (64,)), ("gqa_k_norm_w", (64,)), ("hgrn_lower_bound", (256,)),
    ("mix_out_norm_w", (D,)), ("w_out", (D, D)), ("norm_xattn_w", (D,)), ("norm_mem_w", (D,)),
    ("xattn_w_q", (D, 256)), ("xattn_w_kv", (D, 512)), ("xattn_w_o", (256, D)),
    ("norm_ffn_w", (D,)), ("ffn_w_up", (D, 2 * DFF)), ("ffn_conv_w", (3, DFF)),
    ("ffn_conv_b", (DFF,)), ("ffn_w_down", (DFF, D)),
]


class Arena:
    def __init__(self, sched, t, nbytes):
        self.sched = sched
        self.t = t
        self.n = nbytes
        self.off = 0

    def reset(self):
        self.off = 0

    def alloc(self, name, free_shape, dtype, parts=128):
        esz = 4 if dtype == F32 else 2
        n = int(np.prod(free_shape))
        nb = (n * esz + 31) // 32 * 32
        assert self.off + nb <= self.n, f"arena overflow at {name}: {self.off}+{nb}>{self.n}"
        v = self.t[0:parts, self.off // 4:(self.off + nb) // 4]
        scope = self.sched.epoch
        self.off += nb
        if dtype != F32:
            v = v.bitcast(dtype)
        v = v[:, 0:n]
        if len(free_shape) == 2:
            v = v.rearrange("p (a b) -> p a b", a=free_shape[0])
        elif len(free_shape) == 3:
            v = v.rearrange("p (a b c) -> p a b c", a=free_shape[0], b=free_shape[1])
        b = Buf(name, v)
        b.scope = scope
        return b


def host_consts(maxS):
    import ml_dtypes
    c = {}
    c["ident"] = np.eye(128, dtype=np.float32).astype(ml_dtypes.bfloat16)
    c["identf"] = np.eye(128, dtype=np.float32)
    t = np.arange(maxS)
    inv = 1.0 / (10000.0 ** (np.arange(0, 32, 2, dtype=np.float32) / 32.0))
    ang = np.concatenate([(t // 64).astype(np.float32)[:, None] * inv,
                          (t % 64).astype(np.float32)[:, None] * inv], axis=-1)
    cos = np.cos(ang).T.astype(np.float32)
    sin = np.sin(ang).T.astype(np.float32)
    c["ropec"] = np.concatenate([cos, cos], 0)
    c["ropes"] = np.concatenate([-sin, sin], 0)
    P = np.zeros((64, 64), np.float32)
    for i in range(32):
        P[32 + i, i] = 1.0
        P[i, 32 + i] = 1.0
    c["rotp"] = P.astype(ml_dtypes.bfloat16)
    c["ones64"] = np.full((64, 64), 1.0 / 64.0, np.float32).astype(ml_dtypes.bfloat16)
    cm = np.zeros((32, 64, 64), np.float32)
    for qc in range(64):
        c0 = int(np.clip(qc - 8, 0, 48))
        for kc in range(64):
            if c0 <= kc < c0 + 16:
                j = int(np.clip(kc - qc, -15, 15)) + 15
                cm[j, kc, qc] = 1.0
            else:
                cm[31, kc, qc] = NEG
    c["na_cm"] = cm.reshape(32, 4096)
    s_ = np.arange(128)
    c["tri_le"] = (s_[:, None] <= s_[None, :]).astype(np.float32)
    c["tri_ge"] = (s_[:, None] >= s_[None, :]).astype(np.float32)
    return c


CONST_SPECS = [("ident", (128, 128), BF16), ("identf", (128, 128), F32), ("ropec", None, F32), ("ropes", None, F32),
               ("rotp", (64, 64), BF16), ("ones64", (64, 64), BF16), ("na_cm", (32, 4096), F32),
               ("tri_le", (128, 128), F32), ("tri_ge", (128, 128), F32)]


def _kpow_table():
    ks = [8.0 * (2 ** m) for m in range(12)]
    f = [7 - s for s in range(8)] + [t - 7 for t in range(8)] + [t + 1 for t in range(8)] + ks
    b = [s for s in range(8)] + [-t for t in range(8)] + [8 - t for t in range(8)] + ks
    f = f + [1.0]
    b = b + [1.0]
    return np.array([f, b], np.float32)


KP_P, KP_Q, KP_C, KP_KS, KP_ONE, NKP = 0, 8, 16, 24, 36, 37
MAGIC = 12582912.0
TWO_PI = 6.283185307179586


class Prog:
    def __init__(self, seqs, L=2, dbg=(), layers=None):
        self.seqs = list(seqs)
        self.L = L
        self.layers = list(range(L)) if layers is None else list(layers)
        self.dbg = dbg
        self.maxS = maxS = max(seqs)
        nc = self.nc = bass.Bass("TRN2", target_bir_lowering=False)
        S = self.S = Sched(nc)
        self.ext = Buf("ext")
        self.ext.is_dram = True
        self.x_in = [nc.dram_tensor(f"x{i}", [s, D], F32, kind="ExternalInput").ap() for i, s in enumerate(seqs)]
        self.m_in = [nc.dram_tensor(f"m{i}", [MEMT, D], F32, kind="ExternalInput").ap() for i, s in enumerate(seqs)]
        self.y_out = [nc.dram_tensor(f"y{i}", [s, D], F32, kind="ExternalOutput").ap() for i, s in enumerate(seqs)]
        self.yb = [Buf(f"y{i}") for i in range(len(seqs))]
        for b in self.yb:
            b.accum = True
            b.is_dram = True
        self.W = {n: nc.dram_tensor(n, [L] + list(sh), F32, kind="ExternalInput").ap() for n, sh in WEIGHT_SPECS}
        self.fnw = nc.dram_tensor("final_norm_w", [D], F32, kind="ExternalInput").ap()
        self.C = {}
        for n, sh, dt in CONST_SPECS:
            if sh is None:
                sh = (64, maxS)
            self.C[n] = nc.dram_tensor("c_" + n, list(sh), dt, kind="ExternalInput").ap()
        self.C["tri_gt"] = nc.dram_tensor("c_tri_gt", [128, 128], F32, kind="ExternalInput").ap()
        self.C["tri_lt"] = nc.dram_tensor("c_tri_lt", [128, 128], F32, kind="ExternalInput").ap()
        self.C["kpow"] = nc.dram_tensor("c_kpow", [2, NKP], F32, kind="ExternalInput").ap()
        self.C["bmask"] = nc.dram_tensor("c_bmask", [2, 128, 128], F32, kind="ExternalInput").ap()

        def scr(name, shape, dt):
            kind = "ExternalOutput" if name in dbg else "Internal"
            t = nc.dram_tensor("s_" + name, list(shape), dt, kind=kind).ap()
            b = Buf(name, t)
            b.accum = True
            b.is_dram = True
            return b
        self.xA = scr("xA", [maxS, D], F32)
        self.x2 = scr("x2", [maxS, D], F32)
        self.x2b = scr("x2b", [maxS, D], F32)
        self.qaT = scr("qaT", [4, 64, maxS], BF16)
        self.kaT = scr("kaT", [4, 64, maxS], BF16)
        self.qcT = scr("qcT", [4, 64, maxS], BF16)
        self.kcT = scr("kcT", [2, 64, maxS], BF16)
        self.qdT = scr("qdT", [4, 64, maxS], BF16)
        self.va = scr("va", [maxS, 256], BF16)
        self.ub = scr("ub", [maxS, 256], BF16)
        self.vc = scr("vc", [maxS, 128], BF16)
        self.zf = scr("zf", [maxS, 512], F32)
        self.vd = scr("vd", [maxS, 256], BF16)
        self.gd = scr("gd", [maxS, 256], F32)
        self.mixA = scr("mixA", [maxS, 256], F32)
        self.mixB = scr("mixB", [maxS, 256], F32)
        self.mixC = scr("mixC", [maxS, 256], F32)
        self.mixD = scr("mixD", [maxS, 256], F32)
        self.mixDf = scr("mixDf", [maxS, 256], F32)
        self.Gd = scr("Gd", [4, 15, 4096], F32)
        self.ident = S.sbuf("ident", [128, 128], BF16)
        self.identf = S.sbuf("identf", [128, 128], F32)
        S.dma("sp", self.ident.t[:, :], self.C["ident"], reads=[self.ext], writes=[self.ident])
        S.dma("sp", self.identf.t[:, :], self.C["identf"], reads=[self.ext], writes=[self.identf])
        ARENA_F32 = 45056
        at = S.stack.enter_context(nc.sbuf_tensor("arena", [128, ARENA_F32], F32))
        self.A = Arena(S, at, ARENA_F32 * 4)
        self.banks = []
        for i in range(8):
            b = S.psum(f"bank{i}", [128, 512], F32)
            b.f = b.t[:, :]
            b.bf = b.t[:, :].bitcast(BF16)
            self.banks.append(b)
        self.bi = 0

    def bank(self):
        b = self.banks[self.bi % 8]
        self.bi += 1
        return b

    def load_w(self, dst, src, gain, K, N, tag, dap=None):
        S, A = self.S, self.A
        nchunk = (N + 1407) // 1408
        cw = N // nchunk
        cache = getattr(self, "_stg", None)
        if cache is not None and cache[0] == S.epoch and cache[1] >= cw:
            stg = cache[2]
        else:
            stg = [A.alloc(f"{tag}_stg{i}", [max(cw, 1408)], F32) for i in range(2)]
            self._stg = (S.epoch, max(cw, 1408), stg)
        gt = None
        if gain is not None:
            gt = A.alloc(f"{tag}_g", [K], F32)
            S.dma("sp", gt.t, gain.rearrange("(k p) -> p k", p=128), reads=[self.ext], writes=[gt],
                  allow_slow_non_contiguous=True)
        n = 0
        for k in range(K):
            for c in range(nchunk):
                st = stg[n % 2]
                S.dma("sp" if n % 2 == 0 else "act", st.t[:, 0:cw], src[k * 128:(k + 1) * 128, c * cw:(c + 1) * cw],
                      reads=[self.ext], writes=[st])
                o = (dst.t if dap is None else dap)[:, k, c * cw:(c + 1) * cw]
                if gt is not None:
                    if n % 2 == 0:
                        S.act(o, st.t[:, 0:cw], AF.Copy, [st, gt], [dst], scale=gt.t[:, k:k + 1])
                    else:
                        S.ts("pool", o, st.t[:, 0:cw], gt.t[:, k:k + 1], None, ALU.mult, None, [st, gt], [dst])
                else:
                    S.copy("act" if n % 2 == 0 else "pool", o, st.t[:, 0:cw], [st], [dst])
                n += 1

    def rsqrt(self, ob, o, i, ibufs):
        S = self.S
        S.ts("dve", o, i, EPS, None, ALU.add, None, ibufs, [ob])
        S.recip(o, o, [ob], [ob])
        S.act(o, o, AF.Sqrt, [ob], [ob])

    def norm_tile(self, xt, nj, xn, ss, junk):
        S = self.S
        S.memset("dve", ss.t, 0.0, [ss])
        for j in range(nj):
            S.act(junk.t, xt.t[:, j, :], AF.Square, [xt, ss], [junk, ss], scale=1.0 / 32.0, accum_out=ss.t[:, j:j + 1])
        self.rsqrt(ss, ss.t, ss.t, [ss])
        for j in range(nj):
            S.act(xn.t[:, j, :], xt.t[:, j, :], AF.Copy, [xt, ss], [xn], scale=ss.t[:, j:j + 1])

    def transpose_to(self, hT, xn, nj, ncol0=0):
        S = self.S
        for k in range(8):
            b = self.bank()
            for j in range(nj):
                S.tr(b.bf[:, j * 128:(j + 1) * 128], xn.t[:, j, k * 128:(k + 1) * 128], self.ident.t[:, :],
                     [xn, self.ident], [b])
            S.copy("dve" if k % 2 == 0 else "act", hT.t[:, k, ncol0:ncol0 + nj * 128], b.bf[:, 0:nj * 128], [b], [hT])

    def phase_A(self, l, si):
        S, A, W = self.S, self.A, self.W
        Sq = self.seqs[si]
        S.barrier()
        A.reset()
        if l == self.layers[0]:
            xsrc, xb = self.x_in[si], self.ext
        else:
            xsrc, xb = self.xA.t, self.xA
        w = A.alloc("w_in", [8, INW], BF16)
        self.load_w(w, W["w_in"][l], W["norm_mix_w"][l], 8, INW, "win")
        import os
        STG = int(os.environ.get("STG", "9"))
        if STG <= 1:
            return
        qnw = A.alloc("qnw", [1], F32, parts=64)
        knw = A.alloc("knw", [1], F32, parts=64)
        S.dma("sp", qnw.t, W["gqa_q_norm_w"][l].rearrange("(p o) -> p o", o=1), reads=[self.ext], writes=[qnw])
        S.dma("sp", knw.t, W["gqa_k_norm_w"][l].rearrange("(p o) -> p o", o=1), reads=[self.ext], writes=[knw])
        rotp = A.alloc("rotp", [64], BF16, parts=64)
        ones64 = A.alloc("ones64", [64], BF16, parts=64)
        S.dma("sp", rotp.t, self.C["rotp"], reads=[self.ext], writes=[rotp])
        S.dma("sp", ones64.t, self.C["ones64"], reads=[self.ext], writes=[ones64])
        xts = [A.alloc(f"xt{i}", [4, D], F32) for i in range(2)]
        xn = A.alloc("xn", [4, D], BF16)
        junk = A.alloc("junk", [D], BF16)
        sss = [A.alloc(f"ss{i}", [4], F32) for i in range(2)]
        hTs = [A.alloc(f"hT{i}", [8, 512], BF16) for i in range(2)]
        rcs = [A.alloc(f"rc{i}", [512], F32, parts=64) for i in range(2)]
        rss = [A.alloc(f"rs{i}", [512], F32, parts=64) for i in range(2)]
        st_va = [A.alloc(f"st_va{i}", [4, 256], BF16) for i in range(2)]
        st_ub = [A.alloc(f"st_ub{i}", [4, 256], BF16) for i in range(2)]
        st_vc = [A.alloc(f"st_vc{i}", [4, 128], BF16) for i in range(2)]
        st_zf = [A.alloc(f"st_zf{i}", [4, 512], F32) for i in range(2)]
        st_vd = [A.alloc(f"st_vd{i}", [4, 256], BF16) for i in range(2)]
        st_gd = [A.alloc(f"st_gd{i}", [4, 256], F32) for i in range(2)]
        for lst in (st_va, st_ub, st_vc, st_zf, st_vd, st_gd):
            for b in lst:
                b.accum = True
        fst = [A.alloc(f"fst{i}", [512], BF16, parts=64) for i in range(4)]
        sq = A.alloc("sq", [512], BF16, parts=64)
        rstd = A.alloc("rstd", [512], F32, parts=64)
        qh = A.alloc("qh", [512], BF16, parts=64)
        t1 = A.alloc("t1", [512], F32, parts=64)
        t2 = A.alloc("t2", [512], F32, parts=64)
        nf = 0
        for ti in range(Sq // 512):
            t0 = ti * 512
            p = ti % 2
            xt, ss, hT = xts[p], sss[p], hTs[p]
            S.dma("sp", xt.t, xsrc[t0:t0 + 512, :].rearrange("(j p) d -> p j d", p=128), reads=[xb], writes=[xt])
            S.dma("sp", rcs[p].t, self.C["ropec"][:, t0:t0 + 512], reads=[self.ext], writes=[rcs[p]])
            S.dma("sp", rss[p].t, self.C["ropes"][:, t0:t0 + 512], reads=[self.ext], writes=[rss[p]])
            self.norm_tile(xt, 4, xn, ss, junk)
            if STG <= 2:
                return
            self.transpose_to(hT, xn, 4)
            if STG <= 3:
                return
            for j in range(4):
                for (c0, n, outs) in ((O_VA, 512, ((st_va[p], 0, 256), (st_ub[p], 256, 256))),
                                      (O_VC, 128, ((st_vc[p], 0, 128),)),
                                      (O_ZF, 512, ((st_zf[p], 0, 512),)),
                                      (O_VD, 512, ((st_vd[p], 0, 256), (st_gd[p], 256, 256)))):
                    b = self.bank()
                    for k in range(8):
                        S.mm(b.f[:, 0:n], hT.t[:, k, j * 128:(j + 1) * 128], w.t[:, k, c0:c0 + n], k == 0, k == 7,
                             [hT, w, b], [b])
                    if os.environ.get("NOEV"):
                        continue
                    for ii, (st, o0, nn) in enumerate(outs):
                        S.copy(os.environ.get("EVE", "dve"), st.t[:, j, :], b.f[:, o0:o0 + nn], [b], [st])
            for st, dst in ((st_va[p], self.va), (st_ub[p], self.ub), (st_vc[p], self.vc), (st_zf[p], self.zf),
                            (st_vd[p], self.vd), (st_gd[p], self.gd)):
                if os.environ.get("NOST"):
                    continue
                S.dma(os.environ.get("STQ", "pool"), dst.t[t0:t0 + 512, :].rearrange("(j p) c -> p j c", p=128), st.t, reads=[st], writes=[dst])
            if STG <= 4:
                return
            for (c0, nh, dst, nw) in ((O_QA, 4, self.qaT, None), (O_KA, 4, self.kaT, None), (O_QD, 4, self.qdT, None),
                                      (O_QC, 4, self.qcT, qnw), (O_KC, 2, self.kcT, knw)):
                for h in range(nh):
                    b = self.bank()
                    for k in range(8):
                        S.mm(b.f[0:64, :], w.t[:, k, c0 + h * 64:c0 + (h + 1) * 64], hT.t[:, k, :], k == 0, k == 7,
                             [hT, w, b], [b])
                    fs = fst[nf % 4]
                    nf += 1
                    if nw is None:
                        S.copy("act", fs.t, b.f[0:64, :], [b], [fs])
                    else:
                        S.act(sq.t, b.f[0:64, :], AF.Square, [b], [sq])
                        b2 = self.bank()
                        S.mm(b2.f[0:64, :], ones64.t, sq.t, True, True, [ones64, sq], [b2])
                        self.rsqrt(rstd, rstd.t, b2.f[0:64, :], [b2])
                        S.stt("dve", qh.t, b.f[0:64, :], nw.t[:, 0:1], rstd.t, ALU.mult, ALU.mult, [b, nw, rstd], [qh])
                        b3 = self.bank()
                        S.mm(b3.f[0:64, :], rotp.t, qh.t, True, True, [rotp, qh], [b3])
                        S.tt("pool", t1.t, qh.t, rcs[p].t, ALU.mult, [qh, rcs[p]], [t1])
                        S.tt("dve", t2.t, b3.f[0:64, :], rss[p].t, ALU.mult, [b3, rss[p]], [t2])
                        S.tt("pool", fs.t, t1.t, t2.t, ALU.add, [t1, t2], [fs])
                    S.dma("pool", dst.t[h, :, t0:t0 + 512], fs.t, reads=[fs], writes=[dst])

    def phase_NA(self, l, si):
        S, A, W = self.S, self.A, self.W
        Sq = self.seqs[si]
        rows = Sq // 64
        S.barrier()
        A.reset()
        rpbT = A.alloc("rpbT", [4, 15], F32, parts=32)
        S.memset("dve", rpbT.t, 1.0, [rpbT])
        S.dma("sp", rpbT.t[0:31], W["na_rpb"][l].rearrange("h r j -> j h r"), reads=[self.ext], writes=[rpbT],
              allow_slow_non_contiguous=True)
        cm = A.alloc("cm", [4096], F32, parts=32)
        S.dma("sp", cm.t, self.C["na_cm"], reads=[self.ext], writes=[cm])
        gst = A.alloc("gst", [4096], F32, parts=15)
        gst.accum = True
        for h in range(4):
            for c in range(8):
                b = self.bank()
                S.mm(b.f[0:15, :], rpbT.t[:, h, :], cm.t[:, c * 512:(c + 1) * 512], True, True, [rpbT, cm], [b])
                S.copy("act" if c % 2 else "dve", gst.t[:, c * 512:(c + 1) * 512], b.f[0:15, :], [b], [gst])
            S.dma("sp", self.Gd.t[h], gst.t, reads=[gst], writes=[self.Gd])
        Gsh = A.alloc("Gsh", [4, 16, 64], F32)
        Gsh.accum = True
        S.memset("pool", Gsh.t, 0.0, [Gsh])
        for h in range(4):
            S.dma("sp", Gsh.t[0:64, h, 0:15, :], self.Gd.t[h].rearrange("r (k q) -> k r q", q=64),
                  reads=[self.Gd], writes=[Gsh])
            S.dma("sp", Gsh.t[64:128, h, 0:14, :], self.Gd.t[h, 1:15, :].rearrange("r (k q) -> k r q", q=64),
                  reads=[self.Gd], writes=[Gsh])
        BR = min(32, rows)
        qT = [A.alloc(f"na_q{i}", [4, BR * 64], BF16, parts=64) for i in range(1)]
        kT = [A.alloc(f"na_k{i}", [4, (BR + 7) * 64], BF16, parts=64) for i in range(1)]
        NP = (BR + 7 + 1) // 2
        Ve = [A.alloc(f"na_ve{i}", [NP, 4, 65], BF16) for i in range(1)]
        Vo = [A.alloc(f"na_vo{i}", [NP, 4, 65], BF16) for i in range(1)]
        sb = [A.alloc(f"na_s{i}", [4, 64], F32) for i in range(2)]
        pT = [A.alloc(f"na_p{i}", [4, 64], BF16) for i in range(2)]
        rec = [A.alloc(f"na_r{i}", [1], F32, parts=64) for i in range(2)]
        og = [A.alloc(f"na_o{i}", [8, 256], F32, parts=64) for i in range(2)]
        for b_ in og + Ve + Vo:
            b_.accum = True
        n = 0
        for bi, rb0 in enumerate(range(0, rows, BR)):
            rb1 = rb0 + BR
            p = 0
            r0f = lambda r: int(np.clip(r - 4, 0, rows - 8))
            kr_lo = r0f(rb0)
            kr_hi = r0f(rb1 - 1) + 8
            nkr = kr_hi - kr_lo
            S.dma("sp", qT[p].t, self.qaT.t[:, :, rb0 * 64:rb1 * 64].rearrange("h d s -> d h s"),
                  reads=[self.qaT], writes=[qT[p]])
            S.dma("sp", kT[p].t[:, :, 0:nkr * 64], self.kaT.t[:, :, kr_lo * 64:kr_hi * 64].rearrange("h d s -> d h s"),
                  reads=[self.kaT], writes=[kT[p]])
            S.memset("pool", Ve[p].t, 1.0, [Ve[p]])
            S.memset("pool", Vo[p].t, 1.0, [Vo[p]])
            npe = nkr // 2
            npo = (nkr - 1) // 2
            for h in range(4):
                S.dma("sp", Ve[p].t[:, 0:npe, h, 0:64],
                      self.va.t[kr_lo * 64:(kr_lo + 2 * npe) * 64, h * 64:(h + 1) * 64].rearrange("(m p) d -> p m d", p=128),
                      reads=[self.va], writes=[Ve[p]])
                S.dma("sp", Vo[p].t[:, 0:npo, h, 0:64],
                      self.va.t[(kr_lo + 1) * 64:(kr_lo + 1 + 2 * npo) * 64, h * 64:(h + 1) * 64].rearrange("(m p) d -> p m d", p=128),
                      reads=[self.va], writes=[Vo[p]])
            for r in range(rb0, rb1):
                kr = r0f(r)
                o = kr - r
                ogb = og[(r // 8) % 2]
                for h in range(4):
                    q = n % 2
                    n += 1
                    b = self.bank()
                    for c in range(4):
                        ko = (kr + 2 * c - kr_lo) * 64
                        S.mm(b.f[:, c * 64:(c + 1) * 64], kT[p].t[:, h, ko:ko + 128],
                             qT[p].t[:, h, (r - rb0) * 64:(r - rb0 + 1) * 64], True, True, [kT[p], qT[p]], [b])
                    S.stt("dve", sb[q].t, b.f[:, 0:256].rearrange("p (c q) -> p c q", c=4), 0.125,
                          Gsh.t[:, h, o + 7:o + 15:2, :], ALU.mult, ALU.add, [b, Gsh], [sb[q]])
                    S.act(pT[q].t, sb[q].t, AF.Exp, [sb[q]], [pT[q]])
                    b2 = self.bank()
                    for c in range(4):
                        rel = kr + 2 * c - kr_lo
                        Vs = Ve[p] if rel % 2 == 0 else Vo[p]
                        S.mm(b2.f[0:64, 0:65], pT[q].t[:, c, :], Vs.t[:, rel // 2, h, :], c == 0, c == 3,
                             [pT[q], Vs, b2], [b2])
                    S.recip(rec[q].t, b2.f[0:64, 64:65], [b2], [rec[q]])
                    S.ts("dve", ogb.t[:, r % 8, h * 64:(h + 1) * 64], b2.f[0:64, 0:64], rec[q].t[:, 0:1], None,
                         ALU.mult, None, [b2, rec[q]], [ogb])
                if r % 8 == 7:
                    S.dma("pool", self.mixA.t[(r - 7) * 64:(r + 1) * 64, :].rearrange("(r p) c -> p r c", p=64),
                          ogb.t, reads=[ogb], writes=[self.mixA])

    def phase_GQA(self, l, si):
        S, A = self.S, self.A
        Sq = self.seqs[si]
        nkt = Sq // 128
        S.barrier()
        A.reset()
        kT = A.alloc("g_k", [2, Sq], BF16, parts=64)
        S.dma("sp", kT.t, self.kcT.t[:, :, 0:Sq].rearrange("h d s -> d h s"), reads=[self.kcT], writes=[kT])
        V = A.alloc("g_v", [nkt, 2, 65], BF16)
        V.accum = True
        S.memset("pool", V.t, 1.0, [V])
        for h in range(2):
            S.dma("sp", V.t[:, :, h, 0:64], self.vc.t[0:Sq, h * 64:(h + 1) * 64].rearrange("(m p) d -> p m d", p=128),
                  reads=[self.vc], writes=[V])
        qT = [A.alloc(f"g_q{i}", [4, 512], BF16, parts=64) for i in range(2)]
        pT = [A.alloc(f"g_p{i}", [512], BF16) for i in range(3)]
        oT = A.alloc("g_oT", [512], F32, parts=65)
        rec = A.alloc("g_rec", [4], F32)
        osb = [A.alloc(f"g_o{i}", [4, 256], F32) for i in range(2)]
        for b_ in osb:
            b_.accum = True
        n = 0
        for qb in range(Sq // 512):
            p = qb % 2
            S.dma("sp", qT[p].t, self.qcT.t[:, :, qb * 512:(qb + 1) * 512].rearrange("h d s -> d h s"),
                  reads=[self.qcT], writes=[qT[p]])
            for h in range(4):
                hk = h // 2
                acc = self.bank()

                def qk(kt):
                    b = self.bank()
                    if b is acc:
                        b = self.bank()
                    S.mm(b.f[:, :], kT.t[:, hk, kt * 128:(kt + 1) * 128], qT[p].t[:, h, :], True, True,
                         [kT, qT[p]], [b])
                    return b

                bnext = qk(0)
                for kt in range(nkt):
                    b = bnext
                    if kt + 1 < nkt:
                        bnext = qk(kt + 1)
                    pp = pT[n % 3]
                    n += 1
                    S.act(pp.t, b.f[:, :], AF.Exp, [b], [pp], scale=0.125)
                    S.mm(acc.f[0:65, :], V.t[:, kt, hk, :], pp.t, kt == 0, kt == nkt - 1, [V, pp, acc], [acc])
                S.copy("dve", oT.t, acc.f[0:65, :], [acc], [oT])
                b = self.bank()
                for j in range(4):
                    S.tr(b.f[:, j * 65:(j + 1) * 65], oT.t[:, j * 128:(j + 1) * 128], self.identf.t[0:65, 0:65],
                         [oT, self.identf], [b])
                bv = b.f[:, 0:260].rearrange("p (j c) -> p j c", j=4)
                S.recip(rec.t, bv[:, :, 64], [b], [rec])
                S.tt("dve", osb[p].t[:, :, h * 64:(h + 1) * 64], bv[:, :, 0:64],
                     rec.t.unsqueeze(2).to_broadcast([128, 4, 64]), ALU.mult, [b, rec], [osb[p]])
            S.dma("pool", self.mixC.t[qb * 512:(qb + 1) * 512, :].rearrange("(j p) c -> p j c", p=128), osb[p].t,
                  reads=[osb[p]], writes=[self.mixC])

    def phase_HGRN(self, l, si):
        S, A, W = self.S, self.A, self.W
        Sq = self.seqs[si]
        nch = Sq // 128
        S.barrier()
        A.reset()
        L = self.L
        pj = A.alloc("pj", [L, 256], F32)
        pj.accum = True
        for j in range(L):
            S.dma("sp", pj.t[:, j, :], W["hgrn_lower_bound"][j].partition_broadcast(128), reads=[self.ext], writes=[pj])
        S.act(pj.t, pj.t, AF.Exp, [pj], [pj])
        tot = A.alloc("tot", [256], F32)
        lb = A.alloc("lb", [256], F32)
        oml = A.alloc("oml", [256], F32)
        S.copy("dve", tot.t, pj.t[:, 0, :], [pj], [tot])
        for j in range(1, L):
            S.tt("dve", tot.t, tot.t, pj.t[:, j, :], ALU.add, [tot, pj], [tot])
        S.memset("dve", lb.t, 0.0, [lb])
        for j in range(l):
            S.tt("dve", lb.t, lb.t, pj.t[:, j, :], ALU.add, [lb, pj], [lb])
        S.recip(tot.t, tot.t, [tot], [tot])
        S.tt("dve", lb.t, lb.t, tot.t, ALU.mult, [lb, tot], [lb])
        S.ts("dve", oml.t, lb.t, -1.0, 1.0, ALU.mult, ALU.add, [lb], [oml])
        msk = {}
        for n_ in ("tri_le", "tri_ge", "tri_gt", "tri_lt"):
            msk[n_] = A.alloc(n_, [128], F32)
            S.dma("sp", msk[n_].t, self.C[n_], reads=[self.ext], writes=[msk[n_]])
        zt = [A.alloc(f"zt{i}", [256], F32) for i in range(2)]
        vt = [A.alloc(f"vt{i}", [256], BF16) for i in range(2)]
        qT = [A.alloc(f"hq{i}", [4, 128], BF16, parts=64) for i in range(2)]
        ofw = [A.alloc(f"ofw{i}", [2, 256], F32, parts=64) for i in range(2)]
        e = A.alloc("e", [256], F32)
        f = A.alloc("f", [256], F32)
        logf = A.alloc("logf", [256], F32)
        kin = A.alloc("kin", [256], F32)
        bm = A.alloc("bm", [4], F32, parts=64)
        d1 = A.alloc("d1", [4, 128], F32, parts=64)
        e1 = A.alloc("e1", [4, 128], F32, parts=64)
        e2 = A.alloc("e2", [4, 128], F32, parts=64)
        e3 = A.alloc("e3", [4, 128], F32, parts=64)
        qtl = A.alloc("qtl", [4, 128], BF16, parts=64)
        ktl = A.alloc("ktl", [4, 128], BF16, parts=64)
        qbr = A.alloc("qbr", [4, 128], BF16, parts=64)
        ek = A.alloc("ek", [256], F32)
        khat = A.alloc("khat", [256], BF16)
        attF = A.alloc("attF", [4, 128], BF16, parts=64)
        attG = A.alloc("attG", [4, 64], BF16, parts=64)
        v2 = [A.alloc(f"v2{i}", [2, 256], BF16, parts=64) for i in range(2)]
        mF = [A.alloc(f"mF{i}", [128], F32, parts=64) for i in range(2)]
        mG = [A.alloc(f"mG{i}", [64], F32, parts=64) for i in range(2)]
        S.dma("sp", mF[0].t, self.C["tri_le"][0:64, :], reads=[self.ext], writes=[mF[0]])
        S.dma("sp", mF[1].t, self.C["tri_ge"][64:128, :], reads=[self.ext], writes=[mF[1]])
        S.dma("sp", mG[0].t, self.C["tri_le"][0:64, 0:64], reads=[self.ext], writes=[mG[0]])
        S.dma("sp", mG[1].t, self.C["tri_ge"][0:64, 0:64], reads=[self.ext], writes=[mG[1]])
        St = A.alloc("St", [4, 64], F32, parts=64)
        Sbf = A.alloc("Sbf", [4, 64], BF16, parts=64)
        osb = [A.alloc(f"ho{i}", [2, 256], F32, parts=64) for i in range(2)]
        for d in range(2):
            M1 = msk["tri_le"] if d == 0 else msk["tri_ge"]
            M2 = msk["tri_gt"] if d == 0 else msk["tri_lt"]
            order = range(nch) if d == 0 else range(nch - 1, -1, -1)
            for ci, c in enumerate(order):
                p = ci % 2
                t0 = c * 128
                first = ci == 0
                S.dma("sp", zt[p].t, self.zf.t[t0:t0 + 128, d * 256:(d + 1) * 256], reads=[self.zf], writes=[zt[p]])
                S.dma("sp", vt[p].t, self.vd.t[t0:t0 + 128, :], reads=[self.vd], writes=[vt[p]])
                S.dma("sp", v2[p].t, self.vd.t[t0:t0 + 128, :].rearrange("(a p) c -> p a c", p=64), reads=[self.vd], writes=[v2[p]])
                S.dma("sp", qT[p].t, self.qdT.t[:, :, t0:t0 + 128].rearrange("h d s -> d h s"), reads=[self.qdT], writes=[qT[p]])
                if d == 1:
                    S.dma("sp", ofw[p].t, self.mixDf.t[t0:t0 + 128, :].rearrange("(a p) c -> p a c", p=64), reads=[self.mixDf], writes=[ofw[p]])
                S.act(e.t, zt[p].t, AF.Exp, [zt[p]], [e], scale=-1.0)
                S.ts("dve", e.t, e.t, 1.0, None, ALU.add, None, [e], [e])
                S.recip(e.t, e.t, [e], [e])
                S.tt("dve", f.t, e.t, oml.t, ALU.mult, [e, oml], [f])
                S.tt("dve", f.t, f.t, lb.t, ALU.add, [f, lb], [f])
                S.act(logf.t, f.t, AF.Ln, [f], [logf])
                S.ts("dve", kin.t, f.t, -1.0, 1.0, ALU.mult, ALU.add, [f], [kin])
                b1, b2, b3 = self.bank(), self.bank(), self.bank()
                for h in range(4):
                    S.mm(b1.f[0:64, h * 128:(h + 1) * 128], logf.t[:, h * 64:(h + 1) * 64], M1.t, True, True, [logf, M1], [b1])
                S.mm(b2.f[:, 0:256], M2.t, logf.t, True, True, [logf, M2], [b2])
                for h in range(4):
                    S.tr(b3.f[0:64, h * 128:(h + 1) * 128], kin.t[:, h * 64:(h + 1) * 64], self.identf.t[:, :], [kin, self.identf], [b3])
                b1v = b1.f[0:64, :].rearrange("p (h t) -> p h t", h=4)
                S.copy("dve", bm.t, b1v[:, :, 64], [b1], [bm])
                S.tt("dve", d1.t, b1v, bm.t.unsqueeze(2).to_broadcast([64, 4, 128]), ALU.subtract, [b1, bm], [d1])
                S.act(e1.t, d1.t, AF.Exp, [d1], [e1])
                S.act(e2.t, d1.t, AF.Exp, [d1], [e2], scale=-1.0)
                S.act(e3.t, b1v, AF.Exp, [b1], [e3])
                S.tt("dve", qtl.t, qT[p].t, e1.t, ALU.mult, [qT[p], e1], [qtl])
                S.tt("dve", ktl.t, b3.f[0:64, :].rearrange("p (h t) -> p h t", h=4), e2.t, ALU.mult, [b3, e2], [ktl])
                S.tt("dve", qbr.t, qT[p].t, e3.t, ALU.mult, [qT[p], e3], [qbr])
                S.act(ek.t, b2.f[:, 0:256], AF.Exp, [b2], [ek])
                S.tt("dve", khat.t, kin.t, ek.t, ALU.mult, [kin, ek], [khat])
                Fs, Gs = (slice(0, 64), slice(64, 128)) if d == 0 else (slice(64, 128), slice(0, 64))
                fa, ga = (0, 1) if d == 0 else (1, 0)
                b4, b4g = self.bank(), self.bank()
                for h in range(4):
                    S.mm(b4.f[0:64, h * 128:(h + 1) * 128], ktl.t[:, h, Fs], qtl.t[:, h, :], True, True, [ktl, qtl], [b4])
                    S.mm(b4g.f[0:64, h * 64:(h + 1) * 64], ktl.t[:, h, Gs], qtl.t[:, h, Gs], True, True, [ktl, qtl], [b4g])
                S.tt("dve", attF.t, b4.f[0:64, :].rearrange("p (h t) -> p h t", h=4),
                     mF[d].t.unsqueeze(1).to_broadcast([64, 4, 128]), ALU.mult, [b4, mF[d]], [attF])
                S.tt("dve", attG.t, b4g.f[0:64, 0:256].rearrange("p (h t) -> p h t", h=4),
                     mG[d].t.unsqueeze(1).to_broadcast([64, 4, 64]), ALU.mult, [b4g, mG[d]], [attG])
                b5 = self.bank()
                for h in range(4):
                    oF = b5.f[0:64, fa * 256 + h * 64:fa * 256 + (h + 1) * 64]
                    oG = b5.f[0:64, ga * 256 + h * 64:ga * 256 + (h + 1) * 64]
                    vF = v2[p].t[:, fa, h * 64:(h + 1) * 64]
                    vG = v2[p].t[:, ga, h * 64:(h + 1) * 64]
                    S.mm(oF, attF.t[:, h, Fs], vF, True, first, [attF, v2[p]], [b5])
                    if not first:
                        S.mm(oF, qbr.t[:, h, Fs], Sbf.t[:, h, :], False, True, [qbr, Sbf, b5], [b5])
                    S.mm(oG, attF.t[:, h, Gs], vF, True, False, [attF, v2[p]], [b5])
                    S.mm(oG, attG.t[:, h, :], vG, False, first, [attG, v2[p], b5], [b5])
                    if not first:
                        S.mm(oG, qbr.t[:, h, Gs], Sbf.t[:, h, :], False, True, [qbr, Sbf, b5], [b5])
                b6 = self.bank()
                for h in range(4):
                    S.mm(b6.f[0:64, h * 64:(h + 1) * 64], khat.t[:, h * 64:(h + 1) * 64], vt[p].t[:, h * 64:(h + 1) * 64],
                         True, True, [khat, vt[p]], [b6])
                b6v = b6.f[0:64, 0:256].rearrange("p (h v) -> p h v", h=4)
                if first:
                    S.copy("dve", St.t, b6v, [b6], [St])
                else:
                    col = 127 if d == 0 else 0
                    S.tt("dve", St.t, St.t, e3.t[:, :, col:col + 1].to_broadcast([64, 4, 64]), ALU.mult, [St, e3], [St])
                    S.tt("dve", St.t, St.t, b6v, ALU.add, [St, b6], [St])
                S.copy("dve", Sbf.t, St.t, [St], [Sbf])
                o = osb[p]
                b5v = b5.f[0:64, :].rearrange("p (a c) -> p a c", a=2)
                if d == 0:
                    S.copy("dve", o.t, b5v, [b5], [o])
                    S.dma("pool", self.mixDf.t[t0:t0 + 128, :].rearrange("(a p) c -> p a c", p=64), o.t, reads=[o], writes=[self.mixDf])
                else:
                    S.tt("dve", o.t, b5v, ofw[p].t, ALU.add, [b5, ofw[p]], [o])
                    S.dma("pool", self.mixD.t[t0:t0 + 128, :].rearrange("(a p) c -> p a c", p=64), o.t, reads=[o], writes=[self.mixD])

    def phase_S5(self, l, si):
        S, A, W = self.S, self.A, self.W
        Sq = self.seqs[si]
        SEG = 1024
        assert Sq % SEG == 0
        T1 = SEG // 8
        nseg = Sq // SEG
        S.barrier()
        A.reset()
        PR = [A.alloc(f"PR{d}", [16, NKP], F32, parts=64) for d in range(2)]
        PI = [A.alloc(f"PI{d}", [16, NKP], F32, parts=64) for d in range(2)]
        TPb = A.alloc("TPb", [16, 128], BF16)
        TPb.accum = True
        LW = [[A.alloc(f"LW{d}{ri}", [16, 64], BF16) for ri in range(2)] for d in range(2)]
        CS = [[A.alloc(f"CS{d}{ri}", [16, 128], BF16, parts=64) for ri in range(2)] for d in range(2)]
        cf = [A.alloc(f"cf{ri}", [16], F32, parts=64) for ri in range(2)]
        cbs = [[A.alloc(f"cbs{g}_{ri}", [16], F32, parts=64) for ri in range(2)] for g in range(nseg)]
        mark = A.off
        acc = A.alloc("acc", [16, 128], F32)
        acc.accum = True
        dv = A.alloc("dv", [16], F32)
        dv.accum = True
        for s0 in range(8):
            S.dma("sp", dv.t[s0 * 16:(s0 + 1) * 16, :], W["s5_d"][l].rearrange("(g c) -> c g", c=16), reads=[self.ext],
                  writes=[dv], allow_slow_non_contiguous=True)
        bmk = [A.alloc(f"bmk{d}", [128], F32) for d in range(2)]
        for d in range(2):
            S.dma("sp", bmk[d].t, self.C["bmask"][d], reads=[self.ext], writes=[bmk[d]])
        lre = A.alloc("lre", [16], F32, parts=64)
        lim = A.alloc("lim", [16], F32, parts=64)
        dt = A.alloc("dt", [16], F32, parts=64)
        xx = A.alloc("xx", [16], F32, parts=64)
        ang = A.alloc("ang", [16], F32, parts=64)
        kp = A.alloc("kp", [NKP], F32, parts=64)
        Ta = A.alloc("Ta", [16, NKP], F32, parts=64)
        Tb = A.alloc("Tb", [16, NKP], F32, parts=64)
        Tc = A.alloc("Tc", [16, NKP], F32, parts=64)
        EX = A.alloc("EX", [16, NKP], F32, parts=64)
        SN = A.alloc("SN", [16, NKP], F32, parts=64)
        CN = A.alloc("CN", [16, NKP], F32, parts=64)
        sm = [A.alloc(f"sm{i}", [16], F32, parts=64) for i in range(6)]
        Bre = A.alloc("Bre", [16, 16], F32, parts=64)
        Bim = A.alloc("Bim", [16, 16], F32, parts=64)
        bR = A.alloc("bR", [16, 16], F32, parts=64)
        bI = A.alloc("bI", [16, 16], F32, parts=64)
        t16 = A.alloc("t16", [16, 16], F32, parts=64)
        Craw = A.alloc("Craw", [2, 64], F32)
        Cre = A.alloc("Cre", [16, 16], F32, parts=64)
        Cim = A.alloc("Cim", [16, 16], F32, parts=64)
        PmR = A.alloc("PmR", [16, 8, 16], F32, parts=64)
        PmI = A.alloc("PmI", [16, 8, 16], F32, parts=64)
        QmR = A.alloc("QmR", [16, 8, 16], F32, parts=64)
        NQmI = A.alloc("NQmI", [16, 8, 16], F32, parts=64)
        ta = A.alloc("ta", [16, 8, 16], F32, parts=64)
        tb = A.alloc("tb", [16, 8, 16], F32, parts=64)
        B4 = [64, 16, 8, 16]

        def bk(ap):
            return ap.unsqueeze(3).to_broadcast(B4)

        def bc(ap):
            return ap.unsqueeze(2).to_broadcast(B4)

        def cplx(oR, oI, pr, pi, xr, xi, neg_im, bufs_in):
            S.tt("dve", ta.t, bk(pr), bc(xr.t), ALU.mult, bufs_in, [ta])
            S.tt("dve", tb.t, bk(pi), bc(xi.t), ALU.mult, bufs_in, [tb])
            S.tt("dve", oR.t, ta.t, tb.t, ALU.subtract, [ta, tb], [oR])
            S.tt("dve", ta.t, bk(pr), bc(xi.t), ALU.mult, bufs_in, [ta])
            S.tt("dve", tb.t, bk(pi), bc(xr.t), ALU.mult, bufs_in, [tb])
            if neg_im:
                S.stt("dve", oI.t, ta.t, -1.0, tb.t, ALU.mult, ALU.subtract, [ta, tb], [oI])
            else:
                S.tt("dve", oI.t, ta.t, tb.t, ALU.add, [ta, tb], [oI])

        for d in range(2):
            S.dma("sp", lre.t, W["s5_lambda_re"][l, d].rearrange("g n -> n g"), reads=[self.ext], writes=[lre],
                  allow_slow_non_contiguous=True)
            S.dma("sp", lim.t, W["s5_lambda_im"][l, d].rearrange("g n -> n g"), reads=[self.ext], writes=[lim],
                  allow_slow_non_contiguous=True)
            S.dma("sp", dt.t, W["s5_log_dt"][l, d].partition_broadcast(64), reads=[self.ext], writes=[dt])
            S.dma("sp", kp.t, self.C["kpow"][d].partition_broadcast(64), reads=[self.ext], writes=[kp])
            S.dma("sp", Bre.t, W["s5_b_re"][l, d].rearrange("g n c -> n g c"), reads=[self.ext], writes=[Bre])
            S.dma("sp", Bim.t, W["s5_b_im"][l, d].rearrange("g n c -> n g c"), reads=[self.ext], writes=[Bim])
            S.ts("dve", lre.t, lre.t, -1e-4, None, ALU.min, None, [lre], [lre])
            S.act(dt.t, dt.t, AF.Exp, [dt], [dt])
            S.tt("dve", xx.t, lre.t, dt.t, ALU.mult, [lre, dt], [xx])
            S.tt("dve", ang.t, lim.t, dt.t, ALU.mult, [lim, dt], [ang])
            B3 = [64, 16, NKP]
            kpb = kp.t.unsqueeze(1).to_broadcast(B3)
            S.tt("dve", Ta.t, xx.t.unsqueeze(2).to_broadcast(B3), kpb, ALU.mult, [xx, kp], [Ta])
            S.act(EX.t, Ta.t, AF.Exp, [Ta], [EX])
            S.tt("dve", Ta.t, ang.t.unsqueeze(2).to_broadcast(B3), kpb, ALU.mult, [ang, kp], [Ta])
            for (dst, shift) in ((SN, 0.0), (CN, 1.5707963267948966)):
                if shift:
                    S.ts("dve", Tc.t, Ta.t, shift, None, ALU.add, None, [Ta], [Tc])
                    src = Tc
                else:
                    src = Ta
                S.ts("dve", Tb.t, src.t, 1.0 / TWO_PI, MAGIC, ALU.mult, ALU.add, [src], [Tb])
                S.ts("dve", Tb.t, Tb.t, -MAGIC, None, ALU.add, None, [Tb], [Tb])
                S.stt("dve", Tb.t, Tb.t, -TWO_PI, src.t, ALU.mult, ALU.add, [Tb, src], [Tb])
                S.act(dst.t, Tb.t, AF.Sin, [Tb], [dst])
            S.tt("dve", PR[d].t, EX.t, CN.t, ALU.mult, [EX, CN], [PR[d]])
            S.tt("dve", PI[d].t, EX.t, SN.t, ALU.mult, [EX, SN], [PI[d]])
            aR = PR[d].t[:, :, KP_ONE]
            aI = PI[d].t[:, :, KP_ONE]
            den, nre, cR, cI, u0, u1 = sm
            S.tt("dve", den.t, lre.t, lre.t, ALU.mult, [lre], [den])
            S.tt("dve", u0.t, lim.t, lim.t, ALU.mult, [lim], [u0])
            S.tt("dve", den.t, den.t, u0.t, ALU.add, [den, u0], [den])
            S.recip(den.t, den.t, [den], [den])
            S.ts("dve", nre.t, aR, -1.0, None, ALU.add, None, [PR[d]], [nre])
            S.tt("dve", u0.t, nre.t, lre.t, ALU.mult, [nre, lre], [u0])
            S.tt("dve", u1.t, aI, lim.t, ALU.mult, [PI[d], lim], [u1])
            S.tt("dve", u0.t, u0.t, u1.t, ALU.add, [u0, u1], [u0])
            S.tt("dve", cR.t, u0.t, den.t, ALU.mult, [u0, den], [cR])
            S.tt("dve", u0.t, aI, lre.t, ALU.mult, [PI[d], lre], [u0])
            S.tt("dve", u1.t, nre.t, lim.t, ALU.mult, [nre, lim], [u1])
            S.tt("dve", u0.t, u0.t, u1.t, ALU.subtract, [u0, u1], [u0])
            S.tt("dve", cI.t, u0.t, den.t, ALU.mult, [u0, den], [cI])
            B16 = [64, 16, 16]
            cRb = cR.t.unsqueeze(2).to_broadcast(B16)
            cIb = cI.t.unsqueeze(2).to_broadcast(B16)
            S.tt("dve", bR.t, cRb, Bre.t, ALU.mult, [cR, Bre], [bR])
            S.tt("dve", t16.t, cIb, Bim.t, ALU.mult, [cI, Bim], [t16])
            S.tt("dve", bR.t, bR.t, t16.t, ALU.subtract, [bR, t16], [bR])
            S.tt("dve", bI.t, cRb, Bim.t, ALU.mult, [cR, Bim], [bI])
            S.tt("dve", t16.t, cIb, Bre.t, ALU.mult, [cI, Bre], [t16])
            S.tt("dve", bI.t, bI.t, t16.t, ALU.add, [bI, t16], [bI])
            for (src_w, dstC) in (("s5_c_re", Cre), ("s5_c_im", Cim)):
                S.dma("sp", Craw.t, W[src_w][l, d].rearrange("(j g) c n -> (g c) j n", j=2), reads=[self.ext], writes=[Craw])
                b = self.bank()
                for j in range(2):
                    S.tr(b.f[0:64, j * 128:(j + 1) * 128], Craw.t[:, j, :], self.identf.t[:, :], [Craw, self.identf], [b])
                S.copy("dve", dstC.t, b.f[0:64, 0:256].rearrange("p (g c) -> p g c", g=16), [b], [dstC])
            pin = [PR[d], PI[d]]
            cplx(PmR, PmI, PR[d].t[:, :, KP_P:KP_P + 8], PI[d].t[:, :, KP_P:KP_P + 8], bR, bI, False, pin + [bR, bI])
            cplx(QmR, NQmI, PR[d].t[:, :, KP_Q:KP_Q + 8], PI[d].t[:, :, KP_Q:KP_Q + 8], Cre, Cim, True, pin + [Cre, Cim])
            prc, pic = PR[d].t[:, :, KP_C:KP_C + 8], PI[d].t[:, :, KP_C:KP_C + 8]
            cs0 = CS[d][0].t.rearrange("p g (t c) -> p g t c", t=8)
            cs1 = CS[d][1].t.rearrange("p g (t c) -> p g t c", t=8)
            S.tt("dve", ta.t, bk(prc), bc(Cre.t), ALU.mult, pin + [Cre], [ta])
            S.tt("dve", tb.t, bk(pic), bc(Cim.t), ALU.mult, pin + [Cim], [tb])
            S.tt("dve", cs0, ta.t, tb.t, ALU.subtract, [ta, tb], [CS[d][0]])
            S.tt("dve", ta.t, bk(prc), bc(Cim.t), ALU.mult, pin + [Cim], [ta])
            S.tt("dve", tb.t, bk(pic), bc(Cre.t), ALU.mult, pin + [Cre], [tb])
            S.stt("dve", cs1, ta.t, -1.0, tb.t, ALU.mult, ALU.subtract, [ta, tb], [CS[d][1]])
            for g in range(16):
                b = self.bank()
                S.mm(b.f[:, 0:128], PmR.t[:, g].rearrange("p s c -> p (s c)"), QmR.t[:, g].rearrange("p s c -> p (s c)"),
                     True, False, [PmR, QmR], [b])
                S.mm(b.f[:, 0:128], PmI.t[:, g].rearrange("p s c -> p (s c)"), NQmI.t[:, g].rearrange("p s c -> p (s c)"),
                     False, True, [PmI, NQmI, b], [b])
                if d == 0:
                    S.tt("dve", acc.t[:, g, :], b.f[:, 0:128], bmk[0].t, ALU.mult, [b, bmk[0]], [acc])
                else:
                    S.tt("dve", bmk[0].t, b.f[:, 0:128], bmk[1].t, ALU.mult, [b, bmk[1]], [bmk[0]])
                    S.tt("dve", acc.t[:, g, :], acc.t[:, g, :], bmk[0].t, ALU.add, [acc, bmk[0]], [acc])
            for ri, Pm in ((0, PmR), (1, PmI)):
                for g0 in (0, 8):
                    b = self.bank()
                    for j in range(8):
                        S.tr(b.f[:, j * 64:(j + 1) * 64], Pm.t[:, g0 + j].rearrange("p s c -> p (s c)"),
                             self.identf.t[0:64, 0:64], [Pm, self.identf], [b])
                    S.copy("dve", LW[d][ri].t[:, g0:g0 + 8, :], b.f[:, :].rearrange("p (g n) -> p g n", g=8), [b], [LW[d][ri]])
        for g in range(16):
            S.stt("dve", TPb.t[:, g, :], self.identf.t[:, :], dv.t[:, g:g + 1], acc.t[:, g, :], ALU.mult, ALU.add,
                  [self.identf, dv, acc], [TPb])
        S.barrier()
        A.off = mark
        U = A.alloc("U", [8, 16, 16], BF16)
        U2 = A.alloc("U2", [16, 8, 16], BF16)
        Ut = A.alloc("Ut", [16, T1], BF16)
        Ut.accum = True
        X = [[A.alloc(f"X{pp}{ri}", [16, T1], F32, parts=64) for ri in range(2)] for pp in range(2)]
        for pp in range(2):
            for ri in range(2):
                X[pp][ri].accum = True
        tmp = A.alloc("tmp", [16, T1], F32, parts=64)
        XP = [A.alloc(f"XP{ri}", [16, T1], BF16, parts=64) for ri in range(2)]
        XN = [A.alloc(f"XN{ri}", [16, T1], BF16, parts=64) for ri in range(2)]
        for b_ in XP + XN:
            b_.accum = True
        Yg = [A.alloc(f"Yg{i}", [T1], F32) for i in range(2)]
        Ytok = A.alloc("Ytok", [8, 16, 16], F32)
        Ytok.accum = True
        zc = A.alloc("zc", [16], F32, parts=64)
        S.memset("dve", zc.t, 0.0, [zc])
        ru = [A.alloc(f"ru{i}", [16], F32, parts=64) for i in range(2)]
        NLV = 7

        def load_seg(seg):
            S.dma("sp", U.t.rearrange("p s g c -> p (s g c)"),
                  self.ub.t[seg * SEG:(seg + 1) * SEG, :].rearrange("(t s) c -> t (s c)", s=8), reads=[self.ub], writes=[U])
            S.copy("act", U2.t, U.t.rearrange("p s g c -> p g s c"), [U], [U2])
            for g0 in range(0, 16, 4):
                b = self.bank()
                for j in range(4):
                    S.tr(b.bf[:, j * 128:(j + 1) * 128], U2.t[:, g0 + j].rearrange("p s c -> p (s c)"), self.ident.t[:, :],
                         [U2, self.ident], [b])
                S.copy("dve", Ut.t[:, g0:g0 + 4, :], b.bf[:, 0:512].rearrange("p (g t) -> p g t", g=4), [b], [Ut])

        def scan(d, carry):
            for ri in range(2):
                for g0 in range(0, 16, 4):
                    b = self.bank()
                    for j in range(4):
                        S.mm(b.f[0:64, j * T1:(j + 1) * T1], LW[d][ri].t[:, g0 + j, :], Ut.t[:, g0 + j, :], True, True,
                             [LW[d][ri], Ut], [b])
                    S.copy("act", X[0][ri].t[:, g0:g0 + 4, :], b.f[0:64, 0:4 * T1].rearrange("p (g t) -> p g t", g=4),
                           [b], [X[0][ri]])
            col = 0 if d == 0 else T1 - 1
            a8R = PR[d].t[:, :, KP_KS]
            a8I = PI[d].t[:, :, KP_KS]
            pin = [PR[d], PI[d]] + carry
            u0, u1 = ru
            for (ri, p1, c1, p2, c2, op2) in ((0, a8R, carry[0], a8I, carry[1], ALU.subtract),
                                               (1, a8R, carry[1], a8I, carry[0], ALU.add)):
                S.tt("dve", u0.t, p1, c1.t, ALU.mult, pin, [u0])
                S.tt("dve", u1.t, p2, c2.t, ALU.mult, pin, [u1])
                S.tt("dve", u0.t, u0.t, u1.t, op2, [u0, u1], [u0])
                S.tt("dve", X[0][ri].t[:, :, col], X[0][ri].t[:, :, col], u0.t, ALU.add, [X[0][ri], u0], [X[0][ri]])
            cur = 0
            for m in range(NLV):
                dist = 1 << m
                n = T1 - dist
                src, dstb = X[cur], X[1 - cur]
                if d == 0:
                    hi, lo, keep = slice(dist, T1), slice(0, n), slice(0, dist)
                else:
                    hi, lo, keep = slice(0, n), slice(dist, T1), slice(n, T1)
                cRb = PR[d].t[:, :, KP_KS + m].unsqueeze(2).to_broadcast([64, 16, n])
                cIb = PI[d].t[:, :, KP_KS + m].unsqueeze(2).to_broadcast([64, 16, n])
                tv = tmp.t[:, :, 0:n]
                for (ri, ca, xa, cb_, xb, op2) in ((0, cRb, 0, cIb, 1, ALU.subtract), (1, cRb, 1, cIb, 0, ALU.add)):
                    S.copy("act", dstb[ri].t[:, :, keep], src[ri].t[:, :, keep], [src[ri]], [dstb[ri]])
                    S.tt("dve", tv, ca, src[xa].t[:, :, lo], ALU.mult, [PR[d], src[xa]], [tmp])
                    S.tt("dve", dstb[ri].t[:, :, hi], src[ri].t[:, :, hi], tv, ALU.add, [src[ri], tmp], [dstb[ri]])
                    S.tt("dve", tv, cb_, src[xb].t[:, :, lo], ALU.mult, [PI[d], src[xb]], [tmp])
                    S.tt("dve", dstb[ri].t[:, :, hi], dstb[ri].t[:, :, hi], tv, op2, [dstb[ri], tmp], [dstb[ri]])
                cur = 1 - cur
            return X[cur]

        for ri in range(2):
            S.copy("dve", cbs[nseg - 1][ri].t, zc.t, [zc], [cbs[nseg - 1][ri]])
            S.copy("dve", cf[ri].t, zc.t, [zc], [cf[ri]])
        for seg in range(nseg - 1, 0, -1):
            load_seg(seg)
            Xs = scan(1, cbs[seg])
            for ri in range(2):
                S.copy("dve", cbs[seg - 1][ri].t, Xs[ri].t[:, :, 0], [Xs[ri]], [cbs[seg - 1][ri]])
        for seg in range(nseg):
            load_seg(seg)
            Xf = scan(0, cf)
            for ri in range(2):
                S.copy("act", XP[ri].t[:, :, 1:T1], Xf[ri].t[:, :, 0:T1 - 1], [Xf[ri]], [XP[ri]])
                S.copy("act", XP[ri].t[:, :, 0], cf[ri].t, [cf[ri]], [XP[ri]])
            for ri in range(2):
                S.copy("dve", cf[ri].t, Xf[ri].t[:, :, T1 - 1], [Xf[ri]], [cf[ri]])
            Xb = scan(1, cbs[seg])
            for ri in range(2):
                S.copy("act", XN[ri].t[:, :, 0:T1 - 1], Xb[ri].t[:, :, 1:T1], [Xb[ri]], [XN[ri]])
                S.copy("act", XN[ri].t[:, :, T1 - 1], cbs[seg][ri].t, [cbs[seg][ri]], [XN[ri]])
            for g0 in range(0, 16, 4):
                bt = self.bank()
                for j in range(4):
                    g = g0 + j
                    b = self.bank()
                    if b is bt:
                        b = self.bank()
                    S.mm(b.f[:, 0:T1], TPb.t[:, g, :], Ut.t[:, g, :], True, False, [TPb, Ut], [b])
                    S.mm(b.f[:, 0:T1], CS[0][0].t[:, g, :], XP[0].t[:, g, :], False, False, [CS[0][0], XP[0], b], [b])
                    S.mm(b.f[:, 0:T1], CS[0][1].t[:, g, :], XP[1].t[:, g, :], False, False, [CS[0][1], XP[1], b], [b])
                    S.mm(b.f[:, 0:T1], CS[1][0].t[:, g, :], XN[0].t[:, g, :], False, False, [CS[1][0], XN[0], b], [b])
                    S.mm(b.f[:, 0:T1], CS[1][1].t[:, g, :], XN[1].t[:, g, :], False, True, [CS[1][1], XN[1], b], [b])
                    yg = Yg[g % 2]
                    S.copy("act", yg.t, b.f[:, 0:T1], [b], [yg])
                    S.tr(bt.f[:, j * 128:(j + 1) * 128], yg.t, self.identf.t[:, :], [yg, self.identf], [bt])
                S.copy("dve", Ytok.t[:, :, g0:g0 + 4, :], bt.f[:, :].rearrange("p (g t c) -> p t g c", g=4, t=8), [bt], [Ytok])
            S.dma("act", self.mixB.t[seg * SEG:(seg + 1) * SEG, :].rearrange("(t s) c -> t (s c)", s=8),
                  Ytok.t.rearrange("p s g c -> p (s g c)"), reads=[Ytok], writes=[self.mixB])

    def phase_stub(self, l, si):
        S, A = self.S, self.A
        Sq = self.seqs[si]
        S.barrier()
        A.reset()
        z = A.alloc("z", [4, 256], F32)
        S.memset("dve", z.t, 0.0, [z])
        for ti in range(Sq // 512):
            for dst in (self.mixB, self.mixD):
                S.dma("sp", dst.t[ti * 512:(ti + 1) * 512, :].rearrange("(j p) c -> p j c", p=128), z.t,
                      reads=[z], writes=[dst])

    def phase_C1(self, l, si):
        S, A, W = self.S, self.A, self.W
        Sq = self.seqs[si]
        S.barrier()
        A.reset()
        if l == self.layers[0]:
            xsrc, xb = self.x_in[si], self.ext
        else:
            xsrc, xb = self.xA.t, self.xA
        wout = A.alloc("wout", [8, 1024], BF16)
        self.load_w(wout, W["w_out"][l], W["mix_out_norm_w"][l], 8, 1024, "wo")
        wq = A.alloc("wq", [8, 256], BF16)
        self.load_w(wq, W["xattn_w_q"][l], W["norm_xattn_w"][l], 8, 256, "wq")
        wkv = A.alloc("wkv", [8, 512], BF16)
        self.load_w(wkv, W["xattn_w_kv"][l], W["norm_mem_w"][l], 8, 512, "wkv")
        wo2 = A.alloc("wo2", [2, 1024], BF16)
        self.load_w(wo2, W["xattn_w_o"][l], None, 2, 1024, "wo2")
        glu = A.alloc("glu", [2, 256], BF16)
        self.load_w(glu, W["s5_glu_w"][l], None, 2, 256, "glu")
        glub = A.alloc("glub", [256], F32)
        S.dma("sp", glub.t, W["s5_glu_b"][l].partition_broadcast(128), reads=[self.ext], writes=[glub])
        junk = A.alloc("junk", [D], BF16)
        mt = A.alloc("mt", [2, D], F32)
        S.dma("sp", mt.t, self.m_in[si].rearrange("(j p) d -> p j d", p=128), reads=[self.ext], writes=[mt])
        mn = A.alloc("mn", [2, D], BF16)
        ssm = A.alloc("ssm", [2], F32)
        self.norm_tile(mt, 2, mn, ssm, junk)
        hTm = A.alloc("hTm", [8, 256], BF16)
        self.transpose_to(hTm, mn, 2)
        kmT = A.alloc("kmT", [4, 256], BF16, parts=64)
        kmT.accum = True
        for h in range(4):
            b = self.bank()
            for k in range(8):
                S.mm(b.f[0:64, 0:256], wkv.t[:, k, h * 64:(h + 1) * 64], hTm.t[:, k, :], k == 0, k == 7, [wkv, hTm, b], [b])
            S.copy("dve", kmT.t[:, h, :], b.f[0:64, 0:256], [b], [kmT])
        vm = A.alloc("vm", [2, 4, 65], BF16)
        vm.accum = True
        S.memset("dve", vm.t, 1.0, [vm])
        for c in range(2):
            b = self.bank()
            for k in range(8):
                S.mm(b.f[:, 0:256], hTm.t[:, k, c * 128:(c + 1) * 128], wkv.t[:, k, 256:512], k == 0, k == 7, [wkv, hTm, b], [b])
            S.copy("dve", vm.t[:, c, :, 0:64], b.f[:, 0:256].rearrange("p (h d) -> p h d", h=4), [b], [vm])
        mx = A.alloc("mx", [4, 256], F32)
        mx.accum = True
        gdt = A.alloc("gdt", [256], F32)
        xt = A.alloc("xt", [1, D], F32)
        u = A.alloc("u", [256], F32)
        sg = A.alloc("sg", [256], F32)
        hb = A.alloc("hb", [256], F32)
        hb16 = A.alloc("hb16", [1, 256], BF16)
        hbT = A.alloc("hbT", [2, 128], BF16)
        ssg = A.alloc("ssg", [4], F32)
        mg = A.alloc("mg", [1, D], BF16)
        mg.accum = True
        sil = A.alloc("sil", [256], F32)
        mT = A.alloc("mT", [8, 128], BF16)
        x1 = A.alloc("x1", [1, D], F32)
        x1.accum = True
        xn1 = A.alloc("xn1", [1, D], BF16)
        ss1 = A.alloc("ss1", [1], F32)
        h1T = A.alloc("h1T", [8, 128], BF16)
        qT = A.alloc("qT", [4, 128], BF16, parts=64)
        qT.accum = True
        pT = A.alloc("pT", [2, 128], BF16)
        rec = A.alloc("rec", [1], F32)
        xo = A.alloc("xo", [1, 256], BF16)
        xo.accum = True
        xoT = A.alloc("xoT", [2, 128], BF16)
        x2t = A.alloc("x2t", [D], F32)
        x2t.accum = True
        for ti in range(Sq // 128):
            t0 = ti * 128
            for g, src in enumerate((self.mixA, self.mixB, self.mixC, self.mixD)):
                S.dma("sp", mx.t[:, g, :], src.t[t0:t0 + 128, :], reads=[src], writes=[mx])
            S.dma("sp", gdt.t, self.gd.t[t0:t0 + 128, :], reads=[self.gd], writes=[gdt])
            S.dma("sp", xt.t[:, 0, :], xsrc[t0:t0 + 128, :], reads=[xb], writes=[xt])
            y = mx.t[:, 1, :]
            S.tt("dve", u.t, y, y, ALU.mult, [mx], [u])
            S.ts("dve", u.t, u.t, 0.044715, 1.0, ALU.mult, ALU.add, [u], [u])
            S.tt("dve", u.t, u.t, y, ALU.mult, [u, mx], [u])
            S.act(sg.t, u.t, AF.Sigmoid, [u], [sg], scale=1.5957691216057308)
            S.tt("dve", hb.t, y, sg.t, ALU.mult, [mx, sg], [hb])
            S.copy("dve", hb16.t[:, 0, :], hb.t, [hb], [hb16])
            b = self.bank()
            for k in range(2):
                S.tr(b.bf[:, k * 128:(k + 1) * 128], hb16.t[:, 0, k * 128:(k + 1) * 128], self.ident.t[:, :], [hb16, self.ident], [b])
            S.copy("dve", hbT.t, b.bf[:, 0:256].rearrange("p (k t) -> p k t", k=2), [b], [hbT])
            b = self.bank()
            for k in range(2):
                S.mm(b.f[:, 0:256], hbT.t[:, k, :], glu.t[:, k, :], k == 0, k == 1, [hbT, glu, b], [b])
            S.tt("dve", u.t, b.f[:, 0:256], glub.t, ALU.add, [b, glub], [u])
            S.act(sg.t, u.t, AF.Sigmoid, [u], [sg])
            S.tt("dve", mx.t[:, 1, :], hb.t, sg.t, ALU.mult, [hb, sg], [mx])
            S.memset("dve", ssg.t, 0.0, [ssg])
            for g in range(4):
                S.act(junk.t[:, 0:256], mx.t[:, g, :], AF.Square, [mx, ssg], [junk, ssg], scale=1.0 / 16.0,
                      accum_out=ssg.t[:, g:g + 1])
            self.rsqrt(ssg, ssg.t, ssg.t, [ssg])
            for g in range(3):
                S.act(mg.t[:, 0, g * 256:(g + 1) * 256], mx.t[:, g, :], AF.Copy, [mx, ssg], [mg], scale=ssg.t[:, g:g + 1])
            S.act(sil.t, gdt.t, AF.Silu, [gdt], [sil])
            S.stt("dve", mg.t[:, 0, 768:1024], mx.t[:, 3, :], ssg.t[:, 3:4], sil.t, ALU.mult, ALU.mult, [mx, ssg, sil], [mg])
            self.transpose_to(mT, mg, 1)
            for c in range(2):
                b = self.bank()
                for k in range(8):
                    S.mm(b.f[:, :], mT.t[:, k, :], wout.t[:, k, c * 512:(c + 1) * 512], k == 0, k == 7, [mT, wout, b], [b])
                S.tt("dve", x1.t[:, 0, c * 512:(c + 1) * 512], b.f[:, :], xt.t[:, 0, c * 512:(c + 1) * 512], ALU.add, [b, xt], [x1])
            self.norm_tile(x1, 1, xn1, ss1, junk)
            self.transpose_to(h1T, xn1, 1)
            for h in range(4):
                b = self.bank()
                for k in range(8):
                    S.mm(b.f[0:64, 0:128], wq.t[:, k, h * 64:(h + 1) * 64], h1T.t[:, k, :], k == 0, k == 7, [wq, h1T, b], [b])
                S.copy("dve", qT.t[:, h, :], b.f[0:64, 0:128], [b], [qT])
            for h in range(4):
                b = self.bank()
                for c in range(2):
                    S.mm(b.f[:, c * 128:(c + 1) * 128], kmT.t[:, h, c * 128:(c + 1) * 128], qT.t[:, h, :], True, True, [kmT, qT], [b])
                S.act(pT.t, b.f[:, 0:256].rearrange("p (c t) -> p c t", c=2), AF.Exp, [b], [pT], scale=0.125)
                b2 = self.bank()
                for c in range(2):
                    S.mm(b2.f[:, 0:65], pT.t[:, c, :], vm.t[:, c, h, :], c == 0, c == 1, [pT, vm, b2], [b2])
                S.recip(rec.t, b2.f[:, 64:65], [b2], [rec])
                S.ts("dve", xo.t[:, 0, h * 64:(h + 1) * 64], b2.f[:, 0:64], rec.t[:, 0:1], None, ALU.mult, None, [b2, rec], [xo])
            b = self.bank()
            for k in range(2):
                S.tr(b.bf[:, k * 128:(k + 1) * 128], xo.t[:, 0, k * 128:(k + 1) * 128], self.ident.t[:, :], [xo, self.ident], [b])
            S.copy("dve", xoT.t, b.bf[:, 0:256].rearrange("p (k t) -> p k t", k=2), [b], [xoT])
            for c in range(2):
                b = self.bank()
                for k in range(2):
                    S.mm(b.f[:, :], xoT.t[:, k, :], wo2.t[:, k, c * 512:(c + 1) * 512], k == 0, k == 1, [xoT, wo2, b], [b])
                S.tt("dve", x2t.t[:, c * 512:(c + 1) * 512], b.f[:, :], x1.t[:, 0, c * 512:(c + 1) * 512], ALU.add, [b, x1], [x2t])
            S.dma("pool", self.x2.t[t0:t0 + 128, :], x2t.t, reads=[x2t], writes=[self.x2])

    def phase_FFN(self, l, si, hf):
        S, A, W = self.S, self.A, self.W
        Sq = self.seqs[si]
        last = (l == self.layers[-1])
        final = (l == self.L - 1)
        S.barrier()
        A.reset()
        H = 1408
        wu = A.alloc("wu", [8, 2 * H], BF16)
        self.load_w(wu, W["ffn_w_up"][l][:, hf * H:(hf + 1) * H], W["norm_ffn_w"][l], 8, H, "wua", dap=wu.t[:, :, 0:H])
        self.load_w(wu, W["ffn_w_up"][l][:, DFF + hf * H:DFF + (hf + 1) * H], W["norm_ffn_w"][l], 8, H, "wug", dap=wu.t[:, :, H:2 * H])
        wd = A.alloc("wd", [11, D], BF16)
        self.load_w(wd, W["ffn_w_down"][l][hf * H:(hf + 1) * H, :], None, 11, D, "wd")
        cw = A.alloc("cw", [3, 11], F32)
        for j in range(3):
            S.dma("sp", cw.t[:, j, :], W["ffn_conv_w"][l][j, hf * H:(hf + 1) * H].rearrange("(c p) -> p c", p=128),
                  reads=[self.ext], writes=[cw], allow_slow_non_contiguous=True)
        cb = A.alloc("cb", [11], F32)
        S.dma("sp", cb.t, W["ffn_conv_b"][l][hf * H:(hf + 1) * H].rearrange("(c p) -> p c", p=128),
              reads=[self.ext], writes=[cb], allow_slow_non_contiguous=True)
        cw.accum = True
        fnw = None
        if final and hf == 1:
            fnw = A.alloc("fnw", [D], F32)
            S.dma("sp", fnw.t, self.fnw.partition_broadcast(128), reads=[self.ext], writes=[fnw])
        xt = A.alloc("xt", [4, D], F32)
        rin = xt if hf == 0 else A.alloc("rin", [4, D], F32)
        xn = A.alloc("xn", [4, D], BF16)
        junk = A.alloc("junk", [D], BF16)
        ss = A.alloc("ss", [4], F32)
        hT = A.alloc("hT", [8, 514], BF16)
        hT.accum = True
        xh = [A.alloc(f"xh{i}", [1, D], F32, parts=1) for i in range(2)]
        xhn = [A.alloc(f"xhn{i}", [1, D], BF16, parts=1) for i in range(2)]
        ssh = [A.alloc(f"ssh{i}", [1], F32, parts=1) for i in range(2)]
        jh = A.alloc("jh", [D], BF16, parts=1)
        mT = A.alloc("mT", [11, 512], BF16)
        mT.accum = True
        gs = [A.alloc(f"gs{i}", [514], F32) for i in range(2)]
        for g_ in gs:
            g_.accum = True
        tb = [A.alloc(f"tb{i}", [512], F32) for i in range(2)]
        sl = A.alloc("sl", [512], F32)
        ro = rin
        so = A.alloc("so", [4], F32)
        dst = self.x2b if hf == 0 else (self.yb[si] if last else self.xA)
        dst_ap = self.x2b.t if hf == 0 else (self.y_out[si] if last else self.xA.t)
        nt = Sq // 512
        for ti in range(nt):
            t0 = ti * 512
            S.dma("sp", xt.t, self.x2.t[t0:t0 + 512, :].rearrange("(j p) d -> p j d", p=128), reads=[self.x2], writes=[xt])
            if hf == 1:
                S.dma("sp", rin.t, self.x2b.t[t0:t0 + 512, :].rearrange("(j p) d -> p j d", p=128), reads=[self.x2b], writes=[rin])
            self.norm_tile(xt, 4, xn, ss, junk)
            for k in range(8):
                b = self.bank()
                for j in range(4):
                    S.tr(b.bf[:, j * 128:(j + 1) * 128], xn.t[:, j, k * 128:(k + 1) * 128], self.ident.t[:, :], [xn, self.ident], [b])
                S.copy("dve" if k % 2 == 0 else "act", hT.t[:, k, 1:513], b.bf[:, 0:512], [b], [hT])
            for i, tok in enumerate((t0 - 1, t0 + 512)):
                col = 0 if i == 0 else 513
                if tok < 0 or tok >= Sq:
                    S.memset("dve", hT.t[:, :, col:col + 1], 0.0, [hT])
                    continue
                S.dma("sp", xh[i].t[:, 0, :], self.x2.t[tok:tok + 1, :], reads=[self.x2], writes=[xh[i]])
                self.norm_tile(xh[i], 1, xhn[i], ssh[i], jh)
                b = self.bank()
                for k in range(8):
                    S.tr(b.bf[:, 2 * k:2 * k + 1], xhn[i].t[:, 0, k * 128:(k + 1) * 128], self.ident.t[0:1, 0:1], [xhn[i], self.ident], [b])
                S.copy("dve", hT.t[:, :, col:col + 1], b.bf[:, 0:16:2].unsqueeze(2), [b], [hT])
            for c in range(11):
                ba, bg, bh = self.bank(), self.bank(), self.bank()
                for k in range(8):
                    S.mm(ba.f[:, :], wu.t[:, k, c * 128:(c + 1) * 128], hT.t[:, k, 1:513], k == 0, k == 7, [wu, hT, ba], [ba])
                for k in range(8):
                    S.mm(bg.f[:, :], wu.t[:, k, H + c * 128:H + (c + 1) * 128], hT.t[:, k, 1:513], k == 0, k == 7, [wu, hT, bg], [bg])
                for k in range(8):
                    S.mm(bh.f[:, 0:2], wu.t[:, k, H + c * 128:H + (c + 1) * 128], hT.t[:, k, 0:514:513], k == 0, k == 7, [wu, hT, bh], [bh])
                g = gs[c % 2]
                t = tb[c % 2]
                S.copy("act", g.t[:, 1:513], bg.f[:, :], [bg], [g])
                S.copy("dve", g.t[:, 0:514:513], bh.f[:, 0:2], [bh], [g])
                S.ts("dve", t.t, g.t[:, 0:512], cw.t[:, 0, c:c + 1], cb.t[:, c:c + 1], ALU.mult, ALU.add, [g, cw, cb], [t])
                S.stt("dve", t.t, g.t[:, 1:513], cw.t[:, 1, c:c + 1], t.t, ALU.mult, ALU.add, [g, cw, t], [t])
                S.stt("dve", t.t, g.t[:, 2:514], cw.t[:, 2, c:c + 1], t.t, ALU.mult, ALU.add, [g, cw, t], [t])
                S.act(sl.t, t.t, AF.Silu, [t], [sl])
                S.tt("dve", mT.t[:, c, :], sl.t, ba.f[:, :], ALU.mult, [sl, ba], [mT])
            for j in range(4):
                for cc in range(2):
                    b = self.bank()
                    for c in range(11):
                        S.mm(b.f[:, :], mT.t[:, c, j * 128:(j + 1) * 128], wd.t[:, c, cc * 512:(cc + 1) * 512], c == 0, c == 10, [mT, wd, b], [b])
                    S.tt("dve", ro.t[:, j, cc * 512:(cc + 1) * 512], b.f[:, :], rin.t[:, j, cc * 512:(cc + 1) * 512], ALU.add, [b, rin], [ro])
            if fnw is not None:
                S.memset("dve", so.t, 0.0, [so])
                for j in range(4):
                    S.act(junk.t, ro.t[:, j, :], AF.Square, [ro, so], [junk, so], scale=1.0 / 32.0, accum_out=so.t[:, j:j + 1])
                self.rsqrt(so, so.t, so.t, [so])
                for j in range(4):
                    S.stt("dve", ro.t[:, j, :], ro.t[:, j, :], so.t[:, j:j + 1], fnw.t, ALU.mult, ALU.mult, [ro, so, fnw], [ro])
            S.dma("pool", dst_ap[t0:t0 + 512, :].rearrange("(j p) d -> p j d", p=128), ro.t, reads=[ro], writes=[dst])

    def build_all(self):
        for si in range(len(self.seqs)):
            for l in self.layers:
                self.phase_A(l, si)
                self.phase_NA(l, si)
                self.phase_GQA(l, si)
                self.phase_S5(l, si)
                self.phase_HGRN(l, si)
                self.phase_C1(l, si)
                self.phase_FFN(l, si, 0)
                self.phase_FFN(l, si, 1)
        self.S.finish()

    def in_map(self, inputs, xs, ms):
        c = host_consts(self.maxS)
        s_ = np.arange(128)
        c["tri_gt"] = (s_[:, None] > s_[None, :]).astype(np.float32)
        c["tri_lt"] = (s_[:, None] < s_[None, :]).astype(np.float32)
        c["kpow"] = _kpow_table()
        blk = s_ // 16
        c["bmask"] = np.stack([(blk[:, None] <= blk[None, :]), (blk[:, None] >= blk[None, :])]).astype(np.float32)
        m = {}
        for i, (x, mm_) in enumerate(zip(xs, ms)):
            m[f"x{i}"] = np.ascontiguousarray(x, dtype=np.float32)
            m[f"m{i}"] = np.ascontiguousarray(mm_, dtype=np.float32)
        for n, sh in WEIGHT_SPECS:
            m[n] = np.ascontiguousarray(inputs[n], dtype=np.float32)
        m["final_norm_w"] = np.ascontiguousarray(inputs["final_norm_w"], dtype=np.float32)
        for n, v in c.items():
            m["c_" + n] = np.ascontiguousarray(v)
        return m


N_CORES = 8
SEQS = [2048, 2048, 2048, 2048, 16384]
FUSED = os.environ.get("MK_FUSED", "0") == "1"


def _kernel_fused(inputs, xp, xs, mp, ms):
    P = Prog(SEQS, L=2)
    P.build_all()
    in_maps = []
    for c in range(N_CORES):
        sb = c // 4
        xs_c = [xp[4 * c + i] for i in range(4)] + [xs[sb]]
        ms_c = [mp[4 * c + i] for i in range(4)] + [ms[sb]]
        in_maps.append(P.in_map(inputs, xs_c, ms_c))
    res = run_bass_kernel_spmd(P.nc, in_maps, core_ids=list(range(N_CORES)))
    yp = np.empty_like(xp)
    ys = np.empty_like(xs)
    for c in range(N_CORES):
        for i in range(4):
            yp[4 * c + i] = res.results[c][f"y{i}"]
    ys[0] = res.results[0]["y4"]
    ys[1] = res.results[4]["y4"]
    return yp, ys


def _kernel_two_launch(inputs, xp, xs, mp, ms):
    P = Prog([2048] * 4, L=2)
    P.build_all()
    in_maps = [P.in_map(inputs, [xp[4 * c + i] for i in range(4)], [mp[4 * c + i] for i in range(4)])
               for c in range(N_CORES)]
    res = run_bass_kernel_spmd(P.nc, in_maps, core_ids=list(range(N_CORES)))
    yp = np.empty_like(xp)
    for c in range(N_CORES):
        for i in range(4):
            yp[4 * c + i] = res.results[c][f"y{i}"]
    cur = [xs[c] for c in range(2)]
    for l in range(2):
        P2 = Prog([16384], L=2, layers=[l])
        P2.build_all()
        in_maps2 = [P2.in_map(inputs, [cur[c]], [ms[c]]) for c in range(2)]
        res2 = run_bass_kernel_spmd(P2.nc, in_maps2, core_ids=[0, 1])
        cur = [res2.results[c]["y0"] for c in range(2)]
    ys = np.empty_like(xs)
    for c in range(2):
        ys[c] = cur[c]
    return yp, ys


def kernel(**inputs):
    xp = np.asarray(inputs["x_prompt"], dtype=np.float32)
    xs = np.asarray(inputs["x_sample"], dtype=np.float32)
    mp = np.asarray(inputs["mem_prompt"], dtype=np.float32)
    ms = np.asarray(inputs["mem_sample"], dtype=np.float32)
    if FUSED:
        return _kernel_fused(inputs, xp, xs, mp, ms)
    return _kernel_two_launch(inputs, xp, xs, mp, ms)
```
